# Optimizing a Trainium2 kernel written in Bass

```python
import math
import jax, jax.numpy as jnp
from jax import lax
import numpy as np

D_MODEL = 1024
BATCH = 8
SEQ = 2048
DEPTH = 2

N_A_LAYERS = DEPTH // 2
N_B_LAYERS = DEPTH - N_A_LAYERS
RET_HEADS = 4
RET_QK_DIM = D_MODEL // RET_HEADS
RET_V_DIM = 2 * RET_QK_DIM
RET_CHUNK = 128
DIFF_HEAD_DIM = 64
DIFF_HEADS = D_MODEL // (2 * DIFF_HEAD_DIM)
Q_BLOCK = 128
D_FF = ((8 * D_MODEL // 3 + 127) // 128) * 128
CONV_WIDTH = 3
ROPE_BASE = 10000.0
EPS = 1e-6

kernel_name = "yoco_retention_diffattn_convffn"


def rms_norm(x, g=None):
    xf = x.astype(jnp.float32)
    y = xf * lax.rsqrt(jnp.mean(xf * xf, axis=-1, keepdims=True) + EPS)
    if g is not None:
        y = y * g.astype(jnp.float32)
    return y.astype(x.dtype)


def rotary(x, pos):
    half = x.shape[-1] // 2
    inv = 1.0 / (ROPE_BASE ** jnp.linspace(0.0, 1.0, half, dtype=jnp.float32))
    ang = pos.astype(jnp.float32)[:, None] * inv[None, :]
    cos = jnp.cos(ang)[None, :, None, :]
    sin = jnp.sin(ang)[None, :, None, :]
    xf = x.astype(jnp.float32)
    x1, x2 = xf[..., :half], xf[..., half:]
    return jnp.concatenate([x1 * cos - x2 * sin, x1 * sin + x2 * cos], axis=-1).astype(x.dtype)


def retention(xn, w_in, w_out):
    B, S, _ = xn.shape
    H, dk, dv, C = RET_HEADS, RET_QK_DIM, RET_V_DIM, RET_CHUNK
    dt = xn.dtype
    proj = xn @ w_in
    q, k, v, g = jnp.split(proj, [H * dk, 2 * H * dk, 2 * H * dk + H * dv], axis=-1)
    pos = jnp.arange(S)
    q = rotary(q.reshape(B, S, H, dk), pos)
    k = rotary(k.reshape(B, S, H, dk), pos) * (dk ** -0.5)
    v = v.reshape(B, S, H, dv)

    log_gamma = jnp.log1p(-jnp.power(2.0, -5.0 - jnp.arange(H, dtype=jnp.float32)))
    idx = jnp.arange(C, dtype=jnp.float32)
    rel = idx[:, None] - idx[None, :]
    decay_mask = jnp.where(rel[None] >= 0,
                           jnp.exp(log_gamma[:, None, None] * jnp.maximum(rel, 0.0)[None]),
                           0.0).astype(dt)
    q_decay = jnp.exp(log_gamma[None, :] * (idx[:, None] + 1.0)).astype(dt)
    k_decay = jnp.exp(log_gamma[None, :] * (C - 1.0 - idx[:, None])).astype(dt)
    chunk_decay = jnp.exp(log_gamma * C).astype(dt)

    n = S // C

    def to_chunks(t):
        return jnp.moveaxis(t.reshape(B, n, C, H, t.shape[-1]), 1, 0)

    def step(state, qkv):
        qc, kc, vc = qkv
        s = jnp.einsum('bihd,bjhd->bhij', qc, kc) * decay_mask[None]
        inner = jnp.einsum('bhij,bjhe->bihe', s, vc)
        cross = jnp.einsum('bihd,bhde->bihe', qc * q_decay[None, :, :, None], state)
        new_state = (state * chunk_decay[None, :, None, None]
                     + jnp.einsum('bjhd,bjhe->bhde', kc * k_decay[None, :, :, None], vc))
        return new_state, inner + cross

    state0 = jnp.zeros((B, H, dk, dv), dt)
    _, out = lax.scan(step, state0, (to_chunks(q), to_chunks(k), to_chunks(v)))
    out = jnp.moveaxis(out, 0, 1).reshape(B, S, H, dv)
    out = rms_norm(out).reshape(B, S, H * dv)
    return (jax.nn.silu(g) * out) @ w_out


def diff_attention(xn, k_sh, v_sh, w_q, lam_params, subln_g, w_out, lambda_init):
    B, S, _ = xn.shape
    H, d = DIFF_HEADS, DIFF_HEAD_DIM
    q = (xn @ w_q).reshape(B, S, H, 2, d) * (d ** -0.5)
    nb = S // Q_BLOCK
    q_blocks = jnp.moveaxis(q.reshape(B, nb, Q_BLOCK, H, 2, d), 1, 0)
    lp = lam_params.astype(jnp.float32)
    lam = jnp.exp(jnp.sum(lp[0] * lp[1])) - jnp.exp(jnp.sum(lp[2] * lp[3])) + lambda_init
    kpos = jnp.arange(S)

    def block(args):
        i, qb = args
        qpos = i * Q_BLOCK + jnp.arange(Q_BLOCK)
        s = jnp.einsum('bqhcd,bkhcd->bhcqk', qb, k_sh).astype(jnp.float32)
        s = jnp.where(kpos[None, :] <= qpos[:, None], s, -jnp.inf)
        p = jax.nn.softmax(s, axis=-1)
        a = p[:, :, 0] - lam * p[:, :, 1]
        return jnp.einsum('bhqk,bkhe->bqhe', a.astype(v_sh.dtype), v_sh)

    o = lax.map(block, (jnp.arange(nb), q_blocks))
    o = jnp.moveaxis(o, 0, 1).reshape(B, S, H, 2 * d)
    o = rms_norm(o, subln_g) * (1.0 - lambda_init)
    return o.reshape(B, S, H * 2 * d) @ w_out


def conv_ffn(xn, w_up, conv_w, conv_b, w_down):
    S = xn.shape[1]
    h = xn @ w_up
    hp = jnp.pad(h, ((0, 0), (CONV_WIDTH - 1, 0), (0, 0)))
    hc = conv_b
    for j in range(CONV_WIDTH):
        hc = hc + hp[:, j:j + S] * conv_w[j]
    g, u = jnp.split(hc, 2, axis=-1)
    return (jax.nn.silu(g) * u) @ w_down


def setup_inputs(seed: int = 0) -> dict:
    key = jax.random.key(seed)
    ks = jax.random.split(key, 24)
    D, F = D_MODEL, D_FF
    ret_in = 2 * RET_HEADS * RET_QK_DIM + 2 * RET_HEADS * RET_V_DIM
    ret_v = RET_HEADS * RET_V_DIM
    diff_w = DIFF_HEADS * 2 * DIFF_HEAD_DIM

    def w(k, shape, fan_in):
        return jax.random.normal(k, shape, jnp.float32) * (fan_in ** -0.5)

    def gain(k, shape):
        return 1.0 + 0.02 * jax.random.normal(k, shape, jnp.float32)

    return {
        "x": jax.random.normal(ks[0], (BATCH, SEQ, D), jnp.float32),
        "a_norm_pre": gain(ks[1], (N_A_LAYERS, D)),
        "a_norm_post": gain(ks[2], (N_A_LAYERS, D)),
        "a_w_in": w(ks[3], (N_A_LAYERS, D, ret_in), D),
        "a_w_out": w(ks[4], (N_A_LAYERS, ret_v, D), ret_v),
        "kv_norm": gain(ks[5], (D,)),
        "w_kv": w(ks[6], (D, 2 * diff_w), D),
        "b_norm_pre": gain(ks[7], (N_B_LAYERS, D)),
        "b_norm_post": gain(ks[8], (N_B_LAYERS, D)),
        "b_w_q": w(ks[9], (N_B_LAYERS, D, diff_w), D),
        "b_lambda": 0.1 * jax.random.normal(ks[10], (N_B_LAYERS, 4, DIFF_HEAD_DIM), jnp.float32),
        "b_subln": gain(ks[11], (N_B_LAYERS, 2 * DIFF_HEAD_DIM)),
        "b_w_out": w(ks[12], (N_B_LAYERS, diff_w, D), diff_w),
        "ffn_norm_pre": gain(ks[13], (DEPTH, D)),
        "ffn_norm_post": gain(ks[14], (DEPTH, D)),
        "ffn_w_up": w(ks[15], (DEPTH, D, 2 * F), D),
        "ffn_conv_w": w(ks[16], (DEPTH, CONV_WIDTH, 2 * F), CONV_WIDTH),
        "ffn_conv_b": 0.02 * jax.random.normal(ks[17], (DEPTH, 2 * F), jnp.float32),
        "ffn_w_down": w(ks[18], (DEPTH, F, D), F),
    }


def reference(x, a_norm_pre, a_norm_post, a_w_in, a_w_out, kv_norm, w_kv,
              b_norm_pre, b_norm_post, b_w_q, b_lambda, b_subln, b_w_out,
              ffn_norm_pre, ffn_norm_post, ffn_w_up, ffn_conv_w, ffn_conv_b, ffn_w_down):
    B, S, _ = x.shape
    H, d = DIFF_HEADS, DIFF_HEAD_DIM
    k_sh = None
    v_sh = None
    for layer in range(DEPTH):
        if layer < N_A_LAYERS:
            i = layer
            h = retention(rms_norm(x, a_norm_pre[i]), a_w_in[i], a_w_out[i])
            x = x + rms_norm(h, a_norm_post[i])
        else:
            i = layer - N_A_LAYERS
            if i == 0:
                kv = rms_norm(x, kv_norm) @ w_kv
                k_flat, v_flat = jnp.split(kv, 2, axis=-1)
                k_sh = k_flat.reshape(B, S, H, 2, d)
                v_sh = v_flat.reshape(B, S, H, 2 * d)
            lambda_init = 0.8 - 0.6 * math.exp(-0.3 * layer)
            h = diff_attention(rms_norm(x, b_norm_pre[i]), k_sh, v_sh, b_w_q[i],
                               b_lambda[i], b_subln[i], b_w_out[i], lambda_init)
            x = x + rms_norm(h, b_norm_post[i])
        h = conv_ffn(rms_norm(x, ffn_norm_pre[layer]), ffn_w_up[layer],
                     ffn_conv_w[layer], ffn_conv_b[layer], ffn_w_down[layer])
        x = x + rms_norm(h, ffn_norm_post[layer])
    return x
```

```python
import math
from contextlib import ExitStack
import numpy as np
import concourse.bass as bass
import concourse.mybir as mybir
from concourse.bass_utils import run_bass_kernel_spmd

F32 = mybir.dt.float32
BF16 = mybir.dt.bfloat16
U8 = mybir.dt.uint8
ALU = mybir.AluOpType
AF = mybir.ActivationFunctionType
AX = mybir.AxisListType

ENGS = ("pe", "act", "dve", "pool", "sp")
KB = 1024
EPS = 1e-6
T = 2048
NRNG = 4
H_RET = 4
LAMBDA_INIT = 0.8 - 0.6 * math.exp(-0.3 * 1)


class _Rec:
    def __init__(self):
        self.call = None

    def __getattr__(self, name):
        def f(*a, **k):
            self.call = (name, a, k)
            return self
        return f


def _capture(fn):
    r = _Rec()
    fn(r)
    assert r.call is not None
    return r.call


class Sched:
    def __init__(self, nc):
        self.nc = nc
        self.ops = {e: [] for e in ENGS}
        self.count = {e: 0 for e in ENGS}
        self.actors = list(ENGS)
        self.sems = {}
        self.clock = {e: {} for e in ENGS}
        self.ev_clock = {}
        self.res = {}
        self.dma_count = {}
        self.n_waits = 0
        self.n_ops = 0

    def _need(self, eng, ev, waits):
        a, c = ev
        if self.clock[eng].get(a, 0) >= c:
            return
        waits[a] = max(waits.get(a, 0), c)

    def _deps(self, eng, reads, writes, strict=False):
        waits = {}
        for k in reads:
            st = self.res.get(k)
            if st is None:
                continue
            if st[0] is not None:
                self._need(eng, st[0], waits)
        for k in writes:
            st = self.res.get(k)
            if st is None:
                continue
            w = st[0]
            if w is not None and (strict or w[0] != eng or eng != "pe"):
                self._need(eng, w, waits)
            for r in st[1]:
                if strict or r[0] != eng or eng != "pe":
                    self._need(eng, r, waits)
        return waits

    def _apply_waits(self, eng, waits):
        for a, c in waits.items():
            if self.clock[eng].get(a, 0) >= c:
                continue
            self.ops[eng].append(("wait", a, c))
            self.n_waits += 1
            ck = self.ev_clock.get((a, c))
            if ck:
                mine = self.clock[eng]
                for k2, v2 in ck.items():
                    if mine.get(k2, 0) < v2:
                        mine[k2] = v2
            if self.clock[eng].get(a, 0) < c:
                self.clock[eng][a] = c

    def _record(self, ev, reads, writes):
        for k in reads:
            st = self.res.setdefault(k, [None, []])
            st[1].append(ev)
        for k in writes:
            self.res[k] = [ev, []]

    def op(self, eng, fn, reads=(), writes=()):
        self._apply_waits(eng, self._deps(eng, reads, writes))
        self.count[eng] += 1
        ev = (eng, self.count[eng])
        self.ops[eng].append(("op", _capture(fn), eng, 1))
        snap = dict(self.clock[eng])
        snap[eng] = ev[1]
        self.ev_clock[ev] = snap
        self._record(ev, reads, writes)
        self.n_ops += 1
        return ev

    def dma(self, queue, fn, actor, reads=(), writes=()):
        if actor not in self.dma_count:
            self.dma_count[actor] = 0
            self.actors.append(actor)
        self._apply_waits(queue, self._deps(queue, reads, writes, strict=True))
        self.dma_count[actor] += 16
        ev = (actor, self.dma_count[actor])
        self.ops[queue].append(("op", _capture(fn), actor, 16))
        snap = dict(self.clock[queue])
        snap[actor] = ev[1]
        self.ev_clock[ev] = snap
        self._record(ev, reads, writes)
        return ev

    def _targets(self):
        t = {e: self.count[e] for e in ENGS if self.count[e] > 0}
        for a, c in self.dma_count.items():
            if c > 0:
                t[a] = c
        return t

    def barrier(self):
        t = self._targets()
        for e in ENGS:
            self._apply_waits(e, dict(t))
        self.res = {}

    def soft_barrier(self, engs=("act", "dve", "pool")):
        t = {e: self.count[e] for e in engs if self.count[e] > 0}
        for e in engs:
            w = dict(t)
            w.pop(e, None)
            self._apply_waits(e, w)

    def final_wait(self, eng="sp"):
        t = self._targets()
        t.pop(eng, None)
        self._apply_waits(eng, t)

    def emit(self):
        nc = self.nc
        with ExitStack() as st:
            for a in self.actors:
                self.sems[a] = st.enter_context(nc.semaphore("s_" + a))
            block = st.enter_context(nc.Block())
            engmap = {"pe": block.tensor, "act": block.scalar, "dve": block.vector,
                      "pool": block.gpsimd, "sp": block.sync}

            def make(ename):
                lst = self.ops[ename]

                def body(eng):
                    for item in lst:
                        if item[0] == "wait":
                            eng.wait_ge(self.sems[item[1]], item[2])
                        else:
                            _, call, actor, inc = item
                            getattr(eng, call[0])(*call[1], **call[2]).then_inc(self.sems[actor], inc)
                return body

            for ename in ENGS:
                if self.ops[ename]:
                    engmap[ename](make(ename))


PHASES = ["A", "F0", "B", "F1"]


def build(stop_after="F1"):
    nc = bass.Bass("TRN2", target_bir_lowering=False)
    S = Sched(nc)

    def din(name, shape):
        return nc.dram_tensor(name, shape, F32, kind="ExternalInput").ap()

    xT_d = din("xT", [1024, T])
    a_w_in = din("a_w_in", [1024, 6144])
    a_w_out = din("a_w_out", [2048, 1024])
    w_kv = din("w_kv", [1024, 2048])
    b_w_q = din("b_w_q", [1024, 1024])
    b_w_out = din("b_w_out", [1024, 1024])
    w_up_r = [din(f"w_up{l}", [22, 128, 2048]) for l in range(2)]
    w_down_r = [din(f"w_down{l}", [8, 128, 2816]) for l in range(2)]
    cmat_d = din("cmat", [128, 4 * 128])
    cossin_d = din("cossin", [128, 2 * T])
    rtab_d = din("rtab", [128, 4 * 640])
    small_d = din("small", [128, 512])
    lamp_d = din("lamp", [128, 256])
    outT = nc.dram_tensor("outT", [1024, T], F32, kind="ExternalOutput").ap()
    xT_v = xT_d.rearrange("(c p) t -> p c t", p=128)
    outT_v = outT.rearrange("(c p) t -> p c t", p=128)

    ARENA = 212736
    arena = nc.alloc_sbuf_tensor("arena", [128, ARENA], U8)
    psum = nc.alloc_psum_tensor("psum", [128, 4096], F32)

    def carve(off, dims, dt):
        nb = int(np.prod(dims)) * (4 if dt == F32 else 2)
        assert off % 32 == 0 and off + nb <= ARENA, (off, nb)
        v = arena[:, off:off + nb].bitcast(dt)
        if len(dims) == 2:
            v = v.rearrange("p (a b) -> p a b", a=dims[0])
        elif len(dims) == 3:
            v = v.rearrange("p (a b c) -> p a b c", a=dims[0], b=dims[1])
        return v

    def PS(b, n=512, off=0):
        return psum[:, b * 512 + off:b * 512 + off + n]

    cmat = carve(0, [4, 128], BF16)
    ones_bf = cmat[:, 0, :]
    ident_bf = cmat[:, 1, :]
    cmask_bf = cmat[:, 2, :]
    maskneg_bf = cmat[:, 3, :]
    small = carve(1 * KB, [512], F32)
    GAIN0, CONV0, KDEC0, SUBLN0 = 0, 72, 424, 428
    lamv = small[:, 448:464]
    neghalf = carve(3 * KB, [512], BF16)
    lamp = carve(4 * KB, [256], F32)
    BIG = 5 * KB
    O = BIG + 64 * KB

    def gain(gi, c):
        return small[:, GAIN0 + gi * 8 + c:GAIN0 + gi * 8 + c + 1]

    def xT_r(r):
        return carve(BIG + r * 16 * KB, [8, 512], F32)

    S.dma("pool", lambda e: e.dma_start(out=cmat, in_=cmat_d.rearrange("p (a b) -> p a b", a=4)), "cst", writes=["cmat"])
    S.dma("sp", lambda e: e.dma_start(out=small, in_=small_d), "cst2", writes=["small"])
    S.dma("sp", lambda e: e.dma_start(out=lamp, in_=lamp_d), "cst3", writes=["lamp"])
    S.op("pool", lambda e: e.memset(neghalf, -0.5), writes=["neghalf"])
    epsb = small[:, 464:465]

    rot = {"sq": 0, "dvepool": 0}

    def alt_engine():
        rot["dvepool"] ^= 1
        return "dve" if rot["dvepool"] else "pool"

    def stats_rstd(src_fn, src_keys, sq_tiles, rstd_ap, rstd_key, bank=7):
        for c in range(8):
            j = rot["sq"] % len(sq_tiles)
            rot["sq"] += 1
            sq = sq_tiles[j]
            src = src_fn(c)
            S.op("act", lambda e, sq=sq, src=src: e.activation(out=sq, in_=src, func=AF.Square),
                 reads=src_keys(c) + ["small"], writes=[f"sqt{j}"])
            S.op("pe", lambda e, sq=sq, c=c: e.matmul(PS(bank), lhsT=ones_bf, rhs=sq, start=(c == 0), stop=(c == 7)),
                 reads=[f"sqt{j}", "cmat"], writes=[f"ps{bank}"])
        S.op("act", lambda e: e.activation(out=rstd_ap, in_=PS(bank), func=AF.Sqrt, bias=epsb, scale=1.0),
             reads=[f"ps{bank}", "small"], writes=[rstd_key])
        S.op("dve", lambda e: e.reciprocal(out=rstd_ap, in_=rstd_ap),
             reads=[rstd_key], writes=[rstd_key])

    def apply_norm(src_fn, src_keys, gi, rstd_ap, rstd_key, dst_fn, dst_keys):
        for c in range(8):
            eng = "dve"
            S.op(eng, lambda e, c=c: e.scalar_tensor_tensor(out=dst_fn(c), in0=src_fn(c), scalar=gain(gi, c),
                                                            in1=rstd_ap, op0=ALU.mult, op1=ALU.mult),
                 reads=src_keys(c) + [rstd_key, "small"], writes=dst_keys(c))

    def post_residual(r, hbuf, hkey, gi, sq_tiles, rstd_ap, rstd_key, xsrc_fn, xsrc_keys, xdst_keys):
        stats_rstd(lambda c: hbuf[:, c, :], lambda c: [f"{hkey}{c}"], sq_tiles, rstd_ap, rstd_key)
        xr = xT_r(r)
        for c in range(8):
            S.op("dve", lambda e, c=c: e.scalar_tensor_tensor(out=hbuf[:, c, :], in0=hbuf[:, c, :], scalar=gain(gi, c),
                                                              in1=rstd_ap, op0=ALU.mult, op1=ALU.mult),
                 reads=[f"{hkey}{c}", rstd_key, "small"], writes=[f"{hkey}{c}"])
            S.op("pool", lambda e, c=c: e.tensor_tensor(out=xr[:, c, :], in0=hbuf[:, c, :], in1=xsrc_fn(c), op=ALU.add),
                 reads=[f"{hkey}{c}"] + xsrc_keys(c), writes=xdst_keys(c))

    def proj_to_hbuf(r, nk, lhsT_fn, rhs_fn, in_keys, hbuf, hkey, banks=(0, 1, 2, 3, 4, 5)):
        for dc in range(8):
            b = banks[(r * 8 + dc) % len(banks)]
            for k in range(nk):
                S.op("pe", lambda e, k=k, dc=dc, b=b: e.matmul(PS(b), lhsT=lhsT_fn(k, dc), rhs=rhs_fn(k),
                                                               start=(k == 0), stop=(k == nk - 1)),
                     reads=in_keys(k), writes=[f"ps{b}"])
            S.op("act", lambda e, dc=dc, b=b: e.copy(out=hbuf[:, dc, :], in_=PS(b)),
                 reads=[f"ps{b}"], writes=[f"{hkey}{dc}"])

    def proj_post(r, nk, lhsT_fn, rhs_fn, in_keys, hbuf, hkey, gi, sq_tiles, rstd_ap, rstd_key,
                  xsrc_fn, xsrc_keys, xdst_keys, banks=(0, 1, 2, 3, 4), sbank=7):
        pend = None
        for dc in range(8):
            b = banks[(r * 8 + dc) % len(banks)]
            for k in range(nk):
                S.op("pe", lambda e, k=k: e.matmul(PS(b), lhsT=lhsT_fn(k, dc), rhs=rhs_fn(k), start=(k == 0), stop=(k == nk - 1)),
                     reads=in_keys(k), writes=[f"ps{b}"])
            if pend is not None:
                pend()
            S.op("act", lambda e: e.mul(out=hbuf[:, dc, :], in_=PS(b), mul=gain(gi, dc)),
                 reads=[f"ps{b}", "small"], writes=[f"{hkey}{dc}"])
            j = rot["sq"] % len(sq_tiles)
            rot["sq"] += 1
            S.op("act", lambda e: e.activation(out=sq_tiles[j], in_=PS(b), func=AF.Square),
                 reads=[f"ps{b}"], writes=[f"sqt{j}"])

            def pend(j=j, dc=dc):
                S.op("pe", lambda e: e.matmul(PS(sbank), lhsT=ones_bf, rhs=sq_tiles[j], start=(dc == 0), stop=(dc == 7)),
                     reads=[f"sqt{j}", "cmat"], writes=[f"ps{sbank}"])
        pend()
        S.op("act", lambda e: e.activation(out=rstd_ap, in_=PS(sbank), func=AF.Sqrt, bias=epsb, scale=1.0),
             reads=[f"ps{sbank}", "small"], writes=[rstd_key])
        S.op("dve", lambda e: e.reciprocal(out=rstd_ap, in_=rstd_ap), reads=[rstd_key], writes=[rstd_key])
        xr = xT_r(r)
        for c in range(8):
            S.op("dve", lambda e: e.tensor_tensor(out=hbuf[:, c, :], in0=hbuf[:, c, :], in1=rstd_ap, op=ALU.mult),
                 reads=[f"{hkey}{c}", rstd_key], writes=[f"{hkey}{c}"])
            eng = "dve" if c % 3 == 2 else "pool"
            S.op(eng, lambda e: e.tensor_tensor(out=xr[:, c, :], in0=hbuf[:, c, :], in1=xsrc_fn(c), op=ALU.add),
                 reads=[f"{hkey}{c}"] + xsrc_keys(c), writes=xdst_keys(c))

    def ffn_prenorm0(l):
        xnh = carve(O, [8, 1024], BF16)
        sqf = [carve(O + 16 * KB + i * KB, [512], BF16) for i in range(3)]
        rstd_f = [carve(O + 16 * KB + 3 * KB + i * 2 * KB, [512], F32) for i in range(2)]
        for rr in range(2):
            xr = xT_r(rr)
            stats_rstd(lambda c: xr[:, c, :], lambda c: [f"big{rr}"], sqf, rstd_f[rr], f"rstdf{rr}", bank=5)
            apply_norm(lambda c: xr[:, c, :], lambda c: [f"big{rr}"], 5 + l, rstd_f[rr], f"rstdf{rr}",
                       lambda c: xnh[:, c, rr * 512:(rr + 1) * 512], lambda c: [f"xnh{c}_{rr}"])

    XN = O
    xn = carve(XN, [8, T], BF16)
    WQK = O + 32 * KB
    WVG = O + 40 * KB
    QKB = O + 72 * KB
    ROT = O + 80 * KB
    CS = O + 88 * KB
    RTAB = O + 104 * KB
    MISC = O + 114 * KB
    wqk = carve(WQK, [8, 512], BF16)
    wvg = [carve(WVG + i * 16 * KB, [8, 1024], BF16) for i in range(2)]
    qT = carve(QKB, [2, 512], BF16)
    qdT = carve(QKB + 2 * KB, [2, 512], BF16)
    kT = carve(QKB + 4 * KB, [2, 512], BF16)
    ktok = carve(QKB + 6 * KB, [4, 256], BF16)
    rt = [carve(ROT + i * 2 * KB, [512], F32) for i in range(4)]
    cossin = carve(CS, [2, T], F32)
    rtab = carve(RTAB, [4, 640], F32)
    mo = [MISC]

    def misc(dims, dt):
        nb = int(np.prod(dims)) * (4 if dt == F32 else 2)
        nb = (nb + 31) // 32 * 32
        v = carve(mo[0], dims, dt)
        mo[0] += nb
        return v

    vtok = [misc([512], BF16) for _ in range(3)]
    sgt = [misc([512], BF16) for _ in range(3)]
    sTb = [misc([128], BF16) for _ in range(2)]
    Sf = misc([2, 512], F32)
    Sb = [misc([2, 512], BF16) for _ in range(2)]
    ybf = [misc([512], BF16) for _ in range(2)]
    sqj = misc([512], BF16)
    sqa = [misc([512], BF16) for _ in range(2)]
    ssr = misc([8], F32)
    rstd_a = [misc([512], F32) for _ in range(2)]
    assert mo[0] <= ARENA, mo[0]

    wout_a = carve(WVG + ((H_RET - 2) % 2) * 16 * KB, [8, 1024], BF16)
    wout_b = carve(CS, [8, 1024], BF16)

    def load_wqk(h):
        S.dma("pool", lambda e: e.dma_start(out=wqk[:, :, 0:256],
                                            in_=a_w_in[:, h * 256:(h + 1) * 256].rearrange("(c p) n -> p c n", p=128)),
              "wqk_q", writes=["wqk_q"])
        S.dma("pool", lambda e: e.dma_start(out=wqk[:, :, 256:512],
                                            in_=a_w_in[:, 1024 + h * 256:1024 + (h + 1) * 256].rearrange("(c p) n -> p c n", p=128)),
              "wqk_k", writes=["wqk_k"])

    def load_wvg(h):
        sl = h % 2
        S.dma("pool", lambda e: e.dma_start(out=wvg[sl][:, :, 0:512],
                                            in_=a_w_in[:, 2048 + h * 512:2048 + (h + 1) * 512].rearrange("(c p) n -> p c n", p=128)),
              f"wvg_v{sl}", writes=[f"wvg_v{sl}"])
        S.dma("pool", lambda e: e.dma_start(out=wvg[sl][:, :, 512:1024],
                                            in_=a_w_in[:, 4096 + h * 512:4096 + (h + 1) * 512].rearrange("(c p) n -> p c n", p=128)),
              f"wvg_g{sl}", writes=[f"wvg_g{sl}"])

    load_wqk(0)
    load_wvg(0)

    def a0_dma(r):
        sl = r % 2
        xs = carve(BIG + (2 + sl) * 16 * KB, [8, 512], F32)
        S.dma("sp", lambda e: e.dma_start(out=xs, in_=xT_v[:, :, r * 512:(r + 1) * 512]), f"xs{sl}",
              writes=[f"xs{sl}_{c}" for c in range(8)] + [f"big{2 + sl}"])

    def a0_range(r):
        sl = r % 2
        xs = carve(BIG + (2 + sl) * 16 * KB, [8, 512], F32)
        stats_rstd(lambda c: xs[:, c, :], lambda c: [f"xs{sl}_{c}", f"big{2 + sl}"], sqa, rstd_a[sl], f"rstda{sl}")
        apply_norm(lambda c: xs[:, c, :], lambda c: [f"xs{sl}_{c}", f"big{2 + sl}"], 0, rstd_a[sl], f"rstda{sl}",
                   lambda c: xn[:, c, r * 512:(r + 1) * 512], lambda c: [f"xn{c}_{r}"])
        if r + 2 < NRNG:
            a0_dma(r + 2)

    a0_dma(0)
    a0_dma(1)
    S.dma("sp", lambda e: e.dma_start(out=cossin, in_=cossin_d.rearrange("p (a b) -> p a b", a=2)), "cs", writes=["cossin"])
    S.dma("sp", lambda e: e.dma_start(out=rtab, in_=rtab_d.rearrange("p (a b) -> p a b", a=4)), "rtab", writes=["rtab"])

    GAMMA = [1.0 - 2.0 ** (-5.0 - h) for h in range(H_RET)]
    cnt = {"v": 0, "sT": 0, "Sb": 0, "y": 0}
    pend_tr = {"f": None}
    for h in range(H_RET):
        wv = wvg[h % 2]
        wvk = f"wvg_v{h % 2}"
        wgk = f"wvg_g{h % 2}"
        gC = GAMMA[h] ** 128
        for r in range(NRNG):
            tok = slice(r * 512, (r + 1) * 512)
            if h == 0 and r == 0:
                a0_range(0)
            for dc in range(4):
                wkey = "wqk_q" if dc < 2 else "wqk_k"
                for kc in range(8):
                    S.op("pe", lambda e, dc=dc, kc=kc: e.matmul(PS(dc), lhsT=wqk[:, kc, dc * 128:(dc + 1) * 128],
                                                                rhs=xn[:, kc, tok], start=(kc == 0), stop=(kc == 7)),
                         reads=[wkey, f"xn{kc}_{r}"], writes=[f"ps{dc}"])
            if r == NRNG - 1 and h + 1 < H_RET:
                load_wqk(h + 1)
            if h == 0 and r == 0:
                load_wvg(1)
            cos_r = cossin[:, 0, tok]
            sin_r = cossin[:, 1, tok]
            for qk in range(2):
                b1, b2 = 2 * qk, 2 * qk + 1
                S.op("dve", lambda e: e.tensor_tensor(out=rt[0], in0=PS(b1), in1=cos_r, op=ALU.mult),
                     reads=[f"ps{b1}", "cossin"], writes=["rt0"])
                S.op("dve", lambda e: e.tensor_tensor(out=rt[1], in0=PS(b2), in1=sin_r, op=ALU.mult),
                     reads=[f"ps{b2}", "cossin"], writes=["rt1"])
                S.op("dve", lambda e: e.tensor_tensor(out=rt[2], in0=PS(b1), in1=sin_r, op=ALU.mult),
                     reads=[f"ps{b1}", "cossin"], writes=["rt2"])
                S.op("dve", lambda e: e.tensor_tensor(out=rt[3], in0=PS(b2), in1=cos_r, op=ALU.mult),
                     reads=[f"ps{b2}", "cossin"], writes=["rt3"])
                if qk == 0:
                    S.op("dve", lambda e: e.tensor_tensor(out=rt[0], in0=rt[0], in1=rt[1], op=ALU.subtract),
                         reads=["rt0", "rt1"], writes=["rt0"])
                    S.op("dve", lambda e: e.tensor_tensor(out=rt[2], in0=rt[2], in1=rt[3], op=ALU.add),
                         reads=["rt2", "rt3"], writes=["rt2"])
                    S.op("act", lambda e: e.copy(out=qT[:, 0, :], in_=rt[0]), reads=["rt0"], writes=["qT0"])
                    S.op("act", lambda e: e.copy(out=qT[:, 1, :], in_=rt[2]), reads=["rt2"], writes=["qT1"])
                    S.op("dve", lambda e: e.tensor_tensor(out=qdT[:, 0, :], in0=rt[0], in1=rtab[:, h, 128:640], op=ALU.mult),
                         reads=["rt0", "rtab"], writes=["qdT0"])
                    S.op("dve", lambda e: e.tensor_tensor(out=qdT[:, 1, :], in0=rt[2], in1=rtab[:, h, 128:640], op=ALU.mult),
                         reads=["rt2", "rtab"], writes=["qdT1"])
                else:
                    S.op("pool", lambda e: e.tensor_tensor(out=kT[:, 0, :], in0=rt[0], in1=rt[1], op=ALU.subtract),
                         reads=["rt0", "rt1"], writes=["kT0"])
                    S.op("dve", lambda e: e.tensor_tensor(out=kT[:, 1, :], in0=rt[2], in1=rt[3], op=ALU.add),
                         reads=["rt2", "rt3"], writes=["kT1"])

            if h == H_RET - 1 and r == NRNG - 1:
                S.dma("pool", lambda e: e.dma_start(out=wout_b, in_=a_w_out[1024:2048, :].rearrange("(c p) n -> p c n", p=128)),
                      "wout1", writes=["wout1", "cossin"])

            def emit_vg(n):
                gn = r * 4 + n
                tk = slice(gn * 128, (gn + 1) * 128)
                vs = cnt["v"] % 3
                cnt["v"] += 1
                for kc in range(8):
                    S.op("pe", lambda e, kc=kc: e.matmul(PS(0), lhsT=xn[:, kc, tk], rhs=wv[:, kc, 0:512],
                                                         start=(kc == 0), stop=(kc == 7)),
                         reads=[wvk, f"xn{kc}_{r}"], writes=["ps0"])
                for kc in range(8):
                    S.op("pe", lambda e, kc=kc: e.matmul(PS(1), lhsT=xn[:, kc, tk], rhs=wv[:, kc, 512:1024],
                                                         start=(kc == 0), stop=(kc == 7)),
                         reads=[wgk, f"xn{kc}_{r}"], writes=["ps1"])
                if gn == 15 and h + 2 < H_RET:
                    load_wvg(h + 2)
                if gn == 15 and h == H_RET - 2:
                    S.dma("pool", lambda e: e.dma_start(out=wout_a, in_=a_w_out[0:1024, :].rearrange("(c p) n -> p c n", p=128)),
                          "wout0", writes=["wout0", f"wvg_v{h % 2}", f"wvg_g{h % 2}"])
                S.op("act", lambda e: e.copy(out=vtok[vs], in_=PS(0)), reads=["ps0"], writes=[f"vtok{vs}"])
                S.op("act", lambda e: e.activation(out=sgt[vs], in_=PS(1), func=AF.Silu),
                     reads=["ps1"], writes=[f"sgt{vs}"])
                return vs

            vs_next = emit_vg(0)
            ptr = PS(6).bitcast(BF16)
            for n in range(4):
                for dc in range(2):
                    S.op("pe", lambda e, n=n, dc=dc: e.transpose(ptr[:, n * 256 + dc * 128:n * 256 + (dc + 1) * 128],
                                                                 kT[:, dc, n * 128:(n + 1) * 128], ident_bf),
                         reads=[f"kT{dc}", "cmat"], writes=["ps6"])
            S.op("act", lambda e: e.mul(out=ktok.rearrange("p a b -> p (a b)"), in_=ptr, mul=small[:, KDEC0 + h:KDEC0 + h + 1]),
                 reads=["ps6", "small"], writes=["ktok"])
            for n in range(4):
                gn = r * 4 + n
                ck = slice(n * 128, (n + 1) * 128)
                vs = vs_next
                ss = cnt["sT"] % 2
                cnt["sT"] += 1
                for dc in range(2):
                    S.op("pe", lambda e, dc=dc: e.matmul(PS(5, 128), lhsT=kT[:, dc, ck], rhs=qT[:, dc, ck],
                                                         start=(dc == 0), stop=(dc == 1)),
                         reads=[f"kT{dc}", f"qT{dc}"], writes=["ps5"])
                S.op("dve", lambda e: e.tensor_tensor(out=sTb[ss], in0=PS(5, 128), in1=rtab[:, h, 0:128], op=ALU.mult),
                     reads=["ps5", "rtab"], writes=[f"sTb{ss}"])
                if pend_tr["f"] is not None:
                    pend_tr["f"]()
                    pend_tr["f"] = None
                sb_prev = (cnt["Sb"] - 1) % 2
                S.op("pe", lambda e: e.matmul(PS(4), lhsT=sTb[ss], rhs=vtok[vs], start=True, stop=(gn == 0)),
                     reads=[f"sTb{ss}", f"vtok{vs}"], writes=["ps4"])
                if gn > 0:
                    for dc in range(2):
                        S.op("pe", lambda e, dc=dc: e.matmul(PS(4), lhsT=qdT[:, dc, ck], rhs=Sb[sb_prev][:, dc, :],
                                                             start=False, stop=(dc == 1)),
                             reads=[f"qdT{dc}", f"Sb{sb_prev}_{dc}"], writes=["ps4"])
                ys = cnt["y"] % 2
                cnt["y"] += 1
                S.op("act", lambda e: e.activation(out=sqj, in_=PS(4), func=AF.Square, accum_out=ssr[:, 0:1]),
                     reads=["ps4"], writes=["sqj", "ssr0"])
                S.op("dve", lambda e: e.tensor_scalar(out=ssr[:, 1:2], in0=ssr[:, 0:1], scalar1=1.0 / 512, scalar2=EPS,
                                                      op0=ALU.mult, op1=ALU.add),
                     reads=["ssr0"], writes=["ssr1"])
                S.op("pool", lambda e: e.tensor_tensor(out=ssr[:, 2:3], in0=ssr[:, 1:2], in1=neghalf[:, 0:1], op=ALU.pow),
                     reads=["ssr1", "neghalf"], writes=["ssr2"])
                if gn < 15:
                    sb_new = cnt["Sb"] % 2
                    cnt["Sb"] += 1
                    for dc in range(2):
                        S.op("pe", lambda e, dc=dc: e.matmul(PS(2 + dc), lhsT=ktok[:, n, dc * 128:(dc + 1) * 128],
                                                             rhs=vtok[vs], start=True, stop=True),
                             reads=["ktok", f"vtok{vs}"], writes=[f"ps{2 + dc}"])
                        if gn == 0:
                            S.op("dve", lambda e, dc=dc: e.tensor_copy(out=Sf[:, dc, :], in_=PS(2 + dc)),
                                 reads=[f"ps{2 + dc}"], writes=[f"Sf{dc}"])
                        else:
                            S.op("dve", lambda e, dc=dc: e.scalar_tensor_tensor(out=Sf[:, dc, :], in0=Sf[:, dc, :], scalar=gC,
                                                                                in1=PS(2 + dc), op0=ALU.mult, op1=ALU.add),
                                 reads=[f"ps{2 + dc}", f"Sf{dc}"], writes=[f"Sf{dc}"])
                        S.op("act", lambda e, dc=dc: e.copy(out=Sb[sb_new][:, dc, :], in_=Sf[:, dc, :]),
                             reads=[f"Sf{dc}"], writes=[f"Sb{sb_new}_{dc}"])
                S.op("dve", lambda e: e.scalar_tensor_tensor(out=ybf[ys], in0=PS(4), scalar=ssr[:, 2:3], in1=sgt[vs],
                                                             op0=ALU.mult, op1=ALU.mult),
                     reads=["ps4", "ssr2", f"sgt{vs}"], writes=[f"ybf{ys}"])
                if n < 3:
                    vs_next = emit_vg(n + 1)
                if h == 0 and n == 1 and r + 1 < NRNG:
                    a0_range(r + 1)

                def tr_fn(ys=ys, r=r, h=h, ck=ck):
                    pyT = PS(7).bitcast(BF16)
                    for ec in range(4):
                        S.op("pe", lambda e, ec=ec: e.transpose(pyT[:, ec * 128:(ec + 1) * 128],
                                                                ybf[ys][:, ec * 128:(ec + 1) * 128], ident_bf),
                             reads=[f"ybf{ys}", "cmat"], writes=["ps7"])
                    yT_r = carve(BIG + r * 16 * KB, [16, 512], BF16)
                    S.op("act", lambda e: e.copy(out=yT_r[:, h * 4:(h + 1) * 4, ck],
                                                 in_=pyT[:, 0:512].rearrange("p (a b) -> p a b", a=4)),
                         reads=["ps7"], writes=[f"big{r}"])
                pend_tr["f"] = tr_fn
    pend_tr["f"]()
    S.barrier()

    hbufs = [carve(MISC, [8, 512], F32),
             carve(O + 24 * KB, [8, 512], F32)]
    if stop_after != "A":
        for fc in range(2):
            S.dma("pool", lambda e, fc=fc: e.dma_start(out=carve(O + 72 * KB + fc * 4 * KB, [2048], BF16), in_=w_up_r[0][fc]),
                  f"wu{fc}", writes=[f"wu{fc}"])
    xs2 = carve(WVG + (1 - (H_RET - 2) % 2) * 16 * KB, [8, 512], F32)
    def wout(k, dc):
        return (wout_a if k < 8 else wout_b)[:, k % 8, dc * 128:(dc + 1) * 128]

    for r in range(NRNG):
        yT_r = carve(BIG + r * 16 * KB, [16, 512], BF16)
        S.dma("sp", lambda e, r=r: e.dma_start(out=xs2, in_=xT_v[:, :, r * 512:(r + 1) * 512]), "xs2",
              writes=[f"xs2_{c}" for c in range(8)])
        proj_post(r, 16, wout, lambda k: yT_r[:, k, :],
                  lambda k: [f"wout{k // 8}", f"big{r}"], hbufs[r % 2], f"hb{r % 2}_", 1, sqa, rstd_a[r % 2], f"rstda{r % 2}",
                  lambda c: xs2[:, c, :], lambda c: [f"xs2_{c}"], lambda c: [f"big{r}"])
        if r == 2 and stop_after != "A":
            ffn_prenorm0(0)
    if stop_after == "A":
        S.barrier()
    else:
        S.soft_barrier()

    early_out = {"done": False}

    def finish():
        for r in range(2 if early_out["done"] else 0, NRNG):
            S.dma("sp", lambda e, r=r: e.dma_start(out=outT_v[:, :, r * 512:(r + 1) * 512], in_=xT_r(r)), "outd",
                  reads=[f"big{r}"])
        S.final_wait("sp")
        S.emit()
        return nc, S

    if stop_after == "A":
        return finish()

    def ffn(l, pre_done=False, wu_done=0, wu_alias=None):
        XNH = O
        TMP = O + 16 * KB
        MT = O + 28 * KB
        UP = O + 72 * KB
        NWU = 3
        NWD = 3
        xnh = carve(XNH, [8, 1024], BF16)
        sqf = [carve(TMP + i * KB, [512], BF16) for i in range(3)]
        rstd_f = [carve(TMP + 3 * KB + i * 2 * KB, [512], F32) for i in range(2)]
        halo = carve(TMP + 7 * KB, [44, 2], F32)
        rstd_p = [carve(TMP + 8 * KB + i * 2 * KB, [512], F32) for i in range(2)]
        mT = carve(MT, [22, 1024], BF16)
        wu = [carve(UP + i * 4 * KB, [8, 256], BF16) for i in range(NWU)]
        WDC = UP + NWU * 4 * KB
        wdc = [carve(WDC + i * 5632, [22, 128], BF16) for i in range(NWD)]
        HS = WDC + NWD * 5632
        hs = [[carve(HS + (s * 2 + j) * 4128, [1032], F32) for j in range(2)] for s in range(2)]
        AC = HS + 4 * 4128
        acc = [[carve(AC + (s * 2 + j) * 4 * KB, [1024], F32) for j in range(2)] for s in range(2)]
        SG = AC + 16 * KB
        sgf = [carve(SG + s * 2 * KB, [1024], BF16) for s in range(2)]
        assert SG + 4 * KB <= ARENA
        hb2 = [carve(HS + i * 16 * KB, [8, 512], F32) for i in range(2)]

        def hb_overlap(off, size):
            ks = []
            for rr_ in range(2):
                for dc_ in range(8):
                    a0 = rr_ * 16384 + dc_ * 2048
                    if off < a0 + 2048 and off + size > a0:
                        ks.append(f"hb{rr_}_{dc_}")
            return ks

        hs_alias = [[hb_overlap((s_ * 2 + j_) * 4128, 4128) for j_ in range(2)] for s_ in range(2)]
        acc_alias = [[hb_overlap(16512 + (s_ * 2 + j_) * 4096, 4096) for j_ in range(2)] for s_ in range(2)]
        gi_pre, gi_post = 5 + l, 7 + l

        def cw(j, fc):
            o = CONV0 + (l * 4 + j) * 44 + fc
            return small[:, o:o + 1]

        first_load = {"n": 0}

        def load_wu(fc):
            ws = fc % NWU
            extra = []
            if wu_alias is not None and first_load["n"] < NWU:
                extra = wu_alias[ws]
            first_load["n"] += 1
            S.dma("pool", lambda e: e.dma_start(out=wu[ws].rearrange("p a b -> p (a b)"), in_=w_up_r[l][fc]),
                  f"wu{ws}", writes=[f"wu{ws}"] + extra)

        def load_wdc(dc):
            sl = dc % NWD
            S.dma("pool", lambda e: e.dma_start(out=wdc[sl].rearrange("p a b -> p (a b)"), in_=w_down_r[l][dc]),
                  f"wdc{sl}", writes=[f"wdc{sl}"])

        def prenorm(hf):
            for rr in range(2):
                r = hf * 2 + rr
                xr = xT_r(r)
                stats_rstd(lambda c: xr[:, c, :], lambda c: [f"big{r}"], sqf, rstd_f[rr], f"rstdf{rr}", bank=5)
                apply_norm(lambda c: xr[:, c, :], lambda c: [f"big{r}"], gi_pre, rstd_f[rr], f"rstdf{rr}",
                           lambda c: xnh[:, c, rr * 512:(rr + 1) * 512], lambda c: [f"xnh{c}_{rr}"])

        for fc in range(wu_done, NWU - 1):
            load_wu(fc)
        if not pre_done:
            prenorm(0)
        for hf in range(2):
            if hf == 0:
                for st in range(2):
                    for j in range(2):
                        S.op("dve", lambda e: e.memset(hs[st][j][:, 0:2], 0.0), writes=[f"hs{st}{j}h"])
            for fc in range(22):
                ws = fc % NWU
                st = fc % 2
                if fc + NWU - 1 < 22:
                    load_wu(fc + NWU - 1)
                if fc == 16:
                    load_wdc(0)
                if fc == 18:
                    load_wdc(1)
                if fc == 20:
                    load_wdc(2)
                for j in range(2):
                    bb = st * 4 + j * 2
                    for rr in range(2):
                        for kc in range(8):
                            S.op("pe", lambda e, kc=kc, rr=rr: e.matmul(
                                PS(bb + rr), lhsT=wu[ws][:, kc, j * 128:(j + 1) * 128], rhs=xnh[:, kc, rr * 512:(rr + 1) * 512],
                                start=(kc == 0), stop=(kc == 7)),
                                reads=[f"wu{ws}", f"xnh{kc}_{rr}"], writes=[f"ps{bb + rr}"])
                    hsj = hs[st][j]
                    acj = acc[st][j]
                    hk = f"hs{st}{j}"
                    ak = f"acc{st}{j}"
                    fcj = fc + 22 * j
                    hsa = hs_alias[st][j] if hf == 1 else []
                    aca = acc_alias[st][j] if hf == 1 else []
                    if hf == 1:
                        S.op("act", lambda e: e.copy(out=hsj[:, 0:2], in_=halo[:, fcj, :]),
                             reads=[f"halo{fcj}"], writes=[hk + "h"] + hsa)
                    S.op("act", lambda e: e.copy(out=hsj[:, 2:1026], in_=psum[:, bb * 512:bb * 512 + 1024]),
                         reads=[f"ps{bb}", f"ps{bb + 1}"], writes=[hk] + hsa)
                    if hf == 0:
                        S.op("act", lambda e: e.copy(out=halo[:, fcj, :], in_=psum[:, bb * 512 + 1022:bb * 512 + 1024]),
                             reads=[f"ps{bb + 1}"], writes=[f"halo{fcj}"])
                    S.op("act", lambda e: e.activation(
                        out=acj, in_=hsj[:, 0:1024], func=AF.Identity, bias=cw(3, fcj), scale=cw(0, fcj)),
                        reads=[hk, hk + "h", "small"], writes=[ak] + aca)
                    S.op("dve", lambda e: e.scalar_tensor_tensor(
                        out=acj, in0=hsj[:, 1:1025], scalar=cw(1, fcj), in1=acj, op0=ALU.mult, op1=ALU.add),
                        reads=[hk, hk + "h", ak, "small"], writes=[ak])
                    S.op("dve", lambda e: e.scalar_tensor_tensor(
                        out=acj, in0=hsj[:, 2:1026], scalar=cw(2, fcj), in1=acj, op0=ALU.mult, op1=ALU.add),
                        reads=[hk, ak, "small"], writes=[ak])
                S.op("act", lambda e: e.activation(out=sgf[st], in_=acc[st][0], func=AF.Silu),
                     reads=[f"acc{st}0"], writes=[f"sgf{st}"])
                S.op("pool", lambda e: e.tensor_tensor(out=mT[:, fc, :], in0=sgf[st], in1=acc[st][1], op=ALU.mult),
                     reads=[f"sgf{st}", f"acc{st}1"], writes=[f"mT{fc}"])
            S.soft_barrier()
            if hf == 0:
                for fc in range(NWU - 1):
                    load_wu(fc)
            groups = [(dc, rr) for dc in range(8) for rr in range(2)]
            pend = None

            def mm_range(gidx, k0, k1):
                dc, rr = groups[gidx]
                b = gidx % 5
                sl = dc % NWD
                for k in range(k0, k1):
                    S.op("pe", lambda e, k=k: e.matmul(PS(b), lhsT=wdc[sl][:, k, :], rhs=mT[:, k, rr * 512:(rr + 1) * 512],
                                                      start=(k == 0), stop=(k == 21)),
                         reads=[f"wdc{sl}", f"mT{k}"], writes=[f"ps{b}"])

            def evac(gidx):
                nonlocal pend
                dc, rr = groups[gidx]
                b = gidx % 5
                if pend is not None:
                    pend()
                S.op("act", lambda e: e.mul(out=hb2[rr][:, dc, :], in_=PS(b), mul=gain(gi_post, dc)),
                     reads=[f"ps{b}", "small"], writes=[f"hb{rr}_{dc}"])
                j = rot["sq"] % len(sqf)
                rot["sq"] += 1
                S.op("act", lambda e: e.activation(out=sqf[j], in_=PS(b), func=AF.Square),
                     reads=[f"ps{b}"], writes=[f"sqt{j}"])

                def pend(j=j, rr=rr, dc=dc):
                    S.op("pe", lambda e: e.matmul(PS(6 + rr), lhsT=ones_bf, rhs=sqf[j], start=(dc == 0), stop=(dc == 7)),
                         reads=[f"sqt{j}", "cmat"], writes=[f"ps{6 + rr}"])

            for gidx in range(3):
                mm_range(gidx, 0, 20)
            for gidx in range(16):
                dc, rr = groups[gidx]
                if gidx < 3:
                    mm_range(gidx, 20, 22)
                else:
                    mm_range(gidx, 0, 22)
                evac(gidx)
                if rr == 1:
                    if dc + NWD < 8:
                        load_wdc(dc + NWD)
                    if hf == 0 and dc == 2:
                        pend()
                        pend = None
                        prenorm(1)
            pend()
            for rr in range(2):
                S.op("act", lambda e: e.activation(out=rstd_p[rr], in_=PS(6 + rr), func=AF.Sqrt, bias=epsb, scale=1.0),
                     reads=[f"ps{6 + rr}", "small"], writes=[f"rstdp{rr}"])
            for rr in range(2):
                S.op("dve", lambda e: e.reciprocal(out=rstd_p[rr], in_=rstd_p[rr]), reads=[f"rstdp{rr}"], writes=[f"rstdp{rr}"])
            for c in range(8):
                for rr in range(2):
                    r = hf * 2 + rr
                    xr = xT_r(r)
                    S.op("dve", lambda e: e.tensor_tensor(out=hb2[rr][:, c, :], in0=hb2[rr][:, c, :], in1=rstd_p[rr], op=ALU.mult),
                         reads=[f"hb{rr}_{c}", f"rstdp{rr}"], writes=[f"hb{rr}_{c}"])
                    eng = "dve" if (2 * c + rr) % 3 == 2 else "pool"
                    S.op(eng, lambda e: e.tensor_tensor(out=xr[:, c, :], in0=hb2[rr][:, c, :], in1=xr[:, c, :], op=ALU.add),
                         reads=[f"hb{rr}_{c}", f"big{r}"], writes=[f"big{r}"])
            if hf == 0:
                if l == 1 and stop_after == "F1":
                    for r in range(2):
                        S.dma("sp", lambda e, r=r: e.dma_start(out=outT_v[:, :, r * 512:(r + 1) * 512], in_=xT_r(r)), "outd",
                              reads=[f"big{r}"])
                    early_out["done"] = True
        S.barrier()

    ffn(0, pre_done=True, wu_done=2)
    if stop_after == "F0":
        return finish()

    KT = O
    VT = O + 32 * KB
    R1 = O + 65 * KB
    R2 = O + 97 * KB
    SM = O + 129 * KB
    kTb = carve(KT, [8, T], BF16)
    vaug = carve(VT, [16, 8, 129], BF16)
    xnb = carve(R1, [8, T], BF16)
    oT = carve(R1, [8, T], BF16)
    wk = [carve(R2 + i * 2 * KB, [8, 128], BF16) for i in range(3)]
    wvb = carve(R2 + 6 * KB, [8, 1024], BF16)
    rstd_b = carve(R2 + 22 * KB, [T], F32)
    qTb = carve(R2, [8, T], BF16)
    wbo = carve(R2, [8, 1024], BF16)
    hbb = carve(R2 + 16 * KB, [8, 512], F32)
    so = [SM]

    def sm(dims, dt):
        nb = int(np.prod(dims)) * (4 if dt == F32 else 2)
        nb = (nb + 31) // 32 * 32
        v = carve(so[0], dims, dt)
        so[0] += nb
        return v

    so[0] += 3 * KB
    wq = [sm([8, 128], BF16) for _ in range(2)]
    sqb = [carve(R2 + 30 * KB + i * KB, [512], BF16) for i in range(2)]
    of32 = [sm([128], F32) for _ in range(2)]
    onb = [sm([128], BF16) for _ in range(2)]
    rcp = sm([8], F32)
    ssb = sm([8], F32)
    sqjb = sm([128], BF16)
    sqb4 = [carve(SM + i * KB, [512], BF16) for i in range(2)]
    rstd_bo = carve(SM + 3 * KB, [512], F32)
    assert so[0] <= ARENA, so[0]

    lp = lamp.rearrange("p (a b) -> p a b", a=4)
    S.op("dve", lambda e: e.tensor_tensor(out=lamp[:, 0:64], in0=lp[:, 0, :], in1=lp[:, 1, :], op=ALU.mult),
         reads=["lamp"], writes=["lamp"])
    S.op("dve", lambda e: e.tensor_tensor(out=lamp[:, 128:192], in0=lp[:, 2, :], in1=lp[:, 3, :], op=ALU.mult),
         reads=["lamp"], writes=["lamp"])
    S.op("dve", lambda e: e.tensor_reduce(out=lamv[:, 0:1], in_=lamp[:, 0:64], axis=AX.X, op=ALU.add), reads=["lamp"], writes=["lamv0"])
    S.op("dve", lambda e: e.tensor_reduce(out=lamv[:, 1:2], in_=lamp[:, 128:192], axis=AX.X, op=ALU.add), reads=["lamp"], writes=["lamv1"])
    S.op("act", lambda e: e.activation(out=lamv[:, 2:4], in_=lamv[:, 0:2], func=AF.Exp), reads=["lamv0", "lamv1"], writes=["lamv2"])
    S.op("dve", lambda e: e.tensor_tensor(out=lamv[:, 4:5], in0=lamv[:, 3:4], in1=lamv[:, 2:3], op=ALU.subtract),
         reads=["lamv2"], writes=["lamv4"])
    S.op("dve", lambda e: e.tensor_scalar(out=lamv[:, 5:6], in0=lamv[:, 4:5], scalar1=-LAMBDA_INIT, scalar2=None, op0=ALU.add),
         reads=["lamv4"], writes=["neglam"])
    S.op("dve", lambda e: e.tensor_scalar(out=lamv[:, 6:7], in0=small[:, SUBLN0:SUBLN0 + 1], scalar1=1.0 - LAMBDA_INIT,
                                          scalar2=None, op0=ALU.mult),
         reads=["small"], writes=["gsub"])
    neglam = lamv[:, 5:6]
    gsub = lamv[:, 6:7]

    S.dma("pool", lambda e: e.dma_start(out=wvb, in_=w_kv[:, 1024:2048].rearrange("(c p) n -> p c n", p=128)), "wvb", writes=["wvb"])
    for r in range(NRNG):
        xr = xT_r(r)
        stats_rstd(lambda c: xr[:, c, :], lambda c: [f"big{r}"], sqb, rstd_b[:, r * 512:(r + 1) * 512], f"rstdb{r}")
        apply_norm(lambda c: xr[:, c, :], lambda c: [f"big{r}"], 2, rstd_b[:, r * 512:(r + 1) * 512], f"rstdb{r}",
                   lambda c: xnb[:, c, r * 512:(r + 1) * 512], lambda c: [f"xnb{c}_{r}"])
    bk = {"i": 0}

    def nbank():
        bk["i"] += 1
        return bk["i"] % 6

    for c in range(8):
        ws = c % 3
        S.dma("pool", lambda e, ws=ws, c=c: e.dma_start(out=wk[ws], in_=w_kv[:, c * 128:(c + 1) * 128].rearrange("(c p) n -> p c n", p=128)),
              f"wk{ws}", writes=[f"wk{ws}"])
        for r in range(NRNG):
            b = nbank()
            for kc in range(8):
                S.op("pe", lambda e, kc=kc, b=b, ws=ws, r=r: e.matmul(PS(b), lhsT=wk[ws][:, kc, :], rhs=xnb[:, kc, r * 512:(r + 1) * 512],
                                                                    start=(kc == 0), stop=(kc == 7)),
                     reads=[f"wk{ws}", f"xnb{kc}_{r}"], writes=[f"ps{b}"])
            S.op("act", lambda e, b=b, c=c, r=r: e.copy(out=kTb[:, c, r * 512:(r + 1) * 512], in_=PS(b)),
                 reads=[f"ps{b}"], writes=[f"kTb{c}"])
    for tt in range(16):
        S.op("pool", lambda e, tt=tt: e.memset(vaug[:, tt, :, 128:129], 1.0), writes=[f"vone{tt}"])
        for hh in range(2):
            b = nbank()
            for kc in range(8):
                S.op("pe", lambda e, kc=kc, b=b, tt=tt, hh=hh: e.matmul(PS(b), lhsT=xnb[:, kc, tt * 128:(tt + 1) * 128],
                                                                      rhs=wvb[:, kc, hh * 512:(hh + 1) * 512],
                                                                      start=(kc == 0), stop=(kc == 7)),
                     reads=["wvb", f"xnb{kc}_{tt // 4}"], writes=[f"ps{b}"])
            S.op("dve", lambda e, b=b, tt=tt, hh=hh: e.tensor_copy(out=vaug[:, tt, hh * 4:(hh + 1) * 4, 0:128],
                                                                   in_=PS(b).rearrange("p (a b) -> p a b", a=4)),
                 reads=[f"ps{b}"], writes=[f"vaug{tt}_{hh}"])
        if tt % 4 == 3:
            r = tt // 4
            xr = xT_r(r)
            apply_norm(lambda c: xr[:, c, :], lambda c: [f"big{r}"], 3, rstd_b[:, r * 512:(r + 1) * 512], f"rstdb{r}",
                       lambda c: xnb[:, c, r * 512:(r + 1) * 512], lambda c: [f"xnb{c}_{r}"])
    S.barrier()
    if stop_after == "B1":
        return finish()
    for c in range(8):
        ws = c % 2
        S.dma("pool", lambda e, ws=ws, c=c: e.dma_start(out=wq[ws], in_=b_w_q[:, c * 128:(c + 1) * 128].rearrange("(c p) n -> p c n", p=128)),
              f"wq{ws}", writes=[f"wq{ws}"])
        for r in range(NRNG):
            b = nbank()
            for kc in range(8):
                S.op("pe", lambda e, kc=kc, b=b, ws=ws, r=r: e.matmul(PS(b), lhsT=wq[ws][:, kc, :], rhs=xnb[:, kc, r * 512:(r + 1) * 512],
                                                                    start=(kc == 0), stop=(kc == 7)),
                     reads=[f"wq{ws}", f"xnb{kc}_{r}"], writes=[f"ps{b}"])
            S.op("act", lambda e, b=b, c=c, r=r: e.mul(out=qTb[:, c, r * 512:(r + 1) * 512], in_=PS(b), mul=0.125),
                 reads=[f"ps{b}"], writes=[f"qTb{c}"])
    S.barrier()
    if stop_after == "B2":
        return finish()
    pT3 = [carve(SM + i * KB, [512], BF16) for i in range(3)]
    accsb = carve(SM + 3 * KB, [4, 129], F32)
    STB = [0, 1, 3]
    ptrb = PS(2).bitcast(BF16)

    QREG = {0: ("oT", 6), 1: ("oT", 4), 2: ("qTb", 0), 3: ("oT", 6), 4: ("qTb", 0), 5: ("qTb", 2), 6: ("qTb", 0), 7: ("qTb", 2)}

    def alias_keys(h):
        nm, k = QREG[h]
        return [f"{nm}{k}", f"{nm}{k + 1}"]

    def qblk(h):
        nm, k = QREG[h]
        src = oT[:, k:k + 2, :] if nm == "oT" else qTb[:, k:k + 2, :]
        return src.rearrange("p a b -> p (a b)")

    def build_qblk(h):
        flat = qblk(h)
        v4 = flat.rearrange("p (r m c) -> p r m c", r=16, m=2)
        ak = alias_keys(h)
        S.op("pool", lambda e: e.memset(flat, 0.0), writes=ak)
        S.op("dve", lambda e: e.tensor_copy(out=v4[0:64, :, 0, :], in_=qTb[0:64, h, :].rearrange("p (r c) -> p r c", r=16)),
             reads=[f"qTb{h}"], writes=ak)
        S.op("dve", lambda e: e.tensor_copy(out=v4[64:128, :, 1, :], in_=qTb[64:128, h, :].rearrange("p (r c) -> p r c", r=16)),
             reads=[f"qTb{h}"], writes=ak)

    def st_mm(h, qb, kb, slot, col0=0):
        if h == 0 and qb == 0 and kb == 0:
            build_qblk(0)
        if qb == 8 and kb == 0 and h + 1 < 8:
            build_qblk(h + 1)
        qb_ = qblk(h)
        rk = alias_keys(h) + [f"kTb{h}"]
        diag = kb == qb
        S.op("pe", lambda e: e.matmul(PS(slot, 256, col0), lhsT=kTb[:, h, kb * 128:(kb + 1) * 128], rhs=qb_[:, qb * 256:(qb + 1) * 256],
                                      start=True, stop=not diag),
             reads=rk, writes=[f"ps{slot}"])
        if diag:
            for m in range(2):
                S.op("pe", lambda e, m=m: e.matmul(PS(slot, 128, col0 + m * 128), lhsT=ident_bf, rhs=maskneg_bf, start=False, stop=(m == 1)),
                     reads=["cmat"], writes=[f"ps{slot}"])

    ACC2 = SM + 3 * KB
    ONB3 = ACC2 + 4128
    onb3 = [carve(ONB3 + p * 1024, [4, 128], BF16) for p in range(2)]
    FSM = ONB3 + 2048
    rcp8 = carve(FSM, [8], F32)
    ss4 = carve(FSM + 32, [8], F32)
    assert FSM + 64 <= ARENA
    BREG = {0: ("oT", 2), 1: ("oT", 2), 2: ("oT", 4), 3: ("oT", 4), 4: ("oT", 6), 5: ("oT", 6), 6: ("qTb", 4), 7: ("qTb", 4)}

    def accbuf(h, g):
        if g % 2 == 0:
            return carve(ACC2, [8, 129], F32), ["acc8A"]
        nm, k = BREG[h]
        base = (R1 if nm == "oT" else R2) + k * 4096
        return carve(base, [8, 129], F32), ["acc8B", f"{nm}{k}", f"{nm}{k + 1}"]

    def fin_copy(h, qb):
        par = qb % 2
        i4 = qb % 4
        ap8, keys = accbuf(h, qb // 4)
        S.op("act", lambda e: e.copy(out=ap8[:, i4 * 2:(i4 + 1) * 2, :],
                                     in_=psum[:, (4 + 2 * par) * 512:(6 + 2 * par) * 512].rearrange("p (a b) -> p a b", a=2)[:, :, 0:129]),
             reads=[f"ps{4 + 2 * par}", f"ps{5 + 2 * par}"], writes=keys)

    def finalize1(h, P, slot):
        acc8, keys = accbuf(h, P)
        a4 = acc8.rearrange("p (q m) c -> p q m c", m=2)
        S.op("dve", lambda e: e.reciprocal(out=rcp8.rearrange("p (a b) -> p a b", b=1), in_=acc8[:, :, 128:129]),
             reads=keys, writes=["rcp8"])
        S.op("dve", lambda e: e.tensor_scalar(out=rcp8.rearrange("p (q m) -> p q m", m=2)[:, :, 1],
                                              in0=rcp8.rearrange("p (q m) -> p q m", m=2)[:, :, 1],
                                              scalar1=neglam, scalar2=None, op0=ALU.mult),
             reads=["rcp8", "neglam"], writes=["rcp8"])
        S.op("dve", lambda e: e.tensor_tensor(out=acc8[:, :, 0:128], in0=acc8[:, :, 0:128],
                                              in1=rcp8.rearrange("p (a b) -> p a b", b=1).to_broadcast([128, 8, 128]), op=ALU.mult),
             reads=keys + ["rcp8"], writes=keys)
        S.op("dve", lambda e: e.tensor_tensor(out=a4[:, :, 0, 0:128], in0=a4[:, :, 0, 0:128], in1=a4[:, :, 1, 0:128], op=ALU.add),
             reads=keys, writes=keys)
        S.op("dve", lambda e: e.tensor_tensor(out=a4[:, :, 1, 0:128], in0=a4[:, :, 0, 0:128], in1=a4[:, :, 0, 0:128], op=ALU.mult),
             reads=keys, writes=keys)
        S.op("dve", lambda e: e.tensor_reduce(out=ss4[:, 0:4], in_=a4[:, :, 1, 0:128], axis=AX.X, op=ALU.add),
             reads=keys, writes=["ss4a"])
        S.op("dve", lambda e: e.tensor_scalar(out=ss4[:, 4:8], in0=ss4[:, 0:4], scalar1=1.0 / 128, scalar2=EPS,
                                              op0=ALU.mult, op1=ALU.add),
             reads=["ss4a"], writes=["ss4b"])
        S.op("pool", lambda e: e.tensor_tensor(out=ss4[:, 0:4], in0=ss4[:, 4:8], in1=neghalf[:, 0:4], op=ALU.pow),
             reads=["ss4b", "neghalf"], writes=["ss4c", "ss4a"])
        S.op("dve", lambda e: e.tensor_tensor(out=onb3[slot], in0=a4[:, :, 0, 0:128],
                                              in1=ss4[:, 0:4].rearrange("p (a b) -> p a b", b=1).to_broadcast([128, 4, 128]), op=ALU.mult),
             reads=keys + ["ss4c"], writes=[f"onb{slot}"])

    def finalize2(h, P, slot):
        q0 = P * 512
        for i4 in range(4):
            S.op("pe", lambda e, i4=i4: e.transpose(ptrb[:, i4 * 128:(i4 + 1) * 128], onb3[slot][:, i4, :], ident_bf),
                 reads=[f"onb{slot}", "cmat"], writes=["ps2"])
        S.op("dve", lambda e: e.tensor_scalar(out=oT[:, h, q0:q0 + 512], in0=ptrb[:, 0:512], scalar1=gsub, scalar2=None, op0=ALU.mult),
             reads=["ps2", "gsub"], writes=[f"oT{h}"])

    pend = {"p": None, "n": 0}

    steps = [(h, qb, kb) for h in range(8) for qb in range(16) for kb in range(qb + 1)]
    pairs = [(steps[2 * i], steps[2 * i + 1]) for i in range(len(steps) // 2)]

    def st_pair(p, slot):
        st_mm(*pairs[p][0], slot, 0)
        st_mm(*pairs[p][1], slot, 256)

    st_pair(0, STB[0])
    st_pair(1, STB[1])
    for p, pr in enumerate(pairs):
        slot = STB[p % 3]
        sl = p % 3
        if p + 2 < len(pairs):
            st_pair(p + 2, STB[(p + 2) % 3])
        S.op("act", lambda e: e.activation(out=pT3[sl], in_=PS(slot), func=AF.Exp),
             reads=[f"ps{slot}"], writes=[f"pT{sl}"])
        for idx, (h, qb, kb) in enumerate(pr):
            c0 = idx * 256
            par = qb % 2
            for m in range(2):
                S.op("pe", lambda e, m=m: e.matmul(PS(4 + 2 * par + m, 129, 0), lhsT=pT3[sl][:, c0 + m * 128:c0 + (m + 1) * 128],
                                                  rhs=vaug[:, kb, h, :], start=(kb == 0), stop=(kb == qb)),
                     reads=[f"pT{sl}", f"vaug{kb}_{h // 4}", f"vone{kb}"], writes=[f"ps{4 + 2 * par + m}"])
            if kb == qb:
                fin_copy(h, qb)
                if qb % 4 == 3:
                    if pend["p"] is not None:
                        finalize2(*pend["p"])
                    slot2 = pend["n"] % 2
                    pend["n"] += 1
                    finalize1(h, qb // 4, slot2)
                    pend["p"] = (h, qb // 4, slot2)
    finalize2(*pend["p"])
    S.barrier()
    if stop_after == "B3":
        return finish()
    S.dma("pool", lambda e: e.dma_start(out=wbo, in_=b_w_out.rearrange("(c p) n -> p c n", p=128)), "wbo", writes=["wbo"])
    for r in range(NRNG):
        xr = xT_r(r)
        proj_post(r, 8, lambda k, dc: wbo[:, k, dc * 128:(dc + 1) * 128], lambda k: oT[:, k, r * 512:(r + 1) * 512],
                  lambda k: ["wbo", f"oT{k}"], hbb, "hbb", 4, sqb4, rstd_bo, "rstdbo",
                  lambda c: xr[:, c, :], lambda c: [f"big{r}"], lambda c: [f"big{r}"])
        if r == 2 and stop_after != "B":
            ffn_prenorm0(1)
    if stop_after == "B":
        S.barrier()
        return finish()
    S.soft_barrier()
    ffn(1, pre_done=True, wu_alias={0: ["oT1", "oT2"], 1: ["oT2", "oT3"], 2: ["oT3", "oT4"]})
    return finish()


def _const_tables():
    f32 = np.float32
    cm = np.zeros((128, 4, 128), f32)
    cm[:, 0, :] = 1.0 / 1024.0
    cm[:, 1, :] = np.eye(128, dtype=f32)
    kk = np.arange(128)
    cm[:, 2, :] = (kk[:, None] <= kk[None, :]).astype(f32)
    cm[:, 3, :] = -30000.0 * (kk[:, None] > kk[None, :]).astype(f32)
    half = 128
    inv = 1.0 / (10000.0 ** np.linspace(0.0, 1.0, half, dtype=np.float64))
    ang = np.arange(T, dtype=np.float64)[None, :] * inv[:, None]
    cs = np.stack([np.cos(ang), np.sin(ang)], axis=1).astype(f32)
    rt = np.zeros((128, 4, 640), f32)
    kdec = np.zeros((128, 4), f32)
    idx = np.arange(128, dtype=np.float64)
    for h in range(4):
        lg = np.log1p(-(2.0 ** (-5.0 - h)))
        rel = idx[None, :] - idx[:, None]
        m = np.where(rel >= 0, np.exp(lg * np.maximum(rel, 0.0)), 0.0) * (256.0 ** -0.5)
        rt[:, h, 0:128] = m
        qd = np.exp(lg * (idx + 1.0))
        rt[:, h, 128:640] = np.tile(qd, 4)[None, :]
        kdec[:, h] = np.exp(lg * (127.0 - idx)) * (256.0 ** -0.5)
    return cm.reshape(128, 512), cs.reshape(128, 2 * T), rt.reshape(128, 4 * 640), kdec


def _lay_up(w):
    return np.ascontiguousarray(w.reshape(8, 128, 2, 22, 128).transpose(3, 1, 0, 2, 4).reshape(22, 128, 2048))


def _lay_down(w):
    return np.ascontiguousarray(w.reshape(22, 128, 8, 128).transpose(2, 1, 0, 3).reshape(8, 128, 2816))


_CACHE = {}


def _get_nc(stop_after="F1"):
    if stop_after not in _CACHE:
        _CACHE[stop_after] = build(stop_after)[0]
    return _CACHE[stop_after]


def make_in_maps(inputs):
    f32 = np.float32
    g = lambda k: np.asarray(inputs[k], dtype=f32)
    cm, cs, rt, kdec = _const_tables()
    small = np.zeros((128, 512), f32)
    gl = [g("a_norm_pre")[0], g("a_norm_post")[0], g("kv_norm"), g("b_norm_pre")[0], g("b_norm_post")[0],
          g("ffn_norm_pre")[0], g("ffn_norm_pre")[1], g("ffn_norm_post")[0], g("ffn_norm_post")[1]]
    for i, v in enumerate(gl):
        small[:, i * 8:(i + 1) * 8] = v.reshape(8, 128).T
    cw, cb = g("ffn_conv_w"), g("ffn_conv_b")
    for l in range(2):
        for j in range(3):
            small[:, 72 + (l * 4 + j) * 44:72 + (l * 4 + j + 1) * 44] = cw[l, j].reshape(44, 128).T
        small[:, 72 + (l * 4 + 3) * 44:72 + (l * 4 + 4) * 44] = cb[l].reshape(44, 128).T
    small[:, 424:428] = kdec
    small[:, 428] = g("b_subln")[0]
    small[:, 464] = EPS
    lamp = np.ascontiguousarray(np.broadcast_to(g("b_lambda")[0].reshape(1, 256), (128, 256)))
    shared = {
        "a_w_in": np.ascontiguousarray(g("a_w_in")[0]), "a_w_out": np.ascontiguousarray(g("a_w_out")[0]),
        "w_kv": np.ascontiguousarray(g("w_kv")), "b_w_q": np.ascontiguousarray(g("b_w_q")[0]),
        "b_w_out": np.ascontiguousarray(g("b_w_out")[0]),
        "w_up0": _lay_up(g("ffn_w_up")[0]), "w_up1": _lay_up(g("ffn_w_up")[1]),
        "w_down0": _lay_down(g("ffn_w_down")[0]), "w_down1": _lay_down(g("ffn_w_down")[1]),
        "cmat": cm, "cossin": cs, "rtab": rt, "small": small, "lamp": lamp,
    }
    x = g("x")
    maps = []
    for b in range(x.shape[0]):
        d = dict(shared)
        d["xT"] = np.ascontiguousarray(x[b].T)
        maps.append(d)
    return maps


def kernel(**inputs):
    nc = _get_nc("F1")
    in_maps = make_in_maps(inputs)
    res = run_bass_kernel_spmd(nc, in_maps, core_ids=list(range(len(in_maps))))
    out = np.stack([np.ascontiguousarray(r["outT"].T) for r in res.results], axis=0)
    return out.astype(np.float32)
```

```python
import math
from contextlib import ExitStack
import numpy as np
import concourse.bass as bass
import concourse.mybir as mybir
from concourse.bass_utils import run_bass_kernel_spmd

F32 = mybir.dt.float32
BF16 = mybir.dt.bfloat16
U8 = mybir.dt.uint8
ALU = mybir.AluOpType
AF = mybir.ActivationFunctionType
AX = mybir.AxisListType

ENGS = ("pe", "act", "dve", "pool", "sp")
KB = 1024
EPS = 1e-6
T = 2048
NRNG = 4
H_RET = 4
LAMBDA_INIT = 0.8 - 0.6 * math.exp(-0.3 * 1)


class _Rec:
    def __init__(self):
        self.call = None

    def __getattr__(self, name):
        def f(*a, **k):
            self.call = (name, a, k)
            return self
        return f


def _capture(fn):
    r = _Rec()
    fn(r)
    assert r.call is not None
    return r.call


class Sched:
    def __init__(self, nc):
        self.nc = nc
        self.ops = {e: [] for e in ENGS}
        self.count = {e: 0 for e in ENGS}
        self.actors = list(ENGS)
        self.sems = {}
        self.clock = {e: {} for e in ENGS}
        self.ev_clock = {}
        self.res = {}
        self.dma_count = {}
        self.n_waits = 0
        self.n_ops = 0

    def _need(self, eng, ev, waits):
        a, c = ev
        if self.clock[eng].get(a, 0) >= c:
            return
        waits[a] = max(waits.get(a, 0), c)

    def _deps(self, eng, reads, writes, strict=False):
        waits = {}
        for k in reads:
            st = self.res.get(k)
            if st is None:
                continue
            if st[0] is not None:
                self._need(eng, st[0], waits)
        for k in writes:
            st = self.res.get(k)
            if st is None:
                continue
            w = st[0]
            if w is not None and (strict or w[0] != eng or eng != "pe"):
                self._need(eng, w, waits)
            for r in st[1]:
                if strict or r[0] != eng or eng != "pe":
                    self._need(eng, r, waits)
        return waits

    def _apply_waits(self, eng, waits):
        for a, c in waits.items():
            if self.clock[eng].get(a, 0) >= c:
                continue
            self.ops[eng].append(("wait", a, c))
            self.n_waits += 1
            ck = self.ev_clock.get((a, c))
            if ck:
                mine = self.clock[eng]
                for k2, v2 in ck.items():
                    if mine.get(k2, 0) < v2:
                        mine[k2] = v2
            if self.clock[eng].get(a, 0) < c:
                self.clock[eng][a] = c

    def _record(self, ev, reads, writes):
        for k in reads:
            st = self.res.setdefault(k, [None, []])
            st[1].append(ev)
        for k in writes:
            self.res[k] = [ev, []]

    def op(self, eng, fn, reads=(), writes=()):
        self._apply_waits(eng, self._deps(eng, reads, writes))
        self.count[eng] += 1
        ev = (eng, self.count[eng])
        self.ops[eng].append(("op", _capture(fn), eng, 1))
        snap = dict(self.clock[eng])
        snap[eng] = ev[1]
        self.ev_clock[ev] = snap
        self._record(ev, reads, writes)
        self.n_ops += 1
        return ev

    def dma(self, queue, fn, actor, reads=(), writes=()):
        if actor not in self.dma_count:
            self.dma_count[actor] = 0
            self.actors.append(actor)
        self._apply_waits(queue, self._deps(queue, reads, writes, strict=True))
        self.dma_count[actor] += 16
        ev = (actor, self.dma_count[actor])
        self.ops[queue].append(("op", _capture(fn), actor, 16))
        snap = dict(self.clock[queue])
        snap[actor] = ev[1]
        self.ev_clock[ev] = snap
        self._record(ev, reads, writes)
        return ev

    def _targets(self):
        t = {e: self.count[e] for e in ENGS if self.count[e] > 0}
        for a, c in self.dma_count.items():
            if c > 0:
                t[a] = c
        return t

    def barrier(self):
        t = self._targets()
        for e in ENGS:
            self._apply_waits(e, dict(t))
        self.res = {}

    def soft_barrier(self, engs=("act", "dve", "pool")):
        t = {e: self.count[e] for e in engs if self.count[e] > 0}
        for e in engs:
            w = dict(t)
            w.pop(e, None)
            self._apply_waits(e, w)

    def final_wait(self, eng="sp"):
        t = self._targets()
        t.pop(eng, None)
        self._apply_waits(eng, t)

    def emit(self):
        nc = self.nc
        with ExitStack() as st:
            for a in self.actors:
                self.sems[a] = st.enter_context(nc.semaphore("s_" + a))
            block = st.enter_context(nc.Block())
            engmap = {"pe": block.tensor, "act": block.scalar, "dve": block.vector,
                      "pool": block.gpsimd, "sp": block.sync}

            def make(ename):
                lst = self.ops[ename]

                def body(eng):
                    for item in lst:
                        if item[0] == "wait":
                            eng.wait_ge(self.sems[item[1]], item[2])
                        else:
                            _, call, actor, inc = item
                            getattr(eng, call[0])(*call[1], **call[2]).then_inc(self.sems[actor], inc)
                return body

            for ename in ENGS:
                if self.ops[ename]:
                    engmap[ename](make(ename))


PHASES = ["A", "F0", "B", "F1"]


def build(stop_after="F1"):
    nc = bass.Bass("TRN2", target_bir_lowering=False)
    S = Sched(nc)

    def din(name, shape):
        return nc.dram_tensor(name, shape, F32, kind="ExternalInput").ap()

    xT_d = din("xT", [1024, T])
    a_w_in = din("a_w_in", [1024, 6144])
    a_w_out = din("a_w_out", [2048, 1024])
    w_kv = din("w_kv", [1024, 2048])
    b_w_q = din("b_w_q", [1024, 1024])
    b_w_out = din("b_w_out", [1024, 1024])
    w_up_r = [din(f"w_up{l}", [22, 128, 2048]) for l in range(2)]
    w_down_r = [din(f"w_down{l}", [8, 128, 2816]) for l in range(2)]
    cmat_d = din("cmat", [128, 4 * 128])
    cossin_d = din("cossin", [128, 2 * T])
    rtab_d = din("rtab", [128, 4 * 640])
    small_d = din("small", [128, 512])
    lamp_d = din("lamp", [128, 256])
    outT = nc.dram_tensor("outT", [1024, T], F32, kind="ExternalOutput").ap()
    xT_v = xT_d.rearrange("(c p) t -> p c t", p=128)
    outT_v = outT.rearrange("(c p) t -> p c t", p=128)

    ARENA = 212736
    arena = nc.alloc_sbuf_tensor("arena", [128, ARENA], U8)
    psum = nc.alloc_psum_tensor("psum", [128, 4096], F32)

    def carve(off, dims, dt):
        nb = int(np.prod(dims)) * (4 if dt == F32 else 2)
        assert off % 32 == 0 and off + nb <= ARENA, (off, nb)
        v = arena[:, off:off + nb].bitcast(dt)
        if len(dims) == 2:
            v = v.rearrange("p (a b) -> p a b", a=dims[0])
        elif len(dims) == 3:
            v = v.rearrange("p (a b c) -> p a b c", a=dims[0], b=dims[1])
        return v

    def PS(b, n=512, off=0):
        return psum[:, b * 512 + off:b * 512 + off + n]

    cmat = carve(0, [4, 128], BF16)
    ones_bf = cmat[:, 0, :]
    ident_bf = cmat[:, 1, :]
    cmask_bf = cmat[:, 2, :]
    maskneg_bf = cmat[:, 3, :]
    small = carve(1 * KB, [512], F32)
    GAIN0, CONV0, KDEC0, SUBLN0 = 0, 72, 424, 428
    lamv = small[:, 448:464]
    neghalf = carve(3 * KB, [512], BF16)
    lamp = carve(4 * KB, [256], F32)
    BIG = 5 * KB
    O = BIG + 64 * KB

    def gain(gi, c):
        return small[:, GAIN0 + gi * 8 + c:GAIN0 + gi * 8 + c + 1]

    def xT_r(r):
        return carve(BIG + r * 16 * KB, [8, 512], F32)

    S.dma("pool", lambda e: e.dma_start(out=cmat, in_=cmat_d.rearrange("p (a b) -> p a b", a=4)), "cst", writes=["cmat"])
    S.dma("sp", lambda e: e.dma_start(out=small, in_=small_d), "cst2", writes=["small"])
    S.dma("sp", lambda e: e.dma_start(out=lamp, in_=lamp_d), "cst3", writes=["lamp"])
    S.op("pool", lambda e: e.memset(neghalf, -0.5), writes=["neghalf"])
    epsb = small[:, 464:465]

    rot = {"sq": 0, "dvepool": 0}

    def alt_engine():
        rot["dvepool"] ^= 1
        return "dve" if rot["dvepool"] else "pool"

    def stats_rstd(src_fn, src_keys, sq_tiles, rstd_ap, rstd_key, bank=7):
        for c in range(8):
            j = rot["sq"] % len(sq_tiles)
            rot["sq"] += 1
            sq = sq_tiles[j]
            src = src_fn(c)
            S.op("act", lambda e, sq=sq, src=src: e.activation(out=sq, in_=src, func=AF.Square),
                 reads=src_keys(c) + ["small"], writes=[f"sqt{j}"])
            S.op("pe", lambda e, sq=sq, c=c: e.matmul(PS(bank), lhsT=ones_bf, rhs=sq, start=(c == 0), stop=(c == 7)),
                 reads=[f"sqt{j}", "cmat"], writes=[f"ps{bank}"])
        S.op("act", lambda e: e.activation(out=rstd_ap, in_=PS(bank), func=AF.Sqrt, bias=epsb, scale=1.0),
             reads=[f"ps{bank}", "small"], writes=[rstd_key])
        S.op("dve", lambda e: e.reciprocal(out=rstd_ap, in_=rstd_ap),
             reads=[rstd_key], writes=[rstd_key])

    def apply_norm(src_fn, src_keys, gi, rstd_ap, rstd_key, dst_fn, dst_keys):
        for c in range(8):
            eng = "dve"
            S.op(eng, lambda e, c=c: e.scalar_tensor_tensor(out=dst_fn(c), in0=src_fn(c), scalar=gain(gi, c),
                                                            in1=rstd_ap, op0=ALU.mult, op1=ALU.mult),
                 reads=src_keys(c) + [rstd_key, "small"], writes=dst_keys(c))

    def post_residual(r, hbuf, hkey, gi, sq_tiles, rstd_ap, rstd_key, xsrc_fn, xsrc_keys, xdst_keys):
        stats_rstd(lambda c: hbuf[:, c, :], lambda c: [f"{hkey}{c}"], sq_tiles, rstd_ap, rstd_key)
        xr = xT_r(r)
        for c in range(8):
            S.op("dve", lambda e, c=c: e.scalar_tensor_tensor(out=hbuf[:, c, :], in0=hbuf[:, c, :], scalar=gain(gi, c),
                                                              in1=rstd_ap, op0=ALU.mult, op1=ALU.mult),
                 reads=[f"{hkey}{c}", rstd_key, "small"], writes=[f"{hkey}{c}"])
            S.op("pool", lambda e, c=c: e.tensor_tensor(out=xr[:, c, :], in0=hbuf[:, c, :], in1=xsrc_fn(c), op=ALU.add),
                 reads=[f"{hkey}{c}"] + xsrc_keys(c), writes=xdst_keys(c))

    def proj_to_hbuf(r, nk, lhsT_fn, rhs_fn, in_keys, hbuf, hkey, banks=(0, 1, 2, 3, 4, 5)):
        for dc in range(8):
            b = banks[(r * 8 + dc) % len(banks)]
            for k in range(nk):
                S.op("pe", lambda e, k=k, dc=dc, b=b: e.matmul(PS(b), lhsT=lhsT_fn(k, dc), rhs=rhs_fn(k),
                                                               start=(k == 0), stop=(k == nk - 1)),
                     reads=in_keys(k), writes=[f"ps{b}"])
            S.op("act", lambda e, dc=dc, b=b: e.copy(out=hbuf[:, dc, :], in_=PS(b)),
                 reads=[f"ps{b}"], writes=[f"{hkey}{dc}"])

    def proj_post(r, nk, lhsT_fn, rhs_fn, in_keys, hbuf, hkey, gi, sq_tiles, rstd_ap, rstd_key,
                  xsrc_fn, xsrc_keys, xdst_keys, banks=(0, 1, 2, 3, 4), sbank=7):
        pend = None
        for dc in range(8):
            b = banks[(r * 8 + dc) % len(banks)]
            for k in range(nk):
                S.op("pe", lambda e, k=k: e.matmul(PS(b), lhsT=lhsT_fn(k, dc), rhs=rhs_fn(k), start=(k == 0), stop=(k == nk - 1)),
                     reads=in_keys(k), writes=[f"ps{b}"])
            if pend is not None:
                pend()
            S.op("act", lambda e: e.mul(out=hbuf[:, dc, :], in_=PS(b), mul=gain(gi, dc)),
                 reads=[f"ps{b}", "small"], writes=[f"{hkey}{dc}"])
            j = rot["sq"] % len(sq_tiles)
            rot["sq"] += 1
            S.op("act", lambda e: e.activation(out=sq_tiles[j], in_=PS(b), func=AF.Square),
                 reads=[f"ps{b}"], writes=[f"sqt{j}"])

            def pend(j=j, dc=dc):
                S.op("pe", lambda e: e.matmul(PS(sbank), lhsT=ones_bf, rhs=sq_tiles[j], start=(dc == 0), stop=(dc == 7)),
                     reads=[f"sqt{j}", "cmat"], writes=[f"ps{sbank}"])
        pend()
        S.op("act", lambda e: e.activation(out=rstd_ap, in_=PS(sbank), func=AF.Sqrt, bias=epsb, scale=1.0),
             reads=[f"ps{sbank}", "small"], writes=[rstd_key])
        S.op("dve", lambda e: e.reciprocal(out=rstd_ap, in_=rstd_ap), reads=[rstd_key], writes=[rstd_key])
        xr = xT_r(r)
        for c in range(8):
            S.op("dve", lambda e: e.tensor_tensor(out=hbuf[:, c, :], in0=hbuf[:, c, :], in1=rstd_ap, op=ALU.mult),
                 reads=[f"{hkey}{c}", rstd_key], writes=[f"{hkey}{c}"])
            eng = "dve"
            S.op(eng, lambda e: e.tensor_tensor(out=xr[:, c, :], in0=hbuf[:, c, :], in1=xsrc_fn(c), op=ALU.add),
                 reads=[f"{hkey}{c}"] + xsrc_keys(c), writes=xdst_keys(c))

    def ffn_prenorm0(l):
        xnh = carve(O, [8, 1024], BF16)
        sqf = [carve(O + 16 * KB + i * KB, [512], BF16) for i in range(3)]
        rstd_f = [carve(O + 16 * KB + 3 * KB + i * 2 * KB, [512], F32) for i in range(2)]
        for rr in range(2):
            xr = xT_r(rr)
            stats_rstd(lambda c: xr[:, c, :], lambda c: [f"big{rr}"], sqf, rstd_f[rr], f"rstdf{rr}", bank=5)
            apply_norm(lambda c: xr[:, c, :], lambda c: [f"big{rr}"], 5 + l, rstd_f[rr], f"rstdf{rr}",
                       lambda c: xnh[:, c, rr * 512:(rr + 1) * 512], lambda c: [f"xnh{c}_{rr}"])

    XN = O
    xn = carve(XN, [8, T], BF16)
    WQK = O + 32 * KB
    WVG = O + 40 * KB
    QKB = O + 72 * KB
    ROT = O + 80 * KB
    CS = O + 88 * KB
    RTAB = O + 104 * KB
    MISC = O + 114 * KB
    wqk = carve(WQK, [8, 512], BF16)
    wvg = [carve(WVG + i * 16 * KB, [8, 1024], BF16) for i in range(2)]
    qT = carve(QKB, [2, 512], BF16)
    qdT = carve(QKB + 2 * KB, [2, 512], BF16)
    kT = carve(QKB + 4 * KB, [2, 512], BF16)
    ktok = carve(QKB + 6 * KB, [4, 256], BF16)
    rt = [carve(ROT + i * 2 * KB, [512], F32) for i in range(4)]
    cossin = carve(CS, [2, T], F32)
    rtab = carve(RTAB, [4, 640], F32)
    mo = [MISC]

    def misc(dims, dt):
        nb = int(np.prod(dims)) * (4 if dt == F32 else 2)
        nb = (nb + 31) // 32 * 32
        v = carve(mo[0], dims, dt)
        mo[0] += nb
        return v

    vtok = [misc([512], BF16) for _ in range(3)]
    sgt = [misc([512], BF16) for _ in range(3)]
    sTb = [misc([128], BF16) for _ in range(2)]
    Sf = misc([2, 512], F32)
    Sb = [misc([2, 512], BF16) for _ in range(2)]
    ybf = [misc([512], BF16) for _ in range(2)]
    sqj = misc([512], BF16)
    sqa = [misc([512], BF16) for _ in range(2)]
    ssr = misc([8], F32)
    rstd_a = [misc([512], F32) for _ in range(2)]
    assert mo[0] <= ARENA, mo[0]

    wout_a = carve(WVG + ((H_RET - 2) % 2) * 16 * KB, [8, 1024], BF16)
    wout_b = carve(CS, [8, 1024], BF16)

    def load_wqk(h):
        S.dma("pool", lambda e: e.dma_start(out=wqk[:, :, 0:256],
                                            in_=a_w_in[:, h * 256:(h + 1) * 256].rearrange("(c p) n -> p c n", p=128)),
              "wqk_q", writes=["wqk_q"])
        S.dma("pool", lambda e: e.dma_start(out=wqk[:, :, 256:512],
                                            in_=a_w_in[:, 1024 + h * 256:1024 + (h + 1) * 256].rearrange("(c p) n -> p c n", p=128)),
              "wqk_k", writes=["wqk_k"])

    def load_wvg(h):
        sl = h % 2
        S.dma("pool", lambda e: e.dma_start(out=wvg[sl][:, :, 0:512],
                                            in_=a_w_in[:, 2048 + h * 512:2048 + (h + 1) * 512].rearrange("(c p) n -> p c n", p=128)),
              f"wvg_v{sl}", writes=[f"wvg_v{sl}"])
        S.dma("pool", lambda e: e.dma_start(out=wvg[sl][:, :, 512:1024],
                                            in_=a_w_in[:, 4096 + h * 512:4096 + (h + 1) * 512].rearrange("(c p) n -> p c n", p=128)),
              f"wvg_g{sl}", writes=[f"wvg_g{sl}"])

    load_wqk(0)
    load_wvg(0)

    def a0_dma(r):
        sl = r % 2
        xs = carve(BIG + (2 + sl) * 16 * KB, [8, 512], F32)
        S.dma("sp", lambda e: e.dma_start(out=xs, in_=xT_v[:, :, r * 512:(r + 1) * 512]), f"xs{sl}",
              writes=[f"xs{sl}_{c}" for c in range(8)] + [f"big{2 + sl}"])

    def a0_range(r):
        sl = r % 2
        xs = carve(BIG + (2 + sl) * 16 * KB, [8, 512], F32)
        stats_rstd(lambda c: xs[:, c, :], lambda c: [f"xs{sl}_{c}", f"big{2 + sl}"], sqa, rstd_a[sl], f"rstda{sl}")
        apply_norm(lambda c: xs[:, c, :], lambda c: [f"xs{sl}_{c}", f"big{2 + sl}"], 0, rstd_a[sl], f"rstda{sl}",
                   lambda c: xn[:, c, r * 512:(r + 1) * 512], lambda c: [f"xn{c}_{r}"])
        if r + 2 < NRNG:
            a0_dma(r + 2)

    a0_dma(0)
    a0_dma(1)
    S.dma("sp", lambda e: e.dma_start(out=cossin, in_=cossin_d.rearrange("p (a b) -> p a b", a=2)), "cs", writes=["cossin"])
    S.dma("sp", lambda e: e.dma_start(out=rtab, in_=rtab_d.rearrange("p (a b) -> p a b", a=4)), "rtab", writes=["rtab"])

    GAMMA = [1.0 - 2.0 ** (-5.0 - h) for h in range(H_RET)]
    cnt = {"v": 0, "sT": 0, "Sb": 0, "y": 0}
    pend_tr = {"f": None}
    for h in range(H_RET):
        wv = wvg[h % 2]
        wvk = f"wvg_v{h % 2}"
        wgk = f"wvg_g{h % 2}"
        gC = GAMMA[h] ** 128
        for r in range(NRNG):
            tok = slice(r * 512, (r + 1) * 512)
            if h == 0 and r == 0:
                a0_range(0)
            for dc in range(4):
                wkey = "wqk_q" if dc < 2 else "wqk_k"
                for kc in range(8):
                    S.op("pe", lambda e, dc=dc, kc=kc: e.matmul(PS(dc), lhsT=wqk[:, kc, dc * 128:(dc + 1) * 128],
                                                                rhs=xn[:, kc, tok], start=(kc == 0), stop=(kc == 7)),
                         reads=[wkey, f"xn{kc}_{r}"], writes=[f"ps{dc}"])
            if r == NRNG - 1 and h + 1 < H_RET:
                load_wqk(h + 1)
            if h == 0 and r == 0:
                load_wvg(1)
            cos_r = cossin[:, 0, tok]
            sin_r = cossin[:, 1, tok]
            for qk in range(2):
                b1, b2 = 2 * qk, 2 * qk + 1
                S.op("dve", lambda e: e.tensor_tensor(out=rt[0], in0=PS(b1), in1=cos_r, op=ALU.mult),
                     reads=[f"ps{b1}", "cossin"], writes=["rt0"])
                S.op("dve", lambda e: e.tensor_tensor(out=rt[1], in0=PS(b2), in1=sin_r, op=ALU.mult),
                     reads=[f"ps{b2}", "cossin"], writes=["rt1"])
                S.op("dve", lambda e: e.tensor_tensor(out=rt[2], in0=PS(b1), in1=sin_r, op=ALU.mult),
                     reads=[f"ps{b1}", "cossin"], writes=["rt2"])
                S.op("dve", lambda e: e.tensor_tensor(out=rt[3], in0=PS(b2), in1=cos_r, op=ALU.mult),
                     reads=[f"ps{b2}", "cossin"], writes=["rt3"])
                if qk == 0:
                    S.op("dve", lambda e: e.tensor_tensor(out=rt[0], in0=rt[0], in1=rt[1], op=ALU.subtract),
                         reads=["rt0", "rt1"], writes=["rt0"])
                    S.op("dve", lambda e: e.tensor_tensor(out=rt[2], in0=rt[2], in1=rt[3], op=ALU.add),
                         reads=["rt2", "rt3"], writes=["rt2"])
                    S.op("act", lambda e: e.copy(out=qT[:, 0, :], in_=rt[0]), reads=["rt0"], writes=["qT0"])
                    S.op("act", lambda e: e.copy(out=qT[:, 1, :], in_=rt[2]), reads=["rt2"], writes=["qT1"])
                    S.op("dve", lambda e: e.tensor_tensor(out=qdT[:, 0, :], in0=rt[0], in1=rtab[:, h, 128:640], op=ALU.mult),
                         reads=["rt0", "rtab"], writes=["qdT0"])
                    S.op("dve", lambda e: e.tensor_tensor(out=qdT[:, 1, :], in0=rt[2], in1=rtab[:, h, 128:640], op=ALU.mult),
                         reads=["rt2", "rtab"], writes=["qdT1"])
                else:
                    S.op("pool", lambda e: e.tensor_tensor(out=kT[:, 0, :], in0=rt[0], in1=rt[1], op=ALU.subtract),
                         reads=["rt0", "rt1"], writes=["kT0"])
                    S.op("dve", lambda e: e.tensor_tensor(out=kT[:, 1, :], in0=rt[2], in1=rt[3], op=ALU.add),
                         reads=["rt2", "rt3"], writes=["kT1"])

            if h == H_RET - 1 and r == NRNG - 1:
                S.dma("pool", lambda e: e.dma_start(out=wout_b, in_=a_w_out[1024:2048, :].rearrange("(c p) n -> p c n", p=128)),
                      "wout1", writes=["wout1", "cossin"])

            def emit_vg(n):
                gn = r * 4 + n
                tk = slice(gn * 128, (gn + 1) * 128)
                vs = cnt["v"] % 3
                cnt["v"] += 1
                for kc in range(8):
                    S.op("pe", lambda e, kc=kc: e.matmul(PS(0), lhsT=xn[:, kc, tk], rhs=wv[:, kc, 0:512],
                                                         start=(kc == 0), stop=(kc == 7)),
                         reads=[wvk, f"xn{kc}_{r}"], writes=["ps0"])
                for kc in range(8):
                    S.op("pe", lambda e, kc=kc: e.matmul(PS(1), lhsT=xn[:, kc, tk], rhs=wv[:, kc, 512:1024],
                                                         start=(kc == 0), stop=(kc == 7)),
                         reads=[wgk, f"xn{kc}_{r}"], writes=["ps1"])
                if gn == 15 and h + 2 < H_RET:
                    load_wvg(h + 2)
                if gn == 15 and h == H_RET - 2:
                    S.dma("pool", lambda e: e.dma_start(out=wout_a, in_=a_w_out[0:1024, :].rearrange("(c p) n -> p c n", p=128)),
                          "wout0", writes=["wout0", f"wvg_v{h % 2}", f"wvg_g{h % 2}"])
                S.op("act", lambda e: e.copy(out=vtok[vs], in_=PS(0)), reads=["ps0"], writes=[f"vtok{vs}"])
                S.op("act", lambda e: e.activation(out=sgt[vs], in_=PS(1), func=AF.Silu),
                     reads=["ps1"], writes=[f"sgt{vs}"])
                return vs

            vs_next = emit_vg(0)
            ptr = PS(6).bitcast(BF16)
            for n in range(4):
                for dc in range(2):
                    S.op("pe", lambda e, n=n, dc=dc: e.transpose(ptr[:, n * 256 + dc * 128:n * 256 + (dc + 1) * 128],
                                                                 kT[:, dc, n * 128:(n + 1) * 128], ident_bf),
                         reads=[f"kT{dc}", "cmat"], writes=["ps6"])
            S.op("act", lambda e: e.mul(out=ktok.rearrange("p a b -> p (a b)"), in_=ptr, mul=small[:, KDEC0 + h:KDEC0 + h + 1]),
                 reads=["ps6", "small"], writes=["ktok"])
            for n in range(4):
                gn = r * 4 + n
                ck = slice(n * 128, (n + 1) * 128)
                vs = vs_next
                ss = cnt["sT"] % 2
                cnt["sT"] += 1
                for dc in range(2):
                    S.op("pe", lambda e, dc=dc: e.matmul(PS(5, 128), lhsT=kT[:, dc, ck], rhs=qT[:, dc, ck],
                                                         start=(dc == 0), stop=(dc == 1)),
                         reads=[f"kT{dc}", f"qT{dc}"], writes=["ps5"])
                S.op("dve", lambda e: e.tensor_tensor(out=sTb[ss], in0=PS(5, 128), in1=rtab[:, h, 0:128], op=ALU.mult),
                     reads=["ps5", "rtab"], writes=[f"sTb{ss}"])
                if pend_tr["f"] is not None:
                    pend_tr["f"]()
                    pend_tr["f"] = None
                sb_prev = (cnt["Sb"] - 1) % 2
                S.op("pe", lambda e: e.matmul(PS(4), lhsT=sTb[ss], rhs=vtok[vs], start=True, stop=(gn == 0)),
                     reads=[f"sTb{ss}", f"vtok{vs}"], writes=["ps4"])
                if gn > 0:
                    for dc in range(2):
                        S.op("pe", lambda e, dc=dc: e.matmul(PS(4), lhsT=qdT[:, dc, ck], rhs=Sb[sb_prev][:, dc, :],
                                                             start=False, stop=(dc == 1)),
                             reads=[f"qdT{dc}", f"Sb{sb_prev}_{dc}"], writes=["ps4"])
                ys = cnt["y"] % 2
                cnt["y"] += 1
                S.op("act", lambda e: e.activation(out=sqj, in_=PS(4), func=AF.Square, accum_out=ssr[:, 0:1]),
                     reads=["ps4"], writes=["sqj", "ssr0"])
                S.op("dve", lambda e: e.tensor_scalar(out=ssr[:, 1:2], in0=ssr[:, 0:1], scalar1=1.0 / 512, scalar2=EPS,
                                                      op0=ALU.mult, op1=ALU.add),
                     reads=["ssr0"], writes=["ssr1"])
                S.op("pool", lambda e: e.tensor_tensor(out=ssr[:, 2:3], in0=ssr[:, 1:2], in1=neghalf[:, 0:1], op=ALU.pow),
                     reads=["ssr1", "neghalf"], writes=["ssr2"])
                if gn < 15:
                    sb_new = cnt["Sb"] % 2
                    cnt["Sb"] += 1
                    for dc in range(2):
                        S.op("pe", lambda e, dc=dc: e.matmul(PS(2 + dc), lhsT=ktok[:, n, dc * 128:(dc + 1) * 128],
                                                             rhs=vtok[vs], start=True, stop=True),
                             reads=["ktok", f"vtok{vs}"], writes=[f"ps{2 + dc}"])
                        if gn == 0:
                            S.op("dve", lambda e, dc=dc: e.tensor_copy(out=Sf[:, dc, :], in_=PS(2 + dc)),
                                 reads=[f"ps{2 + dc}"], writes=[f"Sf{dc}"])
                        else:
                            S.op("dve", lambda e, dc=dc: e.scalar_tensor_tensor(out=Sf[:, dc, :], in0=Sf[:, dc, :], scalar=gC,
                                                                                in1=PS(2 + dc), op0=ALU.mult, op1=ALU.add),
                                 reads=[f"ps{2 + dc}", f"Sf{dc}"], writes=[f"Sf{dc}"])
                        S.op("act", lambda e, dc=dc: e.copy(out=Sb[sb_new][:, dc, :], in_=Sf[:, dc, :]),
                             reads=[f"Sf{dc}"], writes=[f"Sb{sb_new}_{dc}"])
                S.op("dve", lambda e: e.scalar_tensor_tensor(out=ybf[ys], in0=PS(4), scalar=ssr[:, 2:3], in1=sgt[vs],
                                                             op0=ALU.mult, op1=ALU.mult),
                     reads=["ps4", "ssr2", f"sgt{vs}"], writes=[f"ybf{ys}"])
                if n < 3:
                    vs_next = emit_vg(n + 1)
                if h == 0 and n == 1 and r + 1 < NRNG:
                    a0_range(r + 1)

                def tr_fn(ys=ys, r=r, h=h, ck=ck):
                    pyT = PS(7).bitcast(BF16)
                    for ec in range(4):
                        S.op("pe", lambda e, ec=ec: e.transpose(pyT[:, ec * 128:(ec + 1) * 128],
                                                                ybf[ys][:, ec * 128:(ec + 1) * 128], ident_bf),
                             reads=[f"ybf{ys}", "cmat"], writes=["ps7"])
                    yT_r = carve(BIG + r * 16 * KB, [16, 512], BF16)
                    S.op("act", lambda e: e.copy(out=yT_r[:, h * 4:(h + 1) * 4, ck],
                                                 in_=pyT[:, 0:512].rearrange("p (a b) -> p a b", a=4)),
                         reads=["ps7"], writes=[f"big{r}"])
                pend_tr["f"] = tr_fn
    pend_tr["f"]()
    S.barrier()

    hbufs = [carve(MISC, [8, 512], F32),
             carve(O + 24 * KB, [8, 512], F32)]
    if stop_after != "A":
        for fc in range(2):
            S.dma("pool", lambda e, fc=fc: e.dma_start(out=carve(O + 72 * KB + fc * 4 * KB, [2048], BF16), in_=w_up_r[0][fc]),
                  f"wu{fc}", writes=[f"wu{fc}"])
    xs2 = carve(WVG + (1 - (H_RET - 2) % 2) * 16 * KB, [8, 512], F32)
    def wout(k, dc):
        return (wout_a if k < 8 else wout_b)[:, k % 8, dc * 128:(dc + 1) * 128]

    for r in range(NRNG):
        yT_r = carve(BIG + r * 16 * KB, [16, 512], BF16)
        S.dma("sp", lambda e, r=r: e.dma_start(out=xs2, in_=xT_v[:, :, r * 512:(r + 1) * 512]), "xs2",
              writes=[f"xs2_{c}" for c in range(8)])
        proj_post(r, 16, wout, lambda k: yT_r[:, k, :],
                  lambda k: [f"wout{k // 8}", f"big{r}"], hbufs[r % 2], f"hb{r % 2}_", 1, sqa, rstd_a[r % 2], f"rstda{r % 2}",
                  lambda c: xs2[:, c, :], lambda c: [f"xs2_{c}"], lambda c: [f"big{r}"])
        if r == 2 and stop_after != "A":
            ffn_prenorm0(0)
    if stop_after == "A":
        S.barrier()
    else:
        S.soft_barrier()

    early_out = {"done": False}

    def finish():
        for r in range(2 if early_out["done"] else 0, NRNG):
            S.dma("sp", lambda e, r=r: e.dma_start(out=outT_v[:, :, r * 512:(r + 1) * 512], in_=xT_r(r)), "outd",
                  reads=[f"big{r}"])
        S.final_wait("sp")
        S.emit()
        return nc, S

    if stop_after == "A":
        return finish()

    def ffn(l, pre_done=False, wu_done=0, wu_alias=None):
        XNH = O
        TMP = O + 16 * KB
        MT = O + 28 * KB
        UP = O + 72 * KB
        NWU = 3
        NWD = 3
        xnh = carve(XNH, [8, 1024], BF16)
        sqf = [carve(TMP + i * KB, [512], BF16) for i in range(3)]
        rstd_f = [carve(TMP + 3 * KB + i * 2 * KB, [512], F32) for i in range(2)]
        halo = carve(TMP + 7 * KB, [44, 2], F32)
        rstd_p = [carve(TMP + 8 * KB + i * 2 * KB, [512], F32) for i in range(2)]
        mT = carve(MT, [22, 1024], BF16)
        wu = [carve(UP + i * 4 * KB, [8, 256], BF16) for i in range(NWU)]
        WDC = UP + NWU * 4 * KB
        wdc = [carve(WDC + i * 5632, [22, 128], BF16) for i in range(NWD)]
        HS = WDC + NWD * 5632
        hs = [[carve(HS + (s * 2 + j) * 4128, [1032], F32) for j in range(2)] for s in range(2)]
        AC = HS + 4 * 4128
        acc = [[carve(AC + (s * 2 + j) * 4 * KB, [1024], F32) for j in range(2)] for s in range(2)]
        SG = AC + 16 * KB
        sgf = [carve(SG + s * 2 * KB, [1024], BF16) for s in range(2)]
        assert SG + 4 * KB <= ARENA
        hb2 = [carve(HS + i * 16 * KB, [8, 512], F32) for i in range(2)]

        def hb_overlap(off, size):
            ks = []
            for rr_ in range(2):
                for dc_ in range(8):
                    a0 = rr_ * 16384 + dc_ * 2048
                    if off < a0 + 2048 and off + size > a0:
                        ks.append(f"hb{rr_}_{dc_}")
            return ks

        hs_alias = [[hb_overlap((s_ * 2 + j_) * 4128, 4128) for j_ in range(2)] for s_ in range(2)]
        acc_alias = [[hb_overlap(16512 + (s_ * 2 + j_) * 4096, 4096) for j_ in range(2)] for s_ in range(2)]
        gi_pre, gi_post = 5 + l, 7 + l

        def cw(j, fc):
            o = CONV0 + (l * 4 + j) * 44 + fc
            return small[:, o:o + 1]

        first_load = {"n": 0}

        def load_wu(fc):
            ws = fc % NWU
            extra = []
            if wu_alias is not None and first_load["n"] < NWU:
                extra = wu_alias[ws]
            first_load["n"] += 1
            S.dma("pool", lambda e: e.dma_start(out=wu[ws].rearrange("p a b -> p (a b)"), in_=w_up_r[l][fc]),
                  f"wu{ws}", writes=[f"wu{ws}"] + extra)

        def load_wdc(dc):
            sl = dc % NWD
            S.dma("pool", lambda e: e.dma_start(out=wdc[sl].rearrange("p a b -> p (a b)"), in_=w_down_r[l][dc]),
                  f"wdc{sl}", writes=[f"wdc{sl}"])

        def prenorm(hf):
            for rr in range(2):
                r = hf * 2 + rr
                xr = xT_r(r)
                stats_rstd(lambda c: xr[:, c, :], lambda c: [f"big{r}"], sqf, rstd_f[rr], f"rstdf{rr}", bank=5)
                apply_norm(lambda c: xr[:, c, :], lambda c: [f"big{r}"], gi_pre, rstd_f[rr], f"rstdf{rr}",
                           lambda c: xnh[:, c, rr * 512:(rr + 1) * 512], lambda c: [f"xnh{c}_{rr}"])

        for fc in range(wu_done, NWU - 1):
            load_wu(fc)
        if not pre_done:
            prenorm(0)
        for hf in range(2):
            if hf == 0:
                for st in range(2):
                    for j in range(2):
                        S.op("dve", lambda e: e.memset(hs[st][j][:, 0:2], 0.0), writes=[f"hs{st}{j}h"])
            for fc in range(22):
                ws = fc % NWU
                st = fc % 2
                if fc + NWU - 1 < 22:
                    load_wu(fc + NWU - 1)
                if fc == 16:
                    load_wdc(0)
                if fc == 18:
                    load_wdc(1)
                if fc == 20:
                    load_wdc(2)
                for j in range(2):
                    bb = st * 4 + j * 2
                    for rr in range(2):
                        for kc in range(8):
                            S.op("pe", lambda e, kc=kc, rr=rr: e.matmul(
                                PS(bb + rr), lhsT=wu[ws][:, kc, j * 128:(j + 1) * 128], rhs=xnh[:, kc, rr * 512:(rr + 1) * 512],
                                start=(kc == 0), stop=(kc == 7)),
                                reads=[f"wu{ws}", f"xnh{kc}_{rr}"], writes=[f"ps{bb + rr}"])
                    hsj = hs[st][j]
                    acj = acc[st][j]
                    hk = f"hs{st}{j}"
                    ak = f"acc{st}{j}"
                    fcj = fc + 22 * j
                    hsa = hs_alias[st][j] if hf == 1 else []
                    aca = acc_alias[st][j] if hf == 1 else []
                    if hf == 1:
                        S.op("act", lambda e: e.copy(out=hsj[:, 0:2], in_=halo[:, fcj, :]),
                             reads=[f"halo{fcj}"], writes=[hk + "h"] + hsa)
                    S.op("act", lambda e: e.copy(out=hsj[:, 2:1026], in_=psum[:, bb * 512:bb * 512 + 1024]),
                         reads=[f"ps{bb}", f"ps{bb + 1}"], writes=[hk] + hsa)
                    if hf == 0:
                        S.op("act", lambda e: e.copy(out=halo[:, fcj, :], in_=psum[:, bb * 512 + 1022:bb * 512 + 1024]),
                             reads=[f"ps{bb + 1}"], writes=[f"halo{fcj}"])
                    S.op("act", lambda e: e.activation(
                        out=acj, in_=hsj[:, 0:1024], func=AF.Identity, bias=cw(3, fcj), scale=cw(0, fcj)),
                        reads=[hk, hk + "h", "small"], writes=[ak] + aca)
                    S.op("dve", lambda e: e.scalar_tensor_tensor(
                        out=acj, in0=hsj[:, 1:1025], scalar=cw(1, fcj), in1=acj, op0=ALU.mult, op1=ALU.add),
                        reads=[hk, hk + "h", ak, "small"], writes=[ak])
                    S.op("dve", lambda e: e.scalar_tensor_tensor(
                        out=acj, in0=hsj[:, 2:1026], scalar=cw(2, fcj), in1=acj, op0=ALU.mult, op1=ALU.add),
                        reads=[hk, ak, "small"], writes=[ak])
                S.op("act", lambda e: e.activation(out=sgf[st], in_=acc[st][0], func=AF.Silu),
                     reads=[f"acc{st}0"], writes=[f"sgf{st}"])
                S.op("pool", lambda e: e.tensor_tensor(out=mT[:, fc, :], in0=sgf[st], in1=acc[st][1], op=ALU.mult),
                     reads=[f"sgf{st}", f"acc{st}1"], writes=[f"mT{fc}"])
            S.soft_barrier()
            if hf == 0:
                for fc in range(NWU - 1):
                    load_wu(fc)
            groups = [(dc, rr) for dc in range(8) for rr in range(2)]
            pend = None

            def mm_range(gidx, k0, k1):
                dc, rr = groups[gidx]
                b = gidx % 5
                sl = dc % NWD
                for k in range(k0, k1):
                    S.op("pe", lambda e, k=k: e.matmul(PS(b), lhsT=wdc[sl][:, k, :], rhs=mT[:, k, rr * 512:(rr + 1) * 512],
                                                      start=(k == 0), stop=(k == 21)),
                         reads=[f"wdc{sl}", f"mT{k}"], writes=[f"ps{b}"])

            def evac(gidx):
                nonlocal pend
                dc, rr = groups[gidx]
                b = gidx % 5
                if pend is not None:
                    pend()
                S.op("act", lambda e: e.mul(out=hb2[rr][:, dc, :], in_=PS(b), mul=gain(gi_post, dc)),
                     reads=[f"ps{b}", "small"], writes=[f"hb{rr}_{dc}"])
                j = rot["sq"] % len(sqf)
                rot["sq"] += 1
                S.op("act", lambda e: e.activation(out=sqf[j], in_=PS(b), func=AF.Square),
                     reads=[f"ps{b}"], writes=[f"sqt{j}"])

                def pend(j=j, rr=rr, dc=dc):
                    S.op("pe", lambda e: e.matmul(PS(6 + rr), lhsT=ones_bf, rhs=sqf[j], start=(dc == 0), stop=(dc == 7)),
                         reads=[f"sqt{j}", "cmat"], writes=[f"ps{6 + rr}"])

            for gidx in range(3):
                mm_range(gidx, 0, 20)
            for gidx in range(16):
                dc, rr = groups[gidx]
                if gidx < 3:
                    mm_range(gidx, 20, 22)
                else:
                    mm_range(gidx, 0, 22)
                evac(gidx)
                if rr == 1:
                    if dc + NWD < 8:
                        load_wdc(dc + NWD)
                    if hf == 0 and dc == 2:
                        pend()
                        pend = None
                        prenorm(1)
            pend()
            for rr in range(2):
                S.op("act", lambda e: e.activation(out=rstd_p[rr], in_=PS(6 + rr), func=AF.Sqrt, bias=epsb, scale=1.0),
                     reads=[f"ps{6 + rr}", "small"], writes=[f"rstdp{rr}"])
            for rr in range(2):
                S.op("dve", lambda e: e.reciprocal(out=rstd_p[rr], in_=rstd_p[rr]), reads=[f"rstdp{rr}"], writes=[f"rstdp{rr}"])
            for c in range(8):
                for rr in range(2):
                    r = hf * 2 + rr
                    xr = xT_r(r)
                    S.op("dve", lambda e: e.tensor_tensor(out=hb2[rr][:, c, :], in0=hb2[rr][:, c, :], in1=rstd_p[rr], op=ALU.mult),
                         reads=[f"hb{rr}_{c}", f"rstdp{rr}"], writes=[f"hb{rr}_{c}"])
                    eng = "dve"
                    S.op(eng, lambda e: e.tensor_tensor(out=xr[:, c, :], in0=hb2[rr][:, c, :], in1=xr[:, c, :], op=ALU.add),
                         reads=[f"hb{rr}_{c}", f"big{r}"], writes=[f"big{r}"])
            if hf == 0:
                if l == 1 and stop_after == "F1":
                    for r in range(2):
                        S.dma("sp", lambda e, r=r: e.dma_start(out=outT_v[:, :, r * 512:(r + 1) * 512], in_=xT_r(r)), "outd",
                              reads=[f"big{r}"])
                    early_out["done"] = True
        S.barrier()

    ffn(0, pre_done=True, wu_done=2)
    if stop_after == "F0":
        return finish()

    KT = O
    VT = O + 32 * KB
    R1 = O + 65 * KB
    R2 = O + 97 * KB
    SM = O + 129 * KB
    kTb = carve(KT, [8, T], BF16)
    vaug = carve(VT, [16, 8, 129], BF16)
    xnb = carve(R1, [8, T], BF16)
    oT = carve(R1, [8, T], BF16)
    wk = [carve(R2 + i * 2 * KB, [8, 128], BF16) for i in range(3)]
    wvb = carve(R2 + 6 * KB, [8, 1024], BF16)
    rstd_b = carve(R2 + 22 * KB, [T], F32)
    qTb = carve(R2, [8, T], BF16)
    wbo = carve(R2, [8, 1024], BF16)
    hbb = carve(R2 + 16 * KB, [8, 512], F32)
    so = [SM]

    def sm(dims, dt):
        nb = int(np.prod(dims)) * (4 if dt == F32 else 2)
        nb = (nb + 31) // 32 * 32
        v = carve(so[0], dims, dt)
        so[0] += nb
        return v

    so[0] += 3 * KB
    wq = [sm([8, 128], BF16) for _ in range(2)]
    sqb = [carve(R2 + 30 * KB + i * KB, [512], BF16) for i in range(2)]
    of32 = [sm([128], F32) for _ in range(2)]
    onb = [sm([128], BF16) for _ in range(2)]
    rcp = sm([8], F32)
    ssb = sm([8], F32)
    sqjb = sm([128], BF16)
    sqb4 = [carve(SM + i * KB, [512], BF16) for i in range(2)]
    rstd_bo = carve(SM + 3 * KB, [512], F32)
    assert so[0] <= ARENA, so[0]

    lp = lamp.rearrange("p (a b) -> p a b", a=4)
    S.op("dve", lambda e: e.tensor_tensor(out=lamp[:, 0:64], in0=lp[:, 0, :], in1=lp[:, 1, :], op=ALU.mult),
         reads=["lamp"], writes=["lamp"])
    S.op("dve", lambda e: e.tensor_tensor(out=lamp[:, 128:192], in0=lp[:, 2, :], in1=lp[:, 3, :], op=ALU.mult),
         reads=["lamp"], writes=["lamp"])
    S.op("dve", lambda e: e.tensor_reduce(out=lamv[:, 0:1], in_=lamp[:, 0:64], axis=AX.X, op=ALU.add), reads=["lamp"], writes=["lamv0"])
    S.op("dve", lambda e: e.tensor_reduce(out=lamv[:, 1:2], in_=lamp[:, 128:192], axis=AX.X, op=ALU.add), reads=["lamp"], writes=["lamv1"])
    S.op("act", lambda e: e.activation(out=lamv[:, 2:4], in_=lamv[:, 0:2], func=AF.Exp), reads=["lamv0", "lamv1"], writes=["lamv2"])
    S.op("dve", lambda e: e.tensor_tensor(out=lamv[:, 4:5], in0=lamv[:, 3:4], in1=lamv[:, 2:3], op=ALU.subtract),
         reads=["lamv2"], writes=["lamv4"])
    S.op("dve", lambda e: e.tensor_scalar(out=lamv[:, 5:6], in0=lamv[:, 4:5], scalar1=-LAMBDA_INIT, scalar2=None, op0=ALU.add),
         reads=["lamv4"], writes=["neglam"])
    S.op("dve", lambda e: e.tensor_scalar(out=lamv[:, 6:7], in0=small[:, SUBLN0:SUBLN0 + 1], scalar1=1.0 - LAMBDA_INIT,
                                          scalar2=None, op0=ALU.mult),
         reads=["small"], writes=["gsub"])
    neglam = lamv[:, 5:6]
    gsub = lamv[:, 6:7]

    S.dma("pool", lambda e: e.dma_start(out=wvb, in_=w_kv[:, 1024:2048].rearrange("(c p) n -> p c n", p=128)), "wvb", writes=["wvb"])
    for r in range(NRNG):
        xr = xT_r(r)
        stats_rstd(lambda c: xr[:, c, :], lambda c: [f"big{r}"], sqb, rstd_b[:, r * 512:(r + 1) * 512], f"rstdb{r}")
        apply_norm(lambda c: xr[:, c, :], lambda c: [f"big{r}"], 2, rstd_b[:, r * 512:(r + 1) * 512], f"rstdb{r}",
                   lambda c: xnb[:, c, r * 512:(r + 1) * 512], lambda c: [f"xnb{c}_{r}"])
    bk = {"i": 0}

    def nbank():
        bk["i"] += 1
        return bk["i"] % 6

    for c in range(8):
        ws = c % 3
        S.dma("pool", lambda e, ws=ws, c=c: e.dma_start(out=wk[ws], in_=w_kv[:, c * 128:(c + 1) * 128].rearrange("(c p) n -> p c n", p=128)),
              f"wk{ws}", writes=[f"wk{ws}"])
        for r in range(NRNG):
            b = nbank()
            for kc in range(8):
                S.op("pe", lambda e, kc=kc, b=b, ws=ws, r=r: e.matmul(PS(b), lhsT=wk[ws][:, kc, :], rhs=xnb[:, kc, r * 512:(r + 1) * 512],
                                                                    start=(kc == 0), stop=(kc == 7)),
                     reads=[f"wk{ws}", f"xnb{kc}_{r}"], writes=[f"ps{b}"])
            S.op("act", lambda e, b=b, c=c, r=r: e.copy(out=kTb[:, c, r * 512:(r + 1) * 512], in_=PS(b)),
                 reads=[f"ps{b}"], writes=[f"kTb{c}"])
    for tt in range(16):
        S.op("pool", lambda e, tt=tt: e.memset(vaug[:, tt, :, 128:129], 1.0), writes=[f"vone{tt}"])
        for hh in range(2):
            b = nbank()
            for kc in range(8):
                S.op("pe", lambda e, kc=kc, b=b, tt=tt, hh=hh: e.matmul(PS(b), lhsT=xnb[:, kc, tt * 128:(tt + 1) * 128],
                                                                      rhs=wvb[:, kc, hh * 512:(hh + 1) * 512],
                                                                      start=(kc == 0), stop=(kc == 7)),
                     reads=["wvb", f"xnb{kc}_{tt // 4}"], writes=[f"ps{b}"])
            S.op("dve", lambda e, b=b, tt=tt, hh=hh: e.tensor_copy(out=vaug[:, tt, hh * 4:(hh + 1) * 4, 0:128],
                                                                   in_=PS(b).rearrange("p (a b) -> p a b", a=4)),
                 reads=[f"ps{b}"], writes=[f"vaug{tt}_{hh}"])
        if tt % 4 == 3:
            r = tt // 4
            xr = xT_r(r)
            apply_norm(lambda c: xr[:, c, :], lambda c: [f"big{r}"], 3, rstd_b[:, r * 512:(r + 1) * 512], f"rstdb{r}",
                       lambda c: xnb[:, c, r * 512:(r + 1) * 512], lambda c: [f"xnb{c}_{r}"])
    S.barrier()
    if stop_after == "B1":
        return finish()
    for c in range(8):
        ws = c % 2
        S.dma("pool", lambda e, ws=ws, c=c: e.dma_start(out=wq[ws], in_=b_w_q[:, c * 128:(c + 1) * 128].rearrange("(c p) n -> p c n", p=128)),
              f"wq{ws}", writes=[f"wq{ws}"])
        for r in range(NRNG):
            b = nbank()
            for kc in range(8):
                S.op("pe", lambda e, kc=kc, b=b, ws=ws, r=r: e.matmul(PS(b), lhsT=wq[ws][:, kc, :], rhs=xnb[:, kc, r * 512:(r + 1) * 512],
                                                                    start=(kc == 0), stop=(kc == 7)),
                     reads=[f"wq{ws}", f"xnb{kc}_{r}"], writes=[f"ps{b}"])
            S.op("act", lambda e, b=b, c=c, r=r: e.mul(out=qTb[:, c, r * 512:(r + 1) * 512], in_=PS(b), mul=0.125),
                 reads=[f"ps{b}"], writes=[f"qTb{c}"])
    S.barrier()
    if stop_after == "B2":
        return finish()
    pT3 = [carve(SM + i * KB, [512], BF16) for i in range(3)]
    accsb = carve(SM + 3 * KB, [4, 129], F32)
    STB = [0, 1, 3]
    ptrb = PS(2).bitcast(BF16)

    QREG = {0: ("oT", 6), 1: ("oT", 4), 2: ("qTb", 0), 3: ("oT", 6), 4: ("qTb", 0), 5: ("qTb", 2), 6: ("qTb", 0), 7: ("qTb", 2)}

    def alias_keys(h):
        nm, k = QREG[h]
        return [f"{nm}{k}", f"{nm}{k + 1}"]

    def qblk(h):
        nm, k = QREG[h]
        src = oT[:, k:k + 2, :] if nm == "oT" else qTb[:, k:k + 2, :]
        return src.rearrange("p a b -> p (a b)")

    def build_qblk(h):
        flat = qblk(h)
        v4 = flat.rearrange("p (r m c) -> p r m c", r=16, m=2)
        ak = alias_keys(h)
        S.op("pool", lambda e: e.memset(flat, 0.0), writes=ak)
        S.op("dve", lambda e: e.tensor_copy(out=v4[0:64, :, 0, :], in_=qTb[0:64, h, :].rearrange("p (r c) -> p r c", r=16)),
             reads=[f"qTb{h}"], writes=ak)
        S.op("dve", lambda e: e.tensor_copy(out=v4[64:128, :, 1, :], in_=qTb[64:128, h, :].rearrange("p (r c) -> p r c", r=16)),
             reads=[f"qTb{h}"], writes=ak)

    def st_mm(h, qb, kb, slot, col0=0):
        if h == 0 and qb == 0 and kb == 0:
            build_qblk(0)
        if qb == 8 and kb == 0 and h + 1 < 8:
            build_qblk(h + 1)
        qb_ = qblk(h)
        rk = alias_keys(h) + [f"kTb{h}"]
        diag = kb == qb
        S.op("pe", lambda e: e.matmul(PS(slot, 256, col0), lhsT=kTb[:, h, kb * 128:(kb + 1) * 128], rhs=qb_[:, qb * 256:(qb + 1) * 256],
                                      start=True, stop=not diag),
             reads=rk, writes=[f"ps{slot}"])
        if diag:
            for m in range(2):
                S.op("pe", lambda e, m=m: e.matmul(PS(slot, 128, col0 + m * 128), lhsT=ident_bf, rhs=maskneg_bf, start=False, stop=(m == 1)),
                     reads=["cmat"], writes=[f"ps{slot}"])

    ACC2 = SM + 3 * KB
    ONB3 = ACC2 + 4128
    onb3 = [carve(ONB3 + p * 1024, [4, 128], BF16) for p in range(2)]
    FSM = ONB3 + 2048
    rcp8 = carve(FSM, [8], F32)
    ss4 = carve(FSM + 32, [8], F32)
    assert FSM + 64 <= ARENA
    BREG = {0: ("oT", 2), 1: ("oT", 2), 2: ("oT", 4), 3: ("oT", 4), 4: ("oT", 6), 5: ("oT", 6), 6: ("qTb", 4), 7: ("qTb", 4)}

    def accbuf(h, g):
        if g % 2 == 0:
            return carve(ACC2, [8, 129], F32), ["acc8A"]
        nm, k = BREG[h]
        base = (R1 if nm == "oT" else R2) + k * 4096
        return carve(base, [8, 129], F32), ["acc8B", f"{nm}{k}", f"{nm}{k + 1}"]

    def fin_copy(h, qb):
        par = qb % 2
        i4 = qb % 4
        ap8, keys = accbuf(h, qb // 4)
        S.op("act", lambda e: e.copy(out=ap8[:, i4 * 2:(i4 + 1) * 2, :],
                                     in_=psum[:, (4 + 2 * par) * 512:(6 + 2 * par) * 512].rearrange("p (a b) -> p a b", a=2)[:, :, 0:129]),
             reads=[f"ps{4 + 2 * par}", f"ps{5 + 2 * par}"], writes=keys)

    def finalize1(h, P, slot):
        acc8, keys = accbuf(h, P)
        a4 = acc8.rearrange("p (q m) c -> p q m c", m=2)
        S.op("dve", lambda e: e.reciprocal(out=rcp8.rearrange("p (a b) -> p a b", b=1), in_=acc8[:, :, 128:129]),
             reads=keys, writes=["rcp8"])
        S.op("dve", lambda e: e.tensor_scalar(out=rcp8.rearrange("p (q m) -> p q m", m=2)[:, :, 1],
                                              in0=rcp8.rearrange("p (q m) -> p q m", m=2)[:, :, 1],
                                              scalar1=neglam, scalar2=None, op0=ALU.mult),
             reads=["rcp8", "neglam"], writes=["rcp8"])
        S.op("dve", lambda e: e.tensor_tensor(out=acc8[:, :, 0:128], in0=acc8[:, :, 0:128],
                                              in1=rcp8.rearrange("p (a b) -> p a b", b=1).to_broadcast([128, 8, 128]), op=ALU.mult),
             reads=keys + ["rcp8"], writes=keys)
        S.op("dve", lambda e: e.tensor_tensor(out=a4[:, :, 0, 0:128], in0=a4[:, :, 0, 0:128], in1=a4[:, :, 1, 0:128], op=ALU.add),
             reads=keys, writes=keys)
        S.op("dve", lambda e: e.tensor_tensor(out=a4[:, :, 1, 0:128], in0=a4[:, :, 0, 0:128], in1=a4[:, :, 0, 0:128], op=ALU.mult),
             reads=keys, writes=keys)
        S.op("dve", lambda e: e.tensor_reduce(out=ss4[:, 0:4], in_=a4[:, :, 1, 0:128], axis=AX.X, op=ALU.add),
             reads=keys, writes=["ss4a"])
        S.op("dve", lambda e: e.tensor_scalar(out=ss4[:, 4:8], in0=ss4[:, 0:4], scalar1=1.0 / 128, scalar2=EPS,
                                              op0=ALU.mult, op1=ALU.add),
             reads=["ss4a"], writes=["ss4b"])
        S.op("pool", lambda e: e.tensor_tensor(out=ss4[:, 0:4], in0=ss4[:, 4:8], in1=neghalf[:, 0:4], op=ALU.pow),
             reads=["ss4b", "neghalf"], writes=["ss4c", "ss4a"])
        S.op("dve", lambda e: e.tensor_tensor(out=onb3[slot], in0=a4[:, :, 0, 0:128],
                                              in1=ss4[:, 0:4].rearrange("p (a b) -> p a b", b=1).to_broadcast([128, 4, 128]), op=ALU.mult),
             reads=keys + ["ss4c"], writes=[f"onb{slot}"])

    def finalize2(h, P, slot):
        q0 = P * 512
        for i4 in range(4):
            S.op("pe", lambda e, i4=i4: e.transpose(ptrb[:, i4 * 128:(i4 + 1) * 128], onb3[slot][:, i4, :], ident_bf),
                 reads=[f"onb{slot}", "cmat"], writes=["ps2"])
        S.op("dve", lambda e: e.tensor_scalar(out=oT[:, h, q0:q0 + 512], in0=ptrb[:, 0:512], scalar1=gsub, scalar2=None, op0=ALU.mult),
             reads=["ps2", "gsub"], writes=[f"oT{h}"])

    pend = {"p": None, "n": 0}

    steps = [(h, qb, kb) for h in range(8) for qb in range(16) for kb in range(qb + 1)]
    pairs = [(steps[2 * i], steps[2 * i + 1]) for i in range(len(steps) // 2)]

    def st_pair(p, slot):
        st_mm(*pairs[p][0], slot, 0)
        st_mm(*pairs[p][1], slot, 256)

    st_pair(0, STB[0])
    st_pair(1, STB[1])
    for p, pr in enumerate(pairs):
        slot = STB[p % 3]
        sl = p % 3
        if p + 2 < len(pairs):
            st_pair(p + 2, STB[(p + 2) % 3])
        S.op("act", lambda e: e.activation(out=pT3[sl], in_=PS(slot), func=AF.Exp),
             reads=[f"ps{slot}"], writes=[f"pT{sl}"])
        for idx, (h, qb, kb) in enumerate(pr):
            c0 = idx * 256
            par = qb % 2
            for m in range(2):
                S.op("pe", lambda e, m=m: e.matmul(PS(4 + 2 * par + m, 129, 0), lhsT=pT3[sl][:, c0 + m * 128:c0 + (m + 1) * 128],
                                                  rhs=vaug[:, kb, h, :], start=(kb == 0), stop=(kb == qb)),
                     reads=[f"pT{sl}", f"vaug{kb}_{h // 4}", f"vone{kb}"], writes=[f"ps{4 + 2 * par + m}"])
            if kb == qb:
                fin_copy(h, qb)
                if qb % 4 == 3:
                    if pend["p"] is not None:
                        finalize2(*pend["p"])
                    slot2 = pend["n"] % 2
                    pend["n"] += 1
                    finalize1(h, qb // 4, slot2)
                    pend["p"] = (h, qb // 4, slot2)
    finalize2(*pend["p"])
    S.barrier()
    if stop_after == "B3":
        return finish()
    S.dma("pool", lambda e: e.dma_start(out=wbo, in_=b_w_out.rearrange("(c p) n -> p c n", p=128)), "wbo", writes=["wbo"])
    for r in range(NRNG):
        xr = xT_r(r)
        proj_post(r, 8, lambda k, dc: wbo[:, k, dc * 128:(dc + 1) * 128], lambda k: oT[:, k, r * 512:(r + 1) * 512],
                  lambda k: ["wbo", f"oT{k}"], hbb, "hbb", 4, sqb4, rstd_bo, "rstdbo",
                  lambda c: xr[:, c, :], lambda c: [f"big{r}"], lambda c: [f"big{r}"])
        if r == 2 and stop_after != "B":
            ffn_prenorm0(1)
    if stop_after == "B":
        S.barrier()
        return finish()
    S.soft_barrier()
    ffn(1, pre_done=True, wu_alias={0: ["oT1", "oT2"], 1: ["oT2", "oT3"], 2: ["oT3", "oT4"]})
    return finish()


def _const_tables():
    f32 = np.float32
    cm = np.zeros((128, 4, 128), f32)
    cm[:, 0, :] = 1.0 / 1024.0
    cm[:, 1, :] = np.eye(128, dtype=f32)
    kk = np.arange(128)
    cm[:, 2, :] = (kk[:, None] <= kk[None, :]).astype(f32)
    cm[:, 3, :] = -30000.0 * (kk[:, None] > kk[None, :]).astype(f32)
    half = 128
    inv = 1.0 / (10000.0 ** np.linspace(0.0, 1.0, half, dtype=np.float64))
    ang = np.arange(T, dtype=np.float64)[None, :] * inv[:, None]
    cs = np.stack([np.cos(ang), np.sin(ang)], axis=1).astype(f32)
    rt = np.zeros((128, 4, 640), f32)
    kdec = np.zeros((128, 4), f32)
    idx = np.arange(128, dtype=np.float64)
    for h in range(4):
        lg = np.log1p(-(2.0 ** (-5.0 - h)))
        rel = idx[None, :] - idx[:, None]
        m = np.where(rel >= 0, np.exp(lg * np.maximum(rel, 0.0)), 0.0) * (256.0 ** -0.5)
        rt[:, h, 0:128] = m
        qd = np.exp(lg * (idx + 1.0))
        rt[:, h, 128:640] = np.tile(qd, 4)[None, :]
        kdec[:, h] = np.exp(lg * (127.0 - idx)) * (256.0 ** -0.5)
    return cm.reshape(128, 512), cs.reshape(128, 2 * T), rt.reshape(128, 4 * 640), kdec


def _lay_up(w):
    return np.ascontiguousarray(w.reshape(8, 128, 2, 22, 128).transpose(3, 1, 0, 2, 4).reshape(22, 128, 2048))


def _lay_down(w):
    return np.ascontiguousarray(w.reshape(22, 128, 8, 128).transpose(2, 1, 0, 3).reshape(8, 128, 2816))


_CACHE = {}


def _get_nc(stop_after="F1"):
    if stop_after not in _CACHE:
        _CACHE[stop_after] = build(stop_after)[0]
    return _CACHE[stop_after]


def make_in_maps(inputs):
    f32 = np.float32
    g = lambda k: np.asarray(inputs[k], dtype=f32)
    cm, cs, rt, kdec = _const_tables()
    small = np.zeros((128, 512), f32)
    gl = [g("a_norm_pre")[0], g("a_norm_post")[0], g("kv_norm"), g("b_norm_pre")[0], g("b_norm_post")[0],
          g("ffn_norm_pre")[0], g("ffn_norm_pre")[1], g("ffn_norm_post")[0], g("ffn_norm_post")[1]]
    for i, v in enumerate(gl):
        small[:, i * 8:(i + 1) * 8] = v.reshape(8, 128).T
    cw, cb = g("ffn_conv_w"), g("ffn_conv_b")
    for l in range(2):
        for j in range(3):
            small[:, 72 + (l * 4 + j) * 44:72 + (l * 4 + j + 1) * 44] = cw[l, j].reshape(44, 128).T
        small[:, 72 + (l * 4 + 3) * 44:72 + (l * 4 + 4) * 44] = cb[l].reshape(44, 128).T
    small[:, 424:428] = kdec
    small[:, 428] = g("b_subln")[0]
    small[:, 464] = EPS
    lamp = np.ascontiguousarray(np.broadcast_to(g("b_lambda")[0].reshape(1, 256), (128, 256)))
    shared = {
        "a_w_in": np.ascontiguousarray(g("a_w_in")[0]), "a_w_out": np.ascontiguousarray(g("a_w_out")[0]),
        "w_kv": np.ascontiguousarray(g("w_kv")), "b_w_q": np.ascontiguousarray(g("b_w_q")[0]),
        "b_w_out": np.ascontiguousarray(g("b_w_out")[0]),
        "w_up0": _lay_up(g("ffn_w_up")[0]), "w_up1": _lay_up(g("ffn_w_up")[1]),
        "w_down0": _lay_down(g("ffn_w_down")[0]), "w_down1": _lay_down(g("ffn_w_down")[1]),
        "cmat": cm, "cossin": cs, "rtab": rt, "small": small, "lamp": lamp,
    }
    x = g("x")
    maps = []
    for b in range(x.shape[0]):
        d = dict(shared)
        d["xT"] = np.ascontiguousarray(x[b].T)
        maps.append(d)
    return maps


def kernel(**inputs):
    nc = _get_nc("F1")
    in_maps = make_in_maps(inputs)
    res = run_bass_kernel_spmd(nc, in_maps, core_ids=list(range(len(in_maps))))
    out = np.stack([np.ascontiguousarray(r["outT"].T) for r in res.results], axis=0)
    return out.astype(np.float32)
```

```python
import math
from contextlib import ExitStack
import numpy as np
import concourse.bass as bass
import concourse.mybir as mybir
from concourse.bass_utils import run_bass_kernel_spmd

F32 = mybir.dt.float32
BF16 = mybir.dt.bfloat16
U8 = mybir.dt.uint8
ALU = mybir.AluOpType
AF = mybir.ActivationFunctionType
AX = mybir.AxisListType

ENGS = ("pe", "act", "dve", "pool", "sp")
KB = 1024
EPS = 1e-6
T = 2048
NRNG = 4
H_RET = 4
LAMBDA_INIT = 0.8 - 0.6 * math.exp(-0.3 * 1)


class _Rec:
    def __init__(self):
        self.call = None

    def __getattr__(self, name):
        def f(*a, **k):
            self.call = (name, a, k)
            return self
        return f


def _capture(fn):
    r = _Rec()
    fn(r)
    assert r.call is not None
    return r.call


class Sched:
    def __init__(self, nc):
        self.nc = nc
        self.ops = {e: [] for e in ENGS}
        self.count = {e: 0 for e in ENGS}
        self.actors = list(ENGS)
        self.sems = {}
        self.clock = {e: {} for e in ENGS}
        self.ev_clock = {}
        self.res = {}
        self.dma_count = {}
        self.n_waits = 0
        self.n_ops = 0

    def _need(self, eng, ev, waits):
        a, c = ev
        if self.clock[eng].get(a, 0) >= c:
            return
        waits[a] = max(waits.get(a, 0), c)

    def _deps(self, eng, reads, writes, strict=False):
        waits = {}
        for k in reads:
            st = self.res.get(k)
            if st is None:
                continue
            if st[0] is not None:
                self._need(eng, st[0], waits)
        for k in writes:
            st = self.res.get(k)
            if st is None:
                continue
            w = st[0]
            if w is not None and (strict or w[0] != eng or eng != "pe"):
                self._need(eng, w, waits)
            for r in st[1]:
                if strict or r[0] != eng or eng != "pe":
                    self._need(eng, r, waits)
        return waits

    def _apply_waits(self, eng, waits):
        for a, c in waits.items():
            if self.clock[eng].get(a, 0) >= c:
                continue
            self.ops[eng].append(("wait", a, c))
            self.n_waits += 1
            ck = self.ev_clock.get((a, c))
            if ck:
                mine = self.clock[eng]
                for k2, v2 in ck.items():
                    if mine.get(k2, 0) < v2:
                        mine[k2] = v2
            if self.clock[eng].get(a, 0) < c:
                self.clock[eng][a] = c

    def _record(self, ev, reads, writes):
        for k in reads:
            st = self.res.setdefault(k, [None, []])
            st[1].append(ev)
        for k in writes:
            self.res[k] = [ev, []]

    def op(self, eng, fn, reads=(), writes=()):
        self._apply_waits(eng, self._deps(eng, reads, writes))
        self.count[eng] += 1
        ev = (eng, self.count[eng])
        self.ops[eng].append(("op", _capture(fn), eng, 1))
        snap = dict(self.clock[eng])
        snap[eng] = ev[1]
        self.ev_clock[ev] = snap
        self._record(ev, reads, writes)
        self.n_ops += 1
        return ev

    def dma(self, queue, fn, actor, reads=(), writes=()):
        if actor not in self.dma_count:
            self.dma_count[actor] = 0
            self.actors.append(actor)
        self._apply_waits(queue, self._deps(queue, reads, writes, strict=True))
        self.dma_count[actor] += 16
        ev = (actor, self.dma_count[actor])
        self.ops[queue].append(("op", _capture(fn), actor, 16))
        snap = dict(self.clock[queue])
        snap[actor] = ev[1]
        self.ev_clock[ev] = snap
        self._record(ev, reads, writes)
        return ev

    def _targets(self):
        t = {e: self.count[e] for e in ENGS if self.count[e] > 0}
        for a, c in self.dma_count.items():
            if c > 0:
                t[a] = c
        return t

    def barrier(self):
        t = self._targets()
        for e in ENGS:
            self._apply_waits(e, dict(t))
        self.res = {}

    def soft_barrier(self, engs=("act", "dve", "pool")):
        t = {e: self.count[e] for e in engs if self.count[e] > 0}
        for e in engs:
            w = dict(t)
            w.pop(e, None)
            self._apply_waits(e, w)

    def final_wait(self, eng="sp"):
        t = self._targets()
        t.pop(eng, None)
        self._apply_waits(eng, t)

    def emit(self):
        nc = self.nc
        with ExitStack() as st:
            for a in self.actors:
                self.sems[a] = st.enter_context(nc.semaphore("s_" + a))
            block = st.enter_context(nc.Block())
            engmap = {"pe": block.tensor, "act": block.scalar, "dve": block.vector,
                      "pool": block.gpsimd, "sp": block.sync}

            def make(ename):
                lst = self.ops[ename]

                def body(eng):
                    for item in lst:
                        if item[0] == "wait":
                            eng.wait_ge(self.sems[item[1]], item[2])
                        else:
                            _, call, actor, inc = item
                            getattr(eng, call[0])(*call[1], **call[2]).then_inc(self.sems[actor], inc)
                return body

            for ename in ENGS:
                if self.ops[ename]:
                    engmap[ename](make(ename))


PHASES = ["A", "F0", "B", "F1"]


def build(stop_after="F1"):
    nc = bass.Bass("TRN2", target_bir_lowering=False)
    S = Sched(nc)

    def din(name, shape):
        return nc.dram_tensor(name, shape, F32, kind="ExternalInput").ap()

    xT_d = din("xT", [1024, T])
    a_w_in = din("a_w_in", [1024, 6144])
    a_w_out = din("a_w_out", [2048, 1024])
    w_kv = din("w_kv", [1024, 2048])
    b_w_q = din("b_w_q", [1024, 1024])
    b_w_out = din("b_w_out", [1024, 1024])
    w_up_r = [din(f"w_up{l}", [22, 128, 2048]) for l in range(2)]
    w_down_r = [din(f"w_down{l}", [8, 128, 2816]) for l in range(2)]
    cmat_d = din("cmat", [128, 4 * 128])
    cossin_d = din("cossin", [128, 2 * T])
    rtab_d = din("rtab", [128, 4 * 640])
    small_d = din("small", [128, 512])
    lamp_d = din("lamp", [128, 256])
    outT = nc.dram_tensor("outT", [1024, T], F32, kind="ExternalOutput").ap()
    xT_v = xT_d.rearrange("(c p) t -> p c t", p=128)
    outT_v = outT.rearrange("(c p) t -> p c t", p=128)

    ARENA = 212736
    arena = nc.alloc_sbuf_tensor("arena", [128, ARENA], U8)
    psum = nc.alloc_psum_tensor("psum", [128, 4096], F32)

    def carve(off, dims, dt):
        nb = int(np.prod(dims)) * (4 if dt == F32 else 2)
        assert off % 32 == 0 and off + nb <= ARENA, (off, nb)
        v = arena[:, off:off + nb].bitcast(dt)
        if len(dims) == 2:
            v = v.rearrange("p (a b) -> p a b", a=dims[0])
        elif len(dims) == 3:
            v = v.rearrange("p (a b c) -> p a b c", a=dims[0], b=dims[1])
        return v

    def PS(b, n=512, off=0):
        return psum[:, b * 512 + off:b * 512 + off + n]

    cmat = carve(0, [4, 128], BF16)
    ones_bf = cmat[:, 0, :]
    ident_bf = cmat[:, 1, :]
    cmask_bf = cmat[:, 2, :]
    maskneg_bf = cmat[:, 3, :]
    small = carve(1 * KB, [512], F32)
    GAIN0, CONV0, KDEC0, SUBLN0 = 0, 72, 424, 428
    lamv = small[:, 448:464]
    neghalf = carve(3 * KB, [512], BF16)
    lamp = carve(4 * KB, [256], F32)
    BIG = 5 * KB
    O = BIG + 64 * KB

    def gain(gi, c):
        return small[:, GAIN0 + gi * 8 + c:GAIN0 + gi * 8 + c + 1]

    def xT_r(r):
        return carve(BIG + r * 16 * KB, [8, 512], F32)

    S.dma("pool", lambda e: e.dma_start(out=cmat, in_=cmat_d.rearrange("p (a b) -> p a b", a=4)), "cst", writes=["cmat"])
    S.dma("sp", lambda e: e.dma_start(out=small, in_=small_d), "cst2", writes=["small"])
    S.dma("sp", lambda e: e.dma_start(out=lamp, in_=lamp_d), "cst3", writes=["lamp"])
    S.op("pool", lambda e: e.memset(neghalf, -0.5), writes=["neghalf"])
    epsb = small[:, 464:465]

    rot = {"sq": 0, "dvepool": 0}

    def alt_engine():
        rot["dvepool"] ^= 1
        return "dve" if rot["dvepool"] else "pool"

    def stats_rstd(src_fn, src_keys, sq_tiles, rstd_ap, rstd_key, bank=7):
        for c in range(8):
            j = rot["sq"] % len(sq_tiles)
            rot["sq"] += 1
            sq = sq_tiles[j]
            src = src_fn(c)
            S.op("act", lambda e, sq=sq, src=src: e.activation(out=sq, in_=src, func=AF.Square),
                 reads=src_keys(c) + ["small"], writes=[f"sqt{j}"])
            S.op("pe", lambda e, sq=sq, c=c: e.matmul(PS(bank), lhsT=ones_bf, rhs=sq, start=(c == 0), stop=(c == 7)),
                 reads=[f"sqt{j}", "cmat"], writes=[f"ps{bank}"])
        S.op("act", lambda e: e.activation(out=rstd_ap, in_=PS(bank), func=AF.Sqrt, bias=epsb, scale=1.0),
             reads=[f"ps{bank}", "small"], writes=[rstd_key])
        S.op("dve", lambda e: e.reciprocal(out=rstd_ap, in_=rstd_ap),
             reads=[rstd_key], writes=[rstd_key])

    def apply_norm(src_fn, src_keys, gi, rstd_ap, rstd_key, dst_fn, dst_keys):
        for c in range(8):
            eng = "dve"
            S.op(eng, lambda e, c=c: e.scalar_tensor_tensor(out=dst_fn(c), in0=src_fn(c), scalar=gain(gi, c),
                                                            in1=rstd_ap, op0=ALU.mult, op1=ALU.mult),
                 reads=src_keys(c) + [rstd_key, "small"], writes=dst_keys(c))

    def post_residual(r, hbuf, hkey, gi, sq_tiles, rstd_ap, rstd_key, xsrc_fn, xsrc_keys, xdst_keys):
        stats_rstd(lambda c: hbuf[:, c, :], lambda c: [f"{hkey}{c}"], sq_tiles, rstd_ap, rstd_key)
        xr = xT_r(r)
        for c in range(8):
            S.op("dve", lambda e, c=c: e.scalar_tensor_tensor(out=hbuf[:, c, :], in0=hbuf[:, c, :], scalar=gain(gi, c),
                                                              in1=rstd_ap, op0=ALU.mult, op1=ALU.mult),
                 reads=[f"{hkey}{c}", rstd_key, "small"], writes=[f"{hkey}{c}"])
            S.op("pool", lambda e, c=c: e.tensor_tensor(out=xr[:, c, :], in0=hbuf[:, c, :], in1=xsrc_fn(c), op=ALU.add),
                 reads=[f"{hkey}{c}"] + xsrc_keys(c), writes=xdst_keys(c))

    def proj_to_hbuf(r, nk, lhsT_fn, rhs_fn, in_keys, hbuf, hkey, banks=(0, 1, 2, 3, 4, 5)):
        for dc in range(8):
            b = banks[(r * 8 + dc) % len(banks)]
            for k in range(nk):
                S.op("pe", lambda e, k=k, dc=dc, b=b: e.matmul(PS(b), lhsT=lhsT_fn(k, dc), rhs=rhs_fn(k),
                                                               start=(k == 0), stop=(k == nk - 1)),
                     reads=in_keys(k), writes=[f"ps{b}"])
            S.op("act", lambda e, dc=dc, b=b: e.copy(out=hbuf[:, dc, :], in_=PS(b)),
                 reads=[f"ps{b}"], writes=[f"{hkey}{dc}"])

    def proj_post(r, nk, lhsT_fn, rhs_fn, in_keys, hbuf, hkey, gi, sq_tiles, rstd_ap, rstd_key,
                  xsrc_fn, xsrc_keys, xdst_keys, banks=(0, 1, 2, 3, 4), sbank=7):
        pend = None
        for dc in range(8):
            b = banks[(r * 8 + dc) % len(banks)]
            for k in range(nk):
                S.op("pe", lambda e, k=k: e.matmul(PS(b), lhsT=lhsT_fn(k, dc), rhs=rhs_fn(k), start=(k == 0), stop=(k == nk - 1)),
                     reads=in_keys(k), writes=[f"ps{b}"])
            if pend is not None:
                pend()
            S.op("act", lambda e: e.mul(out=hbuf[:, dc, :], in_=PS(b), mul=gain(gi, dc)),
                 reads=[f"ps{b}", "small"], writes=[f"{hkey}{dc}"])
            j = rot["sq"] % len(sq_tiles)
            rot["sq"] += 1
            S.op("act", lambda e: e.activation(out=sq_tiles[j], in_=PS(b), func=AF.Square),
                 reads=[f"ps{b}"], writes=[f"sqt{j}"])

            def pend(j=j, dc=dc):
                S.op("pe", lambda e: e.matmul(PS(sbank), lhsT=ones_bf, rhs=sq_tiles[j], start=(dc == 0), stop=(dc == 7)),
                     reads=[f"sqt{j}", "cmat"], writes=[f"ps{sbank}"])
        pend()
        S.op("act", lambda e: e.activation(out=rstd_ap, in_=PS(sbank), func=AF.Sqrt, bias=epsb, scale=1.0),
             reads=[f"ps{sbank}", "small"], writes=[rstd_key])
        S.op("dve", lambda e: e.reciprocal(out=rstd_ap, in_=rstd_ap), reads=[rstd_key], writes=[rstd_key])
        xr = xT_r(r)
        for c in range(8):
            S.op("dve", lambda e: e.tensor_tensor(out=hbuf[:, c, :], in0=hbuf[:, c, :], in1=rstd_ap, op=ALU.mult),
                 reads=[f"{hkey}{c}", rstd_key], writes=[f"{hkey}{c}"])
            eng = "dve"
            S.op(eng, lambda e: e.tensor_tensor(out=xr[:, c, :], in0=hbuf[:, c, :], in1=xsrc_fn(c), op=ALU.add),
                 reads=[f"{hkey}{c}"] + xsrc_keys(c), writes=xdst_keys(c))

    def ffn_prenorm0(l):
        xnh = carve(O, [8, 1024], BF16)
        sqf = [carve(O + 16 * KB + i * KB, [512], BF16) for i in range(3)]
        rstd_f = [carve(O + 16 * KB + 3 * KB + i * 2 * KB, [512], F32) for i in range(2)]
        for rr in range(2):
            xr = xT_r(rr)
            stats_rstd(lambda c: xr[:, c, :], lambda c: [f"big{rr}"], sqf, rstd_f[rr], f"rstdf{rr}", bank=5)
            apply_norm(lambda c: xr[:, c, :], lambda c: [f"big{rr}"], 5 + l, rstd_f[rr], f"rstdf{rr}",
                       lambda c: xnh[:, c, rr * 512:(rr + 1) * 512], lambda c: [f"xnh{c}_{rr}"])

    XN = O
    xn = carve(XN, [8, T], BF16)
    WQK = O + 32 * KB
    WVG = O + 40 * KB
    QKB = O + 72 * KB
    ROT = O + 80 * KB
    CS = O + 88 * KB
    RTAB = O + 104 * KB
    MISC = O + 114 * KB
    wqk = carve(WQK, [8, 512], BF16)
    wvg = [carve(WVG + i * 16 * KB, [8, 1024], BF16) for i in range(2)]
    qT = carve(QKB, [2, 512], BF16)
    qdT = carve(QKB + 2 * KB, [2, 512], BF16)
    kT = carve(QKB + 4 * KB, [2, 512], BF16)
    ktok = carve(QKB + 6 * KB, [4, 256], BF16)
    rt = [carve(ROT + i * 2 * KB, [512], F32) for i in range(4)]
    cossin = carve(CS, [2, T], F32)
    rtab = carve(RTAB, [4, 640], F32)
    mo = [MISC]

    def misc(dims, dt):
        nb = int(np.prod(dims)) * (4 if dt == F32 else 2)
        nb = (nb + 31) // 32 * 32
        v = carve(mo[0], dims, dt)
        mo[0] += nb
        return v

    vtok = [misc([512], BF16) for _ in range(3)]
    sgt = [misc([512], BF16) for _ in range(3)]
    sTb = [misc([128], BF16) for _ in range(2)]
    Sf = misc([2, 512], F32)
    Sb = [misc([2, 512], BF16) for _ in range(2)]
    ybf = [misc([512], BF16) for _ in range(2)]
    sqj = misc([512], BF16)
    sqa = [misc([512], BF16) for _ in range(2)]
    ssr = misc([8], F32)
    rstd_a = [misc([512], F32) for _ in range(2)]
    assert mo[0] <= ARENA, mo[0]

    wout_a = carve(WVG + ((H_RET - 2) % 2) * 16 * KB, [8, 1024], BF16)
    wout_b = carve(CS, [8, 1024], BF16)

    def load_wqk(h):
        S.dma("pool", lambda e: e.dma_start(out=wqk[:, :, 0:256],
                                            in_=a_w_in[:, h * 256:(h + 1) * 256].rearrange("(c p) n -> p c n", p=128)),
              "wqk_q", writes=["wqk_q"])
        S.dma("pool", lambda e: e.dma_start(out=wqk[:, :, 256:512],
                                            in_=a_w_in[:, 1024 + h * 256:1024 + (h + 1) * 256].rearrange("(c p) n -> p c n", p=128)),
              "wqk_k", writes=["wqk_k"])

    def load_wvg(h):
        sl = h % 2
        S.dma("pool", lambda e: e.dma_start(out=wvg[sl][:, :, 0:512],
                                            in_=a_w_in[:, 2048 + h * 512:2048 + (h + 1) * 512].rearrange("(c p) n -> p c n", p=128)),
              f"wvg_v{sl}", writes=[f"wvg_v{sl}"])
        S.dma("pool", lambda e: e.dma_start(out=wvg[sl][:, :, 512:1024],
                                            in_=a_w_in[:, 4096 + h * 512:4096 + (h + 1) * 512].rearrange("(c p) n -> p c n", p=128)),
              f"wvg_g{sl}", writes=[f"wvg_g{sl}"])

    load_wqk(0)
    load_wvg(0)

    def a0_dma(r):
        sl = r % 2
        xs = carve(BIG + (2 + sl) * 16 * KB, [8, 512], F32)
        S.dma("sp", lambda e: e.dma_start(out=xs, in_=xT_v[:, :, r * 512:(r + 1) * 512]), f"xs{sl}",
              writes=[f"xs{sl}_{c}" for c in range(8)] + [f"big{2 + sl}"])

    def a0_range(r):
        sl = r % 2
        xs = carve(BIG + (2 + sl) * 16 * KB, [8, 512], F32)
        stats_rstd(lambda c: xs[:, c, :], lambda c: [f"xs{sl}_{c}", f"big{2 + sl}"], sqa, rstd_a[sl], f"rstda{sl}")
        apply_norm(lambda c: xs[:, c, :], lambda c: [f"xs{sl}_{c}", f"big{2 + sl}"], 0, rstd_a[sl], f"rstda{sl}",
                   lambda c: xn[:, c, r * 512:(r + 1) * 512], lambda c: [f"xn{c}_{r}"])
        if r + 2 < NRNG:
            a0_dma(r + 2)

    a0_dma(0)
    a0_dma(1)
    S.dma("sp", lambda e: e.dma_start(out=cossin, in_=cossin_d.rearrange("p (a b) -> p a b", a=2)), "cs", writes=["cossin"])
    S.dma("sp", lambda e: e.dma_start(out=rtab, in_=rtab_d.rearrange("p (a b) -> p a b", a=4)), "rtab", writes=["rtab"])

    GAMMA = [1.0 - 2.0 ** (-5.0 - h) for h in range(H_RET)]
    cnt = {"v": 0, "sT": 0, "Sb": 0, "y": 0}
    pend_tr = {"f": None}
    for h in range(H_RET):
        wv = wvg[h % 2]
        wvk = f"wvg_v{h % 2}"
        wgk = f"wvg_g{h % 2}"
        gC = GAMMA[h] ** 128
        for r in range(NRNG):
            tok = slice(r * 512, (r + 1) * 512)
            if h == 0 and r == 0:
                a0_range(0)
            for dc in range(4):
                wkey = "wqk_q" if dc < 2 else "wqk_k"
                for kc in range(8):
                    S.op("pe", lambda e, dc=dc, kc=kc: e.matmul(PS(dc), lhsT=wqk[:, kc, dc * 128:(dc + 1) * 128],
                                                                rhs=xn[:, kc, tok], start=(kc == 0), stop=(kc == 7)),
                         reads=[wkey, f"xn{kc}_{r}"], writes=[f"ps{dc}"])
            if r == NRNG - 1 and h + 1 < H_RET:
                load_wqk(h + 1)
            if h == 0 and r == 0:
                load_wvg(1)
            cos_r = cossin[:, 0, tok]
            sin_r = cossin[:, 1, tok]
            for qk in range(2):
                b1, b2 = 2 * qk, 2 * qk + 1
                S.op("dve", lambda e: e.tensor_tensor(out=rt[0], in0=PS(b1), in1=cos_r, op=ALU.mult),
                     reads=[f"ps{b1}", "cossin"], writes=["rt0"])
                S.op("dve", lambda e: e.tensor_tensor(out=rt[1], in0=PS(b2), in1=sin_r, op=ALU.mult),
                     reads=[f"ps{b2}", "cossin"], writes=["rt1"])
                S.op("dve", lambda e: e.tensor_tensor(out=rt[2], in0=PS(b1), in1=sin_r, op=ALU.mult),
                     reads=[f"ps{b1}", "cossin"], writes=["rt2"])
                S.op("dve", lambda e: e.tensor_tensor(out=rt[3], in0=PS(b2), in1=cos_r, op=ALU.mult),
                     reads=[f"ps{b2}", "cossin"], writes=["rt3"])
                if qk == 0:
                    S.op("dve", lambda e: e.tensor_tensor(out=rt[0], in0=rt[0], in1=rt[1], op=ALU.subtract),
                         reads=["rt0", "rt1"], writes=["rt0"])
                    S.op("dve", lambda e: e.tensor_tensor(out=rt[2], in0=rt[2], in1=rt[3], op=ALU.add),
                         reads=["rt2", "rt3"], writes=["rt2"])
                    S.op("act", lambda e: e.copy(out=qT[:, 0, :], in_=rt[0]), reads=["rt0"], writes=["qT0"])
                    S.op("act", lambda e: e.copy(out=qT[:, 1, :], in_=rt[2]), reads=["rt2"], writes=["qT1"])
                    S.op("dve", lambda e: e.tensor_tensor(out=qdT[:, 0, :], in0=rt[0], in1=rtab[:, h, 128:640], op=ALU.mult),
                         reads=["rt0", "rtab"], writes=["qdT0"])
                    S.op("dve", lambda e: e.tensor_tensor(out=qdT[:, 1, :], in0=rt[2], in1=rtab[:, h, 128:640], op=ALU.mult),
                         reads=["rt2", "rtab"], writes=["qdT1"])
                else:
                    S.op("pool", lambda e: e.tensor_tensor(out=kT[:, 0, :], in0=rt[0], in1=rt[1], op=ALU.subtract),
                         reads=["rt0", "rt1"], writes=["kT0"])
                    S.op("dve", lambda e: e.tensor_tensor(out=kT[:, 1, :], in0=rt[2], in1=rt[3], op=ALU.add),
                         reads=["rt2", "rt3"], writes=["kT1"])

            if h == H_RET - 1 and r == NRNG - 1:
                S.dma("pool", lambda e: e.dma_start(out=wout_b, in_=a_w_out[1024:2048, :].rearrange("(c p) n -> p c n", p=128)),
                      "wout1", writes=["wout1", "cossin"])

            def emit_vg(n):
                gn = r * 4 + n
                tk = slice(gn * 128, (gn + 1) * 128)
                vs = cnt["v"] % 3
                cnt["v"] += 1
                for kc in range(8):
                    S.op("pe", lambda e, kc=kc: e.matmul(PS(0), lhsT=xn[:, kc, tk], rhs=wv[:, kc, 0:512],
                                                         start=(kc == 0), stop=(kc == 7)),
                         reads=[wvk, f"xn{kc}_{r}"], writes=["ps0"])
                for kc in range(8):
                    S.op("pe", lambda e, kc=kc: e.matmul(PS(1), lhsT=xn[:, kc, tk], rhs=wv[:, kc, 512:1024],
                                                         start=(kc == 0), stop=(kc == 7)),
                         reads=[wgk, f"xn{kc}_{r}"], writes=["ps1"])
                if gn == 15 and h + 2 < H_RET:
                    load_wvg(h + 2)
                if gn == 15 and h == H_RET - 2:
                    S.dma("pool", lambda e: e.dma_start(out=wout_a, in_=a_w_out[0:1024, :].rearrange("(c p) n -> p c n", p=128)),
                          "wout0", writes=["wout0", f"wvg_v{h % 2}", f"wvg_g{h % 2}"])
                S.op("act", lambda e: e.copy(out=vtok[vs], in_=PS(0)), reads=["ps0"], writes=[f"vtok{vs}"])
                S.op("act", lambda e: e.activation(out=sgt[vs], in_=PS(1), func=AF.Silu),
                     reads=["ps1"], writes=[f"sgt{vs}"])
                return vs

            vs_next = emit_vg(0)
            ptr = PS(6).bitcast(BF16)
            for n in range(4):
                for dc in range(2):
                    S.op("pe", lambda e, n=n, dc=dc: e.transpose(ptr[:, n * 256 + dc * 128:n * 256 + (dc + 1) * 128],
                                                                 kT[:, dc, n * 128:(n + 1) * 128], ident_bf),
                         reads=[f"kT{dc}", "cmat"], writes=["ps6"])
            S.op("act", lambda e: e.mul(out=ktok.rearrange("p a b -> p (a b)"), in_=ptr, mul=small[:, KDEC0 + h:KDEC0 + h + 1]),
                 reads=["ps6", "small"], writes=["ktok"])
            for n in range(4):
                gn = r * 4 + n
                ck = slice(n * 128, (n + 1) * 128)
                vs = vs_next
                ss = cnt["sT"] % 2
                cnt["sT"] += 1
                for dc in range(2):
                    S.op("pe", lambda e, dc=dc: e.matmul(PS(5, 128), lhsT=kT[:, dc, ck], rhs=qT[:, dc, ck],
                                                         start=(dc == 0), stop=(dc == 1)),
                         reads=[f"kT{dc}", f"qT{dc}"], writes=["ps5"])
                S.op("dve", lambda e: e.tensor_tensor(out=sTb[ss], in0=PS(5, 128), in1=rtab[:, h, 0:128], op=ALU.mult),
                     reads=["ps5", "rtab"], writes=[f"sTb{ss}"])
                if pend_tr["f"] is not None:
                    pend_tr["f"]()
                    pend_tr["f"] = None
                sb_prev = (cnt["Sb"] - 1) % 2
                S.op("pe", lambda e: e.matmul(PS(4), lhsT=sTb[ss], rhs=vtok[vs], start=True, stop=(gn == 0)),
                     reads=[f"sTb{ss}", f"vtok{vs}"], writes=["ps4"])
                if gn > 0:
                    for dc in range(2):
                        S.op("pe", lambda e, dc=dc: e.matmul(PS(4), lhsT=qdT[:, dc, ck], rhs=Sb[sb_prev][:, dc, :],
                                                             start=False, stop=(dc == 1)),
                             reads=[f"qdT{dc}", f"Sb{sb_prev}_{dc}"], writes=["ps4"])
                ys = cnt["y"] % 2
                cnt["y"] += 1
                S.op("act", lambda e: e.activation(out=sqj, in_=PS(4), func=AF.Square, accum_out=ssr[:, 0:1]),
                     reads=["ps4"], writes=["sqj", "ssr0"])
                S.op("dve", lambda e: e.tensor_scalar(out=ssr[:, 1:2], in0=ssr[:, 0:1], scalar1=1.0 / 512, scalar2=EPS,
                                                      op0=ALU.mult, op1=ALU.add),
                     reads=["ssr0"], writes=["ssr1"])
                S.op("pool", lambda e: e.tensor_tensor(out=ssr[:, 2:3], in0=ssr[:, 1:2], in1=neghalf[:, 0:1], op=ALU.pow),
                     reads=["ssr1", "neghalf"], writes=["ssr2"])
                if gn < 15:
                    sb_new = cnt["Sb"] % 2
                    cnt["Sb"] += 1
                    for dc in range(2):
                        S.op("pe", lambda e, dc=dc: e.matmul(PS(2 + dc), lhsT=ktok[:, n, dc * 128:(dc + 1) * 128],
                                                             rhs=vtok[vs], start=True, stop=True),
                             reads=["ktok", f"vtok{vs}"], writes=[f"ps{2 + dc}"])
                        if gn == 0:
                            S.op("dve", lambda e, dc=dc: e.tensor_copy(out=Sf[:, dc, :], in_=PS(2 + dc)),
                                 reads=[f"ps{2 + dc}"], writes=[f"Sf{dc}"])
                        else:
                            S.op("dve", lambda e, dc=dc: e.scalar_tensor_tensor(out=Sf[:, dc, :], in0=Sf[:, dc, :], scalar=gC,
                                                                                in1=PS(2 + dc), op0=ALU.mult, op1=ALU.add),
                                 reads=[f"ps{2 + dc}", f"Sf{dc}"], writes=[f"Sf{dc}"])
                        S.op("act", lambda e, dc=dc: e.copy(out=Sb[sb_new][:, dc, :], in_=Sf[:, dc, :]),
                             reads=[f"Sf{dc}"], writes=[f"Sb{sb_new}_{dc}"])
                S.op("dve", lambda e: e.scalar_tensor_tensor(out=ybf[ys], in0=PS(4), scalar=ssr[:, 2:3], in1=sgt[vs],
                                                             op0=ALU.mult, op1=ALU.mult),
                     reads=["ps4", "ssr2", f"sgt{vs}"], writes=[f"ybf{ys}"])
                if n < 3:
                    vs_next = emit_vg(n + 1)
                if h == 0 and n == 1 and r + 1 < NRNG:
                    a0_range(r + 1)

                def tr_fn(ys=ys, r=r, h=h, ck=ck):
                    pyT = PS(7).bitcast(BF16)
                    for ec in range(4):
                        S.op("pe", lambda e, ec=ec: e.transpose(pyT[:, ec * 128:(ec + 1) * 128],
                                                                ybf[ys][:, ec * 128:(ec + 1) * 128], ident_bf),
                             reads=[f"ybf{ys}", "cmat"], writes=["ps7"])
                    yT_r = carve(BIG + r * 16 * KB, [16, 512], BF16)
                    S.op("act", lambda e: e.copy(out=yT_r[:, h * 4:(h + 1) * 4, ck],
                                                 in_=pyT[:, 0:512].rearrange("p (a b) -> p a b", a=4)),
                         reads=["ps7"], writes=[f"big{r}"])
                pend_tr["f"] = tr_fn
    pend_tr["f"]()
    S.barrier()

    hbufs = [carve(MISC, [8, 512], F32),
             carve(O + 24 * KB, [8, 512], F32)]
    if stop_after != "A":
        for fc in range(2):
            S.dma("pool", lambda e, fc=fc: e.dma_start(out=carve(O + 72 * KB + fc * 4 * KB, [2048], BF16), in_=w_up_r[0][fc]),
                  f"wu{fc}", writes=[f"wu{fc}"])
    xs2 = carve(WVG + (1 - (H_RET - 2) % 2) * 16 * KB, [8, 512], F32)
    def wout(k, dc):
        return (wout_a if k < 8 else wout_b)[:, k % 8, dc * 128:(dc + 1) * 128]

    for r in range(NRNG):
        yT_r = carve(BIG + r * 16 * KB, [16, 512], BF16)
        S.dma("sp", lambda e, r=r: e.dma_start(out=xs2, in_=xT_v[:, :, r * 512:(r + 1) * 512]), "xs2",
              writes=[f"xs2_{c}" for c in range(8)])
        proj_post(r, 16, wout, lambda k: yT_r[:, k, :],
                  lambda k: [f"wout{k // 8}", f"big{r}"], hbufs[r % 2], f"hb{r % 2}_", 1, sqa, rstd_a[r % 2], f"rstda{r % 2}",
                  lambda c: xs2[:, c, :], lambda c: [f"xs2_{c}"], lambda c: [f"big{r}"])
        if r == 2 and stop_after != "A":
            ffn_prenorm0(0)
    if stop_after == "A":
        S.barrier()
    else:
        S.soft_barrier()

    early_out = {"done": False}

    def finish():
        for r in range(2 if early_out["done"] else 0, NRNG):
            S.dma("sp", lambda e, r=r: e.dma_start(out=outT_v[:, :, r * 512:(r + 1) * 512], in_=xT_r(r)), "outd",
                  reads=[f"big{r}"])
        S.final_wait("sp")
        S.emit()
        return nc, S

    if stop_after == "A":
        return finish()

    def ffn(l, pre_done=False, wu_done=0, wu_alias=None):
        XNH = O
        TMP = O + 16 * KB
        MT = O + 28 * KB
        UP = O + 72 * KB
        NWU = 3
        NWD = 3
        xnh = carve(XNH, [8, 1024], BF16)
        sqf = [carve(TMP + i * KB, [512], BF16) for i in range(3)]
        rstd_f = [carve(TMP + 3 * KB + i * 2 * KB, [512], F32) for i in range(2)]
        halo = carve(TMP + 7 * KB, [44, 2], F32)
        rstd_p = [carve(TMP + 8 * KB + i * 2 * KB, [512], F32) for i in range(2)]
        mT = carve(MT, [22, 1024], BF16)
        wu = [carve(UP + i * 4 * KB, [8, 256], BF16) for i in range(NWU)]
        WDC = UP + NWU * 4 * KB
        wdc = [carve(WDC + i * 5632, [22, 128], BF16) for i in range(NWD)]
        HS = WDC + NWD * 5632
        hs = [[carve(HS + (s * 2 + j) * 4128, [1032], F32) for j in range(2)] for s in range(2)]
        AC = HS + 4 * 4128
        acc = [[carve(AC + (s * 2 + j) * 4 * KB, [1024], F32) for j in range(2)] for s in range(2)]
        SG = AC + 16 * KB
        sgf = [carve(SG + s * 2 * KB, [1024], BF16) for s in range(2)]
        assert SG + 4 * KB <= ARENA
        hb2 = [carve(HS + i * 16 * KB, [8, 512], F32) for i in range(2)]

        def hb_overlap(off, size):
            ks = []
            for rr_ in range(2):
                for dc_ in range(8):
                    a0 = rr_ * 16384 + dc_ * 2048
                    if off < a0 + 2048 and off + size > a0:
                        ks.append(f"hb{rr_}_{dc_}")
            return ks

        hs_alias = [[hb_overlap((s_ * 2 + j_) * 4128, 4128) for j_ in range(2)] for s_ in range(2)]
        acc_alias = [[hb_overlap(16512 + (s_ * 2 + j_) * 4096, 4096) for j_ in range(2)] for s_ in range(2)]
        gi_pre, gi_post = 5 + l, 7 + l

        def cw(j, fc):
            o = CONV0 + (l * 4 + j) * 44 + fc
            return small[:, o:o + 1]

        first_load = {"n": 0}

        def load_wu(fc):
            ws = fc % NWU
            extra = []
            if wu_alias is not None and first_load["n"] < NWU:
                extra = wu_alias[ws]
            first_load["n"] += 1
            S.dma("pool", lambda e: e.dma_start(out=wu[ws].rearrange("p a b -> p (a b)"), in_=w_up_r[l][fc]),
                  f"wu{ws}", writes=[f"wu{ws}"] + extra)

        def load_wdc(dc):
            sl = dc % NWD
            S.dma("pool", lambda e: e.dma_start(out=wdc[sl].rearrange("p a b -> p (a b)"), in_=w_down_r[l][dc]),
                  f"wdc{sl}", writes=[f"wdc{sl}"])

        def prenorm(hf):
            for rr in range(2):
                r = hf * 2 + rr
                xr = xT_r(r)
                stats_rstd(lambda c: xr[:, c, :], lambda c: [f"big{r}"], sqf, rstd_f[rr], f"rstdf{rr}", bank=5)
                apply_norm(lambda c: xr[:, c, :], lambda c: [f"big{r}"], gi_pre, rstd_f[rr], f"rstdf{rr}",
                           lambda c: xnh[:, c, rr * 512:(rr + 1) * 512], lambda c: [f"xnh{c}_{rr}"])

        for fc in range(wu_done, NWU - 1):
            load_wu(fc)
        if not pre_done:
            prenorm(0)
        for hf in range(2):
            if hf == 0:
                for st in range(2):
                    for j in range(2):
                        S.op("dve", lambda e: e.memset(hs[st][j][:, 0:2], 0.0), writes=[f"hs{st}{j}h"])
            for fc in range(22):
                ws = fc % NWU
                st = fc % 2
                if fc + NWU - 1 < 22:
                    load_wu(fc + NWU - 1)
                if fc == 16:
                    load_wdc(0)
                if fc == 18:
                    load_wdc(1)
                if fc == 20:
                    load_wdc(2)
                for j in range(2):
                    bb = st * 4 + j * 2
                    for rr in range(2):
                        for kc in range(8):
                            S.op("pe", lambda e, kc=kc, rr=rr: e.matmul(
                                PS(bb + rr), lhsT=wu[ws][:, kc, j * 128:(j + 1) * 128], rhs=xnh[:, kc, rr * 512:(rr + 1) * 512],
                                start=(kc == 0), stop=(kc == 7)),
                                reads=[f"wu{ws}", f"xnh{kc}_{rr}"], writes=[f"ps{bb + rr}"])
                    hsj = hs[st][j]
                    acj = acc[st][j]
                    hk = f"hs{st}{j}"
                    ak = f"acc{st}{j}"
                    fcj = fc + 22 * j
                    hsa = hs_alias[st][j] if hf == 1 else []
                    aca = acc_alias[st][j] if hf == 1 else []
                    if hf == 1:
                        S.op("act", lambda e: e.copy(out=hsj[:, 0:2], in_=halo[:, fcj, :]),
                             reads=[f"halo{fcj}"], writes=[hk + "h"] + hsa)
                    S.op("act", lambda e: e.copy(out=hsj[:, 2:1026], in_=psum[:, bb * 512:bb * 512 + 1024]),
                         reads=[f"ps{bb}", f"ps{bb + 1}"], writes=[hk] + hsa)
                    if hf == 0:
                        S.op("act", lambda e: e.copy(out=halo[:, fcj, :], in_=psum[:, bb * 512 + 1022:bb * 512 + 1024]),
                             reads=[f"ps{bb + 1}"], writes=[f"halo{fcj}"])
                    S.op("act", lambda e: e.activation(
                        out=acj, in_=hsj[:, 0:1024], func=AF.Identity, bias=cw(3, fcj), scale=cw(0, fcj)),
                        reads=[hk, hk + "h", "small"], writes=[ak] + aca)
                    S.op("dve", lambda e: e.scalar_tensor_tensor(
                        out=acj, in0=hsj[:, 1:1025], scalar=cw(1, fcj), in1=acj, op0=ALU.mult, op1=ALU.add),
                        reads=[hk, hk + "h", ak, "small"], writes=[ak])
                    S.op("dve", lambda e: e.scalar_tensor_tensor(
                        out=acj, in0=hsj[:, 2:1026], scalar=cw(2, fcj), in1=acj, op0=ALU.mult, op1=ALU.add),
                        reads=[hk, ak, "small"], writes=[ak])
                S.op("act", lambda e: e.activation(out=sgf[st], in_=acc[st][0], func=AF.Silu),
                     reads=[f"acc{st}0"], writes=[f"sgf{st}"])
                S.op("pool", lambda e: e.tensor_tensor(out=mT[:, fc, :], in0=sgf[st], in1=acc[st][1], op=ALU.mult),
                     reads=[f"sgf{st}", f"acc{st}1"], writes=[f"mT{fc}"])
            S.soft_barrier()
            if hf == 0:
                for fc in range(NWU - 1):
                    load_wu(fc)
            groups = [(dc, rr) for dc in range(8) for rr in range(2)]
            pend = None

            def mm_range(gidx, k0, k1):
                dc, rr = groups[gidx]
                b = gidx % 5
                sl = dc % NWD
                for k in range(k0, k1):
                    S.op("pe", lambda e, k=k: e.matmul(PS(b), lhsT=wdc[sl][:, k, :], rhs=mT[:, k, rr * 512:(rr + 1) * 512],
                                                      start=(k == 0), stop=(k == 21)),
                         reads=[f"wdc{sl}", f"mT{k}"], writes=[f"ps{b}"])

            def evac(gidx):
                nonlocal pend
                dc, rr = groups[gidx]
                b = gidx % 5
                if pend is not None:
                    pend()
                S.op("act", lambda e: e.mul(out=hb2[rr][:, dc, :], in_=PS(b), mul=gain(gi_post, dc)),
                     reads=[f"ps{b}", "small"], writes=[f"hb{rr}_{dc}"])
                j = rot["sq"] % len(sqf)
                rot["sq"] += 1
                S.op("act", lambda e: e.activation(out=sqf[j], in_=PS(b), func=AF.Square),
                     reads=[f"ps{b}"], writes=[f"sqt{j}"])

                def pend(j=j, rr=rr, dc=dc):
                    S.op("pe", lambda e: e.matmul(PS(6 + rr), lhsT=ones_bf, rhs=sqf[j], start=(dc == 0), stop=(dc == 7)),
                         reads=[f"sqt{j}", "cmat"], writes=[f"ps{6 + rr}"])

            for gidx in range(3):
                mm_range(gidx, 0, 20)
            for gidx in range(16):
                dc, rr = groups[gidx]
                if gidx < 3:
                    mm_range(gidx, 20, 22)
                else:
                    mm_range(gidx, 0, 22)
                evac(gidx)
                if rr == 1:
                    if dc + NWD < 8:
                        load_wdc(dc + NWD)
                    if hf == 0 and dc == 2:
                        pend()
                        pend = None
                        prenorm(1)
            pend()
            for rr in range(2):
                S.op("act", lambda e: e.activation(out=rstd_p[rr], in_=PS(6 + rr), func=AF.Sqrt, bias=epsb, scale=1.0),
                     reads=[f"ps{6 + rr}", "small"], writes=[f"rstdp{rr}"])
            for rr in range(2):
                S.op("dve", lambda e: e.reciprocal(out=rstd_p[rr], in_=rstd_p[rr]), reads=[f"rstdp{rr}"], writes=[f"rstdp{rr}"])
            for c in range(8):
                for rr in range(2):
                    r = hf * 2 + rr
                    xr = xT_r(r)
                    S.op("dve", lambda e: e.tensor_tensor(out=hb2[rr][:, c, :], in0=hb2[rr][:, c, :], in1=rstd_p[rr], op=ALU.mult),
                         reads=[f"hb{rr}_{c}", f"rstdp{rr}"], writes=[f"hb{rr}_{c}"])
                    eng = "dve"
                    S.op(eng, lambda e: e.tensor_tensor(out=xr[:, c, :], in0=hb2[rr][:, c, :], in1=xr[:, c, :], op=ALU.add),
                         reads=[f"hb{rr}_{c}", f"big{r}"], writes=[f"big{r}"])
            if hf == 0:
                if l == 1 and stop_after == "F1":
                    for r in range(2):
                        S.dma("sp", lambda e, r=r: e.dma_start(out=outT_v[:, :, r * 512:(r + 1) * 512], in_=xT_r(r)), "outd",
                              reads=[f"big{r}"])
                    early_out["done"] = True
        S.barrier()

    ffn(0, pre_done=True, wu_done=2)
    if stop_after == "F0":
        return finish()

    KT = O
    VT = O + 32 * KB
    R1 = O + 65 * KB
    R2 = O + 97 * KB
    SM = O + 129 * KB
    kTb = carve(KT, [8, T], BF16)
    vaug = carve(VT, [16, 8, 129], BF16)
    xnb = carve(R1, [8, T], BF16)
    oT = carve(R1, [8, T], BF16)
    wk = [carve(R2 + i * 2 * KB, [8, 128], BF16) for i in range(3)]
    wvb = carve(R2 + 6 * KB, [8, 1024], BF16)
    rstd_b = carve(R2 + 22 * KB, [T], F32)
    qTb = carve(R2, [8, T], BF16)
    wbo_a = carve(R2, [4, 1024], BF16)
    wbo_b = carve(R2 + 24 * KB, [4, 1024], BF16)
    hbb = carve(R2 + 8 * KB, [8, 512], F32)
    so = [SM]

    def sm(dims, dt):
        nb = int(np.prod(dims)) * (4 if dt == F32 else 2)
        nb = (nb + 31) // 32 * 32
        v = carve(so[0], dims, dt)
        so[0] += nb
        return v

    so[0] += 3 * KB
    wq = [sm([8, 128], BF16) for _ in range(2)]
    sqb = [carve(R2 + 30 * KB + i * KB, [512], BF16) for i in range(2)]
    of32 = [sm([128], F32) for _ in range(2)]
    onb = [sm([128], BF16) for _ in range(2)]
    rcp = sm([8], F32)
    ssb = sm([8], F32)
    sqjb = sm([128], BF16)
    sqb4 = [carve(SM + i * KB, [512], BF16) for i in range(2)]
    rstd_bo = carve(SM + 3 * KB, [512], F32)
    assert so[0] <= ARENA, so[0]

    lp = lamp.rearrange("p (a b) -> p a b", a=4)
    S.op("dve", lambda e: e.tensor_tensor(out=lamp[:, 0:64], in0=lp[:, 0, :], in1=lp[:, 1, :], op=ALU.mult),
         reads=["lamp"], writes=["lamp"])
    S.op("dve", lambda e: e.tensor_tensor(out=lamp[:, 128:192], in0=lp[:, 2, :], in1=lp[:, 3, :], op=ALU.mult),
         reads=["lamp"], writes=["lamp"])
    S.op("dve", lambda e: e.tensor_reduce(out=lamv[:, 0:1], in_=lamp[:, 0:64], axis=AX.X, op=ALU.add), reads=["lamp"], writes=["lamv0"])
    S.op("dve", lambda e: e.tensor_reduce(out=lamv[:, 1:2], in_=lamp[:, 128:192], axis=AX.X, op=ALU.add), reads=["lamp"], writes=["lamv1"])
    S.op("act", lambda e: e.activation(out=lamv[:, 2:4], in_=lamv[:, 0:2], func=AF.Exp), reads=["lamv0", "lamv1"], writes=["lamv2"])
    S.op("dve", lambda e: e.tensor_tensor(out=lamv[:, 4:5], in0=lamv[:, 3:4], in1=lamv[:, 2:3], op=ALU.subtract),
         reads=["lamv2"], writes=["lamv4"])
    S.op("dve", lambda e: e.tensor_scalar(out=lamv[:, 5:6], in0=lamv[:, 4:5], scalar1=-LAMBDA_INIT, scalar2=None, op0=ALU.add),
         reads=["lamv4"], writes=["neglam"])
    S.op("dve", lambda e: e.tensor_scalar(out=lamv[:, 6:7], in0=small[:, SUBLN0:SUBLN0 + 1], scalar1=1.0 - LAMBDA_INIT,
                                          scalar2=None, op0=ALU.mult),
         reads=["small"], writes=["gsub"])
    neglam = lamv[:, 5:6]
    gsub = lamv[:, 6:7]

    S.dma("pool", lambda e: e.dma_start(out=wvb, in_=w_kv[:, 1024:2048].rearrange("(c p) n -> p c n", p=128)), "wvb", writes=["wvb"])
    for r in range(NRNG):
        xr = xT_r(r)
        stats_rstd(lambda c: xr[:, c, :], lambda c: [f"big{r}"], sqb, rstd_b[:, r * 512:(r + 1) * 512], f"rstdb{r}")
        apply_norm(lambda c: xr[:, c, :], lambda c: [f"big{r}"], 2, rstd_b[:, r * 512:(r + 1) * 512], f"rstdb{r}",
                   lambda c: xnb[:, c, r * 512:(r + 1) * 512], lambda c: [f"xnb{c}_{r}"])
    bk = {"i": 0}

    def nbank():
        bk["i"] += 1
        return bk["i"] % 6

    for c in range(8):
        ws = c % 3
        S.dma("pool", lambda e, ws=ws, c=c: e.dma_start(out=wk[ws], in_=w_kv[:, c * 128:(c + 1) * 128].rearrange("(c p) n -> p c n", p=128)),
              f"wk{ws}", writes=[f"wk{ws}"])
        for r in range(NRNG):
            b = nbank()
            for kc in range(8):
                S.op("pe", lambda e, kc=kc, b=b, ws=ws, r=r: e.matmul(PS(b), lhsT=wk[ws][:, kc, :], rhs=xnb[:, kc, r * 512:(r + 1) * 512],
                                                                    start=(kc == 0), stop=(kc == 7)),
                     reads=[f"wk{ws}", f"xnb{kc}_{r}"], writes=[f"ps{b}"])
            S.op("act", lambda e, b=b, c=c, r=r: e.copy(out=kTb[:, c, r * 512:(r + 1) * 512], in_=PS(b)),
                 reads=[f"ps{b}"], writes=[f"kTb{c}"])
    for tt in range(16):
        S.op("pool", lambda e, tt=tt: e.memset(vaug[:, tt, :, 128:129], 1.0), writes=[f"vone{tt}"])
        for hh in range(2):
            b = nbank()
            for kc in range(8):
                S.op("pe", lambda e, kc=kc, b=b, tt=tt, hh=hh: e.matmul(PS(b), lhsT=xnb[:, kc, tt * 128:(tt + 1) * 128],
                                                                      rhs=wvb[:, kc, hh * 512:(hh + 1) * 512],
                                                                      start=(kc == 0), stop=(kc == 7)),
                     reads=["wvb", f"xnb{kc}_{tt // 4}"], writes=[f"ps{b}"])
            S.op("dve", lambda e, b=b, tt=tt, hh=hh: e.tensor_copy(out=vaug[:, tt, hh * 4:(hh + 1) * 4, 0:128],
                                                                   in_=PS(b).rearrange("p (a b) -> p a b", a=4)),
                 reads=[f"ps{b}"], writes=[f"vaug{tt}_{hh}"])
        if tt % 4 == 3:
            r = tt // 4
            xr = xT_r(r)
            apply_norm(lambda c: xr[:, c, :], lambda c: [f"big{r}"], 3, rstd_b[:, r * 512:(r + 1) * 512], f"rstdb{r}",
                       lambda c: xnb[:, c, r * 512:(r + 1) * 512], lambda c: [f"xnb{c}_{r}"])
    S.barrier()
    if stop_after == "B1":
        return finish()
    for c in range(8):
        ws = c % 2
        S.dma("pool", lambda e, ws=ws, c=c: e.dma_start(out=wq[ws], in_=b_w_q[:, c * 128:(c + 1) * 128].rearrange("(c p) n -> p c n", p=128)),
              f"wq{ws}", writes=[f"wq{ws}"])
        for r in range(NRNG):
            b = nbank()
            for kc in range(8):
                S.op("pe", lambda e, kc=kc, b=b, ws=ws, r=r: e.matmul(PS(b), lhsT=wq[ws][:, kc, :], rhs=xnb[:, kc, r * 512:(r + 1) * 512],
                                                                    start=(kc == 0), stop=(kc == 7)),
                     reads=[f"wq{ws}", f"xnb{kc}_{r}"], writes=[f"ps{b}"])
            S.op("act", lambda e, b=b, c=c, r=r: e.mul(out=qTb[:, c, r * 512:(r + 1) * 512], in_=PS(b), mul=0.125),
                 reads=[f"ps{b}"], writes=[f"qTb{c}"])
    S.barrier()
    if stop_after == "B2":
        return finish()
    pT3 = [carve(SM + i * KB, [512], BF16) for i in range(3)]
    accsb = carve(SM + 3 * KB, [4, 129], F32)
    STB = [0, 1, 3]
    ptrb = PS(2).bitcast(BF16)

    QREG = {0: ("oT", 6), 1: ("oT", 4), 2: ("qTb", 0), 3: ("oT", 6), 4: ("qTb", 0), 5: ("qTb", 2), 6: ("qTb", 0), 7: ("qTb", 2)}

    def alias_keys(h):
        nm, k = QREG[h]
        return [f"{nm}{k}", f"{nm}{k + 1}"]

    def qblk(h):
        nm, k = QREG[h]
        src = oT[:, k:k + 2, :] if nm == "oT" else qTb[:, k:k + 2, :]
        return src.rearrange("p a b -> p (a b)")

    def build_qblk(h):
        flat = qblk(h)
        v4 = flat.rearrange("p (r m c) -> p r m c", r=16, m=2)
        ak = alias_keys(h)
        S.op("pool", lambda e: e.memset(flat, 0.0), writes=ak)
        S.op("dve", lambda e: e.tensor_copy(out=v4[0:64, :, 0, :], in_=qTb[0:64, h, :].rearrange("p (r c) -> p r c", r=16)),
             reads=[f"qTb{h}"], writes=ak)
        S.op("dve", lambda e: e.tensor_copy(out=v4[64:128, :, 1, :], in_=qTb[64:128, h, :].rearrange("p (r c) -> p r c", r=16)),
             reads=[f"qTb{h}"], writes=ak)

    def st_mm(h, qb, kb, slot, col0=0):
        if h == 0 and qb == 0 and kb == 0:
            build_qblk(0)
        if qb == 8 and kb == 0 and h + 1 < 8:
            build_qblk(h + 1)
        if qb == 8 and kb == 0 and h == 7:
            S.dma("pool", lambda e: e.dma_start(out=wbo_a, in_=b_w_out[0:512, :].rearrange("(c p) n -> p c n", p=128)),
                  "wboa", writes=["wboa", "qTb0", "qTb1"])
            S.dma("pool", lambda e: e.dma_start(out=wbo_b, in_=b_w_out[512:1024, :].rearrange("(c p) n -> p c n", p=128)),
                  "wbob", writes=["wbob", "qTb6", "qTb7"])
        qb_ = qblk(h)
        rk = alias_keys(h) + [f"kTb{h}"]
        diag = kb == qb
        S.op("pe", lambda e: e.matmul(PS(slot, 256, col0), lhsT=kTb[:, h, kb * 128:(kb + 1) * 128], rhs=qb_[:, qb * 256:(qb + 1) * 256],
                                      start=True, stop=not diag),
             reads=rk, writes=[f"ps{slot}"])
        if diag:
            for m in range(2):
                S.op("pe", lambda e, m=m: e.matmul(PS(slot, 128, col0 + m * 128), lhsT=ident_bf, rhs=maskneg_bf, start=False, stop=(m == 1)),
                     reads=["cmat"], writes=[f"ps{slot}"])

    ACC2 = SM + 3 * KB
    ONB3 = ACC2 + 4128
    onb3 = [carve(ONB3 + p * 1024, [4, 128], BF16) for p in range(2)]
    FSM = ONB3 + 2048
    rcp8 = carve(FSM, [8], F32)
    ss4 = carve(FSM + 32, [8], F32)
    assert FSM + 64 <= ARENA
    BREG = {0: ("oT", 2), 1: ("oT", 2), 2: ("oT", 4), 3: ("oT", 4), 4: ("oT", 6), 5: ("oT", 6), 6: ("qTb", 4), 7: ("qTb", 4)}

    def accbuf(h, g):
        if g % 2 == 0:
            return carve(ACC2, [8, 129], F32), ["acc8A"]
        nm, k = BREG[h]
        base = (R1 if nm == "oT" else R2) + k * 4096
        return carve(base, [8, 129], F32), ["acc8B", f"{nm}{k}", f"{nm}{k + 1}"]

    def fin_copy(h, qb):
        par = qb % 2
        i4 = qb % 4
        ap8, keys = accbuf(h, qb // 4)
        S.op("act", lambda e: e.copy(out=ap8[:, i4 * 2:(i4 + 1) * 2, :],
                                     in_=psum[:, (4 + 2 * par) * 512:(6 + 2 * par) * 512].rearrange("p (a b) -> p a b", a=2)[:, :, 0:129]),
             reads=[f"ps{4 + 2 * par}", f"ps{5 + 2 * par}"], writes=keys)

    def finalize1(h, P, slot):
        acc8, keys = accbuf(h, P)
        a4 = acc8.rearrange("p (q m) c -> p q m c", m=2)
        S.op("dve", lambda e: e.reciprocal(out=rcp8.rearrange("p (a b) -> p a b", b=1), in_=acc8[:, :, 128:129]),
             reads=keys, writes=["rcp8"])
        S.op("dve", lambda e: e.tensor_scalar(out=rcp8.rearrange("p (q m) -> p q m", m=2)[:, :, 1],
                                              in0=rcp8.rearrange("p (q m) -> p q m", m=2)[:, :, 1],
                                              scalar1=neglam, scalar2=None, op0=ALU.mult),
             reads=["rcp8", "neglam"], writes=["rcp8"])
        S.op("dve", lambda e: e.tensor_tensor(out=acc8[:, :, 0:128], in0=acc8[:, :, 0:128],
                                              in1=rcp8.rearrange("p (a b) -> p a b", b=1).to_broadcast([128, 8, 128]), op=ALU.mult),
             reads=keys + ["rcp8"], writes=keys)
        S.op("dve", lambda e: e.tensor_tensor(out=a4[:, :, 0, 0:128], in0=a4[:, :, 0, 0:128], in1=a4[:, :, 1, 0:128], op=ALU.add),
             reads=keys, writes=keys)
        S.op("dve", lambda e: e.tensor_tensor(out=a4[:, :, 1, 0:128], in0=a4[:, :, 0, 0:128], in1=a4[:, :, 0, 0:128], op=ALU.mult),
             reads=keys, writes=keys)
        S.op("dve", lambda e: e.tensor_reduce(out=ss4[:, 0:4], in_=a4[:, :, 1, 0:128], axis=AX.X, op=ALU.add),
             reads=keys, writes=["ss4a"])
        S.op("dve", lambda e: e.tensor_scalar(out=ss4[:, 4:8], in0=ss4[:, 0:4], scalar1=1.0 / 128, scalar2=EPS,
                                              op0=ALU.mult, op1=ALU.add),
             reads=["ss4a"], writes=["ss4b"])
        S.op("pool", lambda e: e.tensor_tensor(out=ss4[:, 0:4], in0=ss4[:, 4:8], in1=neghalf[:, 0:4], op=ALU.pow),
             reads=["ss4b", "neghalf"], writes=["ss4c", "ss4a"])
        S.op("dve", lambda e: e.tensor_tensor(out=onb3[slot], in0=a4[:, :, 0, 0:128],
                                              in1=ss4[:, 0:4].rearrange("p (a b) -> p a b", b=1).to_broadcast([128, 4, 128]), op=ALU.mult),
             reads=keys + ["ss4c"], writes=[f"onb{slot}"])

    def finalize2(h, P, slot):
        q0 = P * 512
        for i4 in range(4):
            S.op("pe", lambda e, i4=i4: e.transpose(ptrb[:, i4 * 128:(i4 + 1) * 128], onb3[slot][:, i4, :], ident_bf),
                 reads=[f"onb{slot}", "cmat"], writes=["ps2"])
        S.op("dve", lambda e: e.tensor_scalar(out=oT[:, h, q0:q0 + 512], in0=ptrb[:, 0:512], scalar1=gsub, scalar2=None, op0=ALU.mult),
             reads=["ps2", "gsub"], writes=[f"oT{h}"])

    pend = {"p": None, "n": 0}

    steps = [(h, qb, kb) for h in range(8) for qb in range(16) for kb in range(qb + 1)]
    pairs = [(steps[2 * i], steps[2 * i + 1]) for i in range(len(steps) // 2)]

    def st_pair(p, slot):
        st_mm(*pairs[p][0], slot, 0)
        st_mm(*pairs[p][1], slot, 256)

    st_pair(0, STB[0])
    st_pair(1, STB[1])
    for p, pr in enumerate(pairs):
        slot = STB[p % 3]
        sl = p % 3
        if p + 2 < len(pairs):
            st_pair(p + 2, STB[(p + 2) % 3])
        S.op("act", lambda e: e.activation(out=pT3[sl], in_=PS(slot), func=AF.Exp),
             reads=[f"ps{slot}"], writes=[f"pT{sl}"])
        for idx, (h, qb, kb) in enumerate(pr):
            c0 = idx * 256
            par = qb % 2
            for m in range(2):
                S.op("pe", lambda e, m=m: e.matmul(PS(4 + 2 * par + m, 129, 0), lhsT=pT3[sl][:, c0 + m * 128:c0 + (m + 1) * 128],
                                                  rhs=vaug[:, kb, h, :], start=(kb == 0), stop=(kb == qb)),
                     reads=[f"pT{sl}", f"vaug{kb}_{h // 4}", f"vone{kb}"], writes=[f"ps{4 + 2 * par + m}"])
            if kb == qb:
                fin_copy(h, qb)
                if qb % 4 == 3:
                    if pend["p"] is not None:
                        finalize2(*pend["p"])
                    slot2 = pend["n"] % 2
                    pend["n"] += 1
                    finalize1(h, qb // 4, slot2)
                    pend["p"] = (h, qb // 4, slot2)
    finalize2(*pend["p"])
    S.barrier()
    if stop_after == "B3":
        return finish()
    for r in range(NRNG):
        xr = xT_r(r)
        proj_post(r, 8, lambda k, dc: (wbo_a if k < 4 else wbo_b)[:, k % 4, dc * 128:(dc + 1) * 128],
                  lambda k: oT[:, k, r * 512:(r + 1) * 512],
                  lambda k: ["wboa" if k < 4 else "wbob", f"oT{k}"], hbb, "hbb", 4, sqb4, rstd_bo, "rstdbo",
                  lambda c: xr[:, c, :], lambda c: [f"big{r}"], lambda c: [f"big{r}"])
        if r == 2 and stop_after != "B":
            ffn_prenorm0(1)
    if stop_after == "B":
        S.barrier()
        return finish()
    S.soft_barrier()
    ffn(1, pre_done=True, wu_alias={0: ["oT1", "oT2"], 1: ["oT2", "oT3"], 2: ["oT3", "oT4"]})
    return finish()


def _const_tables():
    f32 = np.float32
    cm = np.zeros((128, 4, 128), f32)
    cm[:, 0, :] = 1.0 / 1024.0
    cm[:, 1, :] = np.eye(128, dtype=f32)
    kk = np.arange(128)
    cm[:, 2, :] = (kk[:, None] <= kk[None, :]).astype(f32)
    cm[:, 3, :] = -30000.0 * (kk[:, None] > kk[None, :]).astype(f32)
    half = 128
    inv = 1.0 / (10000.0 ** np.linspace(0.0, 1.0, half, dtype=np.float64))
    ang = np.arange(T, dtype=np.float64)[None, :] * inv[:, None]
    cs = np.stack([np.cos(ang), np.sin(ang)], axis=1).astype(f32)
    rt = np.zeros((128, 4, 640), f32)
    kdec = np.zeros((128, 4), f32)
    idx = np.arange(128, dtype=np.float64)
    for h in range(4):
        lg = np.log1p(-(2.0 ** (-5.0 - h)))
        rel = idx[None, :] - idx[:, None]
        m = np.where(rel >= 0, np.exp(lg * np.maximum(rel, 0.0)), 0.0) * (256.0 ** -0.5)
        rt[:, h, 0:128] = m
        qd = np.exp(lg * (idx + 1.0))
        rt[:, h, 128:640] = np.tile(qd, 4)[None, :]
        kdec[:, h] = np.exp(lg * (127.0 - idx)) * (256.0 ** -0.5)
    return cm.reshape(128, 512), cs.reshape(128, 2 * T), rt.reshape(128, 4 * 640), kdec


def _lay_up(w):
    return np.ascontiguousarray(w.reshape(8, 128, 2, 22, 128).transpose(3, 1, 0, 2, 4).reshape(22, 128, 2048))


def _lay_down(w):
    return np.ascontiguousarray(w.reshape(22, 128, 8, 128).transpose(2, 1, 0, 3).reshape(8, 128, 2816))


_CACHE = {}


def _get_nc(stop_after="F1"):
    if stop_after not in _CACHE:
        _CACHE[stop_after] = build(stop_after)[0]
    return _CACHE[stop_after]


def make_in_maps(inputs):
    f32 = np.float32
    g = lambda k: np.asarray(inputs[k], dtype=f32)
    cm, cs, rt, kdec = _const_tables()
    small = np.zeros((128, 512), f32)
    gl = [g("a_norm_pre")[0], g("a_norm_post")[0], g("kv_norm"), g("b_norm_pre")[0], g("b_norm_post")[0],
          g("ffn_norm_pre")[0], g("ffn_norm_pre")[1], g("ffn_norm_post")[0], g("ffn_norm_post")[1]]
    for i, v in enumerate(gl):
        small[:, i * 8:(i + 1) * 8] = v.reshape(8, 128).T
    cw, cb = g("ffn_conv_w"), g("ffn_conv_b")
    for l in range(2):
        for j in range(3):
            small[:, 72 + (l * 4 + j) * 44:72 + (l * 4 + j + 1) * 44] = cw[l, j].reshape(44, 128).T
        small[:, 72 + (l * 4 + 3) * 44:72 + (l * 4 + 4) * 44] = cb[l].reshape(44, 128).T
    small[:, 424:428] = kdec
    small[:, 428] = g("b_subln")[0]
    small[:, 464] = EPS
    lamp = np.ascontiguousarray(np.broadcast_to(g("b_lambda")[0].reshape(1, 256), (128, 256)))
    shared = {
        "a_w_in": np.ascontiguousarray(g("a_w_in")[0]), "a_w_out": np.ascontiguousarray(g("a_w_out")[0]),
        "w_kv": np.ascontiguousarray(g("w_kv")), "b_w_q": np.ascontiguousarray(g("b_w_q")[0]),
        "b_w_out": np.ascontiguousarray(g("b_w_out")[0]),
        "w_up0": _lay_up(g("ffn_w_up")[0]), "w_up1": _lay_up(g("ffn_w_up")[1]),
        "w_down0": _lay_down(g("ffn_w_down")[0]), "w_down1": _lay_down(g("ffn_w_down")[1]),
        "cmat": cm, "cossin": cs, "rtab": rt, "small": small, "lamp": lamp,
    }
    x = g("x")
    maps = []
    for b in range(x.shape[0]):
        d = dict(shared)
        d["xT"] = np.ascontiguousarray(x[b].T)
        maps.append(d)
    return maps


def kernel(**inputs):
    nc = _get_nc("F1")
    in_maps = make_in_maps(inputs)
    res = run_bass_kernel_spmd(nc, in_maps, core_ids=list(range(len(in_maps))))
    out = np.stack([np.ascontiguousarray(r["outT"].T) for r in res.results], axis=0)
    return out.astype(np.float32)
```

```python
import math
from contextlib import ExitStack
import numpy as np
import concourse.bass as bass
import concourse.mybir as mybir
from concourse.bass_utils import run_bass_kernel_spmd

F32 = mybir.dt.float32
BF16 = mybir.dt.bfloat16
U8 = mybir.dt.uint8
ALU = mybir.AluOpType
AF = mybir.ActivationFunctionType
AX = mybir.AxisListType

ENGS = ("pe", "act", "dve", "pool", "sp")
KB = 1024
EPS = 1e-6
T = 2048
NRNG = 4
H_RET = 4
LAMBDA_INIT = 0.8 - 0.6 * math.exp(-0.3 * 1)


class _Rec:
    def __init__(self):
        self.call = None

    def __getattr__(self, name):
        def f(*a, **k):
            self.call = (name, a, k)
            return self
        return f


def _capture(fn):
    r = _Rec()
    fn(r)
    assert r.call is not None
    return r.call


class Sched:
    def __init__(self, nc):
        self.nc = nc
        self.ops = {e: [] for e in ENGS}
        self.count = {e: 0 for e in ENGS}
        self.actors = list(ENGS)
        self.sems = {}
        self.clock = {e: {} for e in ENGS}
        self.ev_clock = {}
        self.res = {}
        self.dma_count = {}
        self.n_waits = 0
        self.n_ops = 0

    def _need(self, eng, ev, waits):
        a, c = ev
        if self.clock[eng].get(a, 0) >= c:
            return
        waits[a] = max(waits.get(a, 0), c)

    def _deps(self, eng, reads, writes, strict=False):
        waits = {}
        for k in reads:
            st = self.res.get(k)
            if st is None:
                continue
            if st[0] is not None:
                self._need(eng, st[0], waits)
        for k in writes:
            st = self.res.get(k)
            if st is None:
                continue
            w = st[0]
            if w is not None and (strict or w[0] != eng or eng != "pe"):
                self._need(eng, w, waits)
            for r in st[1]:
                if strict or r[0] != eng or eng != "pe":
                    self._need(eng, r, waits)
        return waits

    def _apply_waits(self, eng, waits):
        for a, c in waits.items():
            if self.clock[eng].get(a, 0) >= c:
                continue
            self.ops[eng].append(("wait", a, c))
            self.n_waits += 1
            ck = self.ev_clock.get((a, c))
            if ck:
                mine = self.clock[eng]
                for k2, v2 in ck.items():
                    if mine.get(k2, 0) < v2:
                        mine[k2] = v2
            if self.clock[eng].get(a, 0) < c:
                self.clock[eng][a] = c

    def _record(self, ev, reads, writes):
        for k in reads:
            st = self.res.setdefault(k, [None, []])
            st[1].append(ev)
        for k in writes:
            self.res[k] = [ev, []]

    def op(self, eng, fn, reads=(), writes=()):
        self._apply_waits(eng, self._deps(eng, reads, writes))
        self.count[eng] += 1
        ev = (eng, self.count[eng])
        self.ops[eng].append(("op", _capture(fn), eng, 1))
        snap = dict(self.clock[eng])
        snap[eng] = ev[1]
        self.ev_clock[ev] = snap
        self._record(ev, reads, writes)
        self.n_ops += 1
        return ev

    def dma(self, queue, fn, actor, reads=(), writes=()):
        if actor not in self.dma_count:
            self.dma_count[actor] = 0
            self.actors.append(actor)
        self._apply_waits(queue, self._deps(queue, reads, writes, strict=True))
        self.dma_count[actor] += 16
        ev = (actor, self.dma_count[actor])
        self.ops[queue].append(("op", _capture(fn), actor, 16))
        snap = dict(self.clock[queue])
        snap[actor] = ev[1]
        self.ev_clock[ev] = snap
        self._record(ev, reads, writes)
        return ev

    def _targets(self):
        t = {e: self.count[e] for e in ENGS if self.count[e] > 0}
        for a, c in self.dma_count.items():
            if c > 0:
                t[a] = c
        return t

    def barrier(self):
        t = self._targets()
        for e in ENGS:
            self._apply_waits(e, dict(t))
        self.res = {}

    def soft_barrier(self, engs=("act", "dve", "pool")):
        t = {e: self.count[e] for e in engs if self.count[e] > 0}
        for e in engs:
            w = dict(t)
            w.pop(e, None)
            self._apply_waits(e, w)

    def final_wait(self, eng="sp"):
        t = self._targets()
        t.pop(eng, None)
        self._apply_waits(eng, t)

    def emit(self):
        nc = self.nc
        with ExitStack() as st:
            for a in self.actors:
                self.sems[a] = st.enter_context(nc.semaphore("s_" + a))
            block = st.enter_context(nc.Block())
            engmap = {"pe": block.tensor, "act": block.scalar, "dve": block.vector,
                      "pool": block.gpsimd, "sp": block.sync}

            def make(ename):
                lst = self.ops[ename]

                def body(eng):
                    for item in lst:
                        if item[0] == "wait":
                            eng.wait_ge(self.sems[item[1]], item[2])
                        else:
                            _, call, actor, inc = item
                            getattr(eng, call[0])(*call[1], **call[2]).then_inc(self.sems[actor], inc)
                return body

            for ename in ENGS:
                if self.ops[ename]:
                    engmap[ename](make(ename))


PHASES = ["A", "F0", "B", "F1"]


def build(stop_after="F1"):
    nc = bass.Bass("TRN2", target_bir_lowering=False)
    S = Sched(nc)

    def din(name, shape):
        return nc.dram_tensor(name, shape, F32, kind="ExternalInput").ap()

    xT_d = din("xT", [1024, T])
    a_w_in = din("a_w_in", [1024, 6144])
    a_w_out = din("a_w_out", [2048, 1024])
    w_kv = din("w_kv", [1024, 2048])
    b_w_q = din("b_w_q", [1024, 1024])
    b_w_out = din("b_w_out", [1024, 1024])
    w_up_r = [din(f"w_up{l}", [22, 128, 2048]) for l in range(2)]
    w_down_r = [din(f"w_down{l}", [8, 128, 2816]) for l in range(2)]
    cmat_d = din("cmat", [128, 4 * 128])
    cossin_d = din("cossin", [128, 2 * T])
    rtab_d = din("rtab", [128, 4 * 640])
    small_d = din("small", [128, 512])
    lamp_d = din("lamp", [128, 256])
    outT = nc.dram_tensor("outT", [1024, T], F32, kind="ExternalOutput").ap()
    xT_v = xT_d.rearrange("(c p) t -> p c t", p=128)
    outT_v = outT.rearrange("(c p) t -> p c t", p=128)

    ARENA = 212736
    arena = nc.alloc_sbuf_tensor("arena", [128, ARENA], U8)
    psum = nc.alloc_psum_tensor("psum", [128, 4096], F32)

    def carve(off, dims, dt):
        nb = int(np.prod(dims)) * (4 if dt == F32 else 2)
        assert off % 32 == 0 and off + nb <= ARENA, (off, nb)
        v = arena[:, off:off + nb].bitcast(dt)
        if len(dims) == 2:
            v = v.rearrange("p (a b) -> p a b", a=dims[0])
        elif len(dims) == 3:
            v = v.rearrange("p (a b c) -> p a b c", a=dims[0], b=dims[1])
        return v

    def PS(b, n=512, off=0):
        return psum[:, b * 512 + off:b * 512 + off + n]

    cmat = carve(0, [4, 128], BF16)
    ones_bf = cmat[:, 0, :]
    ident_bf = cmat[:, 1, :]
    cmask_bf = cmat[:, 2, :]
    maskneg_bf = cmat[:, 3, :]
    small = carve(1 * KB, [512], F32)
    GAIN0, CONV0, KDEC0, SUBLN0 = 0, 72, 424, 428
    lamv = small[:, 448:464]
    neghalf = carve(3 * KB, [512], BF16)
    lamp = carve(4 * KB, [256], F32)
    BIG = 5 * KB
    O = BIG + 64 * KB

    def gain(gi, c):
        return small[:, GAIN0 + gi * 8 + c:GAIN0 + gi * 8 + c + 1]

    def xT_r(r):
        return carve(BIG + r * 16 * KB, [8, 512], F32)

    S.dma("pool", lambda e: e.dma_start(out=cmat, in_=cmat_d.rearrange("p (a b) -> p a b", a=4)), "cst", writes=["cmat"])
    S.dma("sp", lambda e: e.dma_start(out=small, in_=small_d), "cst2", writes=["small"])
    S.dma("sp", lambda e: e.dma_start(out=lamp, in_=lamp_d), "cst3", writes=["lamp"])
    S.op("pool", lambda e: e.memset(neghalf, -0.5), writes=["neghalf"])
    epsb = small[:, 464:465]

    rot = {"sq": 0, "dvepool": 0}

    def alt_engine():
        rot["dvepool"] ^= 1
        return "dve" if rot["dvepool"] else "pool"

    def stats_rstd(src_fn, src_keys, sq_tiles, rstd_ap, rstd_key, bank=7):
        for c in range(8):
            j = rot["sq"] % len(sq_tiles)
            rot["sq"] += 1
            sq = sq_tiles[j]
            src = src_fn(c)
            S.op("act", lambda e, sq=sq, src=src: e.activation(out=sq, in_=src, func=AF.Square),
                 reads=src_keys(c) + ["small"], writes=[f"sqt{j}"])
            S.op("pe", lambda e, sq=sq, c=c: e.matmul(PS(bank), lhsT=ones_bf, rhs=sq, start=(c == 0), stop=(c == 7)),
                 reads=[f"sqt{j}", "cmat"], writes=[f"ps{bank}"])
        S.op("act", lambda e: e.activation(out=rstd_ap, in_=PS(bank), func=AF.Sqrt, bias=epsb, scale=1.0),
             reads=[f"ps{bank}", "small"], writes=[rstd_key])
        S.op("dve", lambda e: e.reciprocal(out=rstd_ap, in_=rstd_ap),
             reads=[rstd_key], writes=[rstd_key])

    def apply_norm(src_fn, src_keys, gi, rstd_ap, rstd_key, dst_fn, dst_keys):
        for c in range(8):
            eng = "dve"
            S.op(eng, lambda e, c=c: e.scalar_tensor_tensor(out=dst_fn(c), in0=src_fn(c), scalar=gain(gi, c),
                                                            in1=rstd_ap, op0=ALU.mult, op1=ALU.mult),
                 reads=src_keys(c) + [rstd_key, "small"], writes=dst_keys(c))

    def post_residual(r, hbuf, hkey, gi, sq_tiles, rstd_ap, rstd_key, xsrc_fn, xsrc_keys, xdst_keys):
        stats_rstd(lambda c: hbuf[:, c, :], lambda c: [f"{hkey}{c}"], sq_tiles, rstd_ap, rstd_key)
        xr = xT_r(r)
        for c in range(8):
            S.op("dve", lambda e, c=c: e.scalar_tensor_tensor(out=hbuf[:, c, :], in0=hbuf[:, c, :], scalar=gain(gi, c),
                                                              in1=rstd_ap, op0=ALU.mult, op1=ALU.mult),
                 reads=[f"{hkey}{c}", rstd_key, "small"], writes=[f"{hkey}{c}"])
            S.op("pool", lambda e, c=c: e.tensor_tensor(out=xr[:, c, :], in0=hbuf[:, c, :], in1=xsrc_fn(c), op=ALU.add),
                 reads=[f"{hkey}{c}"] + xsrc_keys(c), writes=xdst_keys(c))

    def proj_to_hbuf(r, nk, lhsT_fn, rhs_fn, in_keys, hbuf, hkey, banks=(0, 1, 2, 3, 4, 5)):
        for dc in range(8):
            b = banks[(r * 8 + dc) % len(banks)]
            for k in range(nk):
                S.op("pe", lambda e, k=k, dc=dc, b=b: e.matmul(PS(b), lhsT=lhsT_fn(k, dc), rhs=rhs_fn(k),
                                                               start=(k == 0), stop=(k == nk - 1)),
                     reads=in_keys(k), writes=[f"ps{b}"])
            S.op("act", lambda e, dc=dc, b=b: e.copy(out=hbuf[:, dc, :], in_=PS(b)),
                 reads=[f"ps{b}"], writes=[f"{hkey}{dc}"])

    def proj_post(r, nk, lhsT_fn, rhs_fn, in_keys, hbuf, hkey, gi, sq_tiles, rstd_ap, rstd_key,
                  xsrc_fn, xsrc_keys, xdst_keys, banks=(0, 1, 2, 3, 4), sbank=7):
        pend = None
        for dc in range(8):
            b = banks[(r * 8 + dc) % len(banks)]
            for k in range(nk):
                S.op("pe", lambda e, k=k: e.matmul(PS(b), lhsT=lhsT_fn(k, dc), rhs=rhs_fn(k), start=(k == 0), stop=(k == nk - 1)),
                     reads=in_keys(k), writes=[f"ps{b}"])
            if pend is not None:
                pend()
            S.op("act", lambda e: e.mul(out=hbuf[:, dc, :], in_=PS(b), mul=gain(gi, dc)),
                 reads=[f"ps{b}", "small"], writes=[f"{hkey}{dc}"])
            j = rot["sq"] % len(sq_tiles)
            rot["sq"] += 1
            S.op("act", lambda e: e.activation(out=sq_tiles[j], in_=PS(b), func=AF.Square),
                 reads=[f"ps{b}"], writes=[f"sqt{j}"])

            def pend(j=j, dc=dc):
                S.op("pe", lambda e: e.matmul(PS(sbank), lhsT=ones_bf, rhs=sq_tiles[j], start=(dc == 0), stop=(dc == 7)),
                     reads=[f"sqt{j}", "cmat"], writes=[f"ps{sbank}"])
        pend()
        S.op("act", lambda e: e.activation(out=rstd_ap, in_=PS(sbank), func=AF.Sqrt, bias=epsb, scale=1.0),
             reads=[f"ps{sbank}", "small"], writes=[rstd_key])
        S.op("dve", lambda e: e.reciprocal(out=rstd_ap, in_=rstd_ap), reads=[rstd_key], writes=[rstd_key])
        xr = xT_r(r)
        for c in range(8):
            S.op("dve", lambda e: e.tensor_tensor(out=hbuf[:, c, :], in0=hbuf[:, c, :], in1=rstd_ap, op=ALU.mult),
                 reads=[f"{hkey}{c}", rstd_key], writes=[f"{hkey}{c}"])
            eng = "dve"
            S.op(eng, lambda e: e.tensor_tensor(out=xr[:, c, :], in0=hbuf[:, c, :], in1=xsrc_fn(c), op=ALU.add),
                 reads=[f"{hkey}{c}"] + xsrc_keys(c), writes=xdst_keys(c))

    def ffn_prenorm0(l):
        xnh = carve(O, [8, 1024], BF16)
        sqf = [carve(O + 16 * KB + i * KB, [512], BF16) for i in range(3)]
        rstd_f = [carve(O + 16 * KB + 3 * KB + i * 2 * KB, [512], F32) for i in range(2)]
        for rr in range(2):
            xr = xT_r(rr)
            stats_rstd(lambda c: xr[:, c, :], lambda c: [f"big{rr}"], sqf, rstd_f[rr], f"rstdf{rr}", bank=5)
            apply_norm(lambda c: xr[:, c, :], lambda c: [f"big{rr}"], 5 + l, rstd_f[rr], f"rstdf{rr}",
                       lambda c: xnh[:, c, rr * 512:(rr + 1) * 512], lambda c: [f"xnh{c}_{rr}"])

    XN = O
    xn = carve(XN, [8, T], BF16)
    WQK = O + 32 * KB
    WVG = O + 40 * KB
    QKB = O + 72 * KB
    ROT = O + 80 * KB
    CS = O + 88 * KB
    RTAB = O + 104 * KB
    MISC = O + 114 * KB
    wqk = carve(WQK, [8, 512], BF16)
    wvg = [carve(WVG + i * 16 * KB, [8, 1024], BF16) for i in range(2)]
    qT = carve(QKB, [2, 512], BF16)
    qdT = carve(QKB + 2 * KB, [2, 512], BF16)
    kT = carve(QKB + 4 * KB, [2, 512], BF16)
    ktok = carve(QKB + 6 * KB, [4, 256], BF16)
    rt = [carve(ROT + i * 2 * KB, [512], F32) for i in range(4)]
    cossin = carve(CS, [2, T], F32)
    rtab = carve(RTAB, [4, 640], F32)
    mo = [MISC]

    def misc(dims, dt):
        nb = int(np.prod(dims)) * (4 if dt == F32 else 2)
        nb = (nb + 31) // 32 * 32
        v = carve(mo[0], dims, dt)
        mo[0] += nb
        return v

    vtok = [misc([512], BF16) for _ in range(3)]
    sgt = [misc([512], BF16) for _ in range(3)]
    sTb = [misc([128], BF16) for _ in range(2)]
    Sf = misc([2, 512], F32)
    Sb = [misc([2, 512], BF16) for _ in range(2)]
    ybf = [misc([512], BF16) for _ in range(2)]
    sqj = misc([512], BF16)
    sqa = [misc([512], BF16) for _ in range(2)]
    ssr = misc([8], F32)
    rstd_a = [misc([512], F32) for _ in range(2)]
    assert mo[0] <= ARENA, mo[0]

    wout_a = carve(WVG + ((H_RET - 2) % 2) * 16 * KB, [8, 1024], BF16)
    wout_b = carve(CS, [8, 1024], BF16)

    def load_wqk(h):
        S.dma("pool", lambda e: e.dma_start(out=wqk[:, :, 0:256],
                                            in_=a_w_in[:, h * 256:(h + 1) * 256].rearrange("(c p) n -> p c n", p=128)),
              "wqk_q", writes=["wqk_q"])
        S.dma("pool", lambda e: e.dma_start(out=wqk[:, :, 256:512],
                                            in_=a_w_in[:, 1024 + h * 256:1024 + (h + 1) * 256].rearrange("(c p) n -> p c n", p=128)),
              "wqk_k", writes=["wqk_k"])

    def load_wvg(h):
        sl = h % 2
        S.dma("pool", lambda e: e.dma_start(out=wvg[sl][:, :, 0:512],
                                            in_=a_w_in[:, 2048 + h * 512:2048 + (h + 1) * 512].rearrange("(c p) n -> p c n", p=128)),
              f"wvg_v{sl}", writes=[f"wvg_v{sl}"])
        S.dma("pool", lambda e: e.dma_start(out=wvg[sl][:, :, 512:1024],
                                            in_=a_w_in[:, 4096 + h * 512:4096 + (h + 1) * 512].rearrange("(c p) n -> p c n", p=128)),
              f"wvg_g{sl}", writes=[f"wvg_g{sl}"])

    load_wqk(0)
    load_wvg(0)

    def a0_dma(r):
        sl = r % 2
        xs = carve(BIG + (2 + sl) * 16 * KB, [8, 512], F32)
        S.dma("sp", lambda e: e.dma_start(out=xs, in_=xT_v[:, :, r * 512:(r + 1) * 512]), f"xs{sl}",
              writes=[f"xs{sl}_{c}" for c in range(8)] + [f"big{2 + sl}"])

    def a0_range(r):
        sl = r % 2
        xs = carve(BIG + (2 + sl) * 16 * KB, [8, 512], F32)
        stats_rstd(lambda c: xs[:, c, :], lambda c: [f"xs{sl}_{c}", f"big{2 + sl}"], sqa, rstd_a[sl], f"rstda{sl}")
        apply_norm(lambda c: xs[:, c, :], lambda c: [f"xs{sl}_{c}", f"big{2 + sl}"], 0, rstd_a[sl], f"rstda{sl}",
                   lambda c: xn[:, c, r * 512:(r + 1) * 512], lambda c: [f"xn{c}_{r}"])
        if r + 2 < NRNG:
            a0_dma(r + 2)

    a0_dma(0)
    a0_dma(1)
    S.dma("sp", lambda e: e.dma_start(out=cossin, in_=cossin_d.rearrange("p (a b) -> p a b", a=2)), "cs", writes=["cossin"])
    S.dma("sp", lambda e: e.dma_start(out=rtab, in_=rtab_d.rearrange("p (a b) -> p a b", a=4)), "rtab", writes=["rtab"])

    GAMMA = [1.0 - 2.0 ** (-5.0 - h) for h in range(H_RET)]
    cnt = {"v": 0, "sT": 0, "Sb": 0, "y": 0}
    pend_tr = {"f": None}
    for h in range(H_RET):
        wv = wvg[h % 2]
        wvk = f"wvg_v{h % 2}"
        wgk = f"wvg_g{h % 2}"
        gC = GAMMA[h] ** 128
        for r in range(NRNG):
            tok = slice(r * 512, (r + 1) * 512)
            if h == 0 and r == 0:
                a0_range(0)
            for dc in range(4):
                wkey = "wqk_q" if dc < 2 else "wqk_k"
                for kc in range(8):
                    S.op("pe", lambda e, dc=dc, kc=kc: e.matmul(PS(dc), lhsT=wqk[:, kc, dc * 128:(dc + 1) * 128],
                                                                rhs=xn[:, kc, tok], start=(kc == 0), stop=(kc == 7)),
                         reads=[wkey, f"xn{kc}_{r}"], writes=[f"ps{dc}"])
            if r == NRNG - 1 and h + 1 < H_RET:
                load_wqk(h + 1)
            if h == 0 and r == 0:
                load_wvg(1)
            cos_r = cossin[:, 0, tok]
            sin_r = cossin[:, 1, tok]
            for qk in range(2):
                b1, b2 = 2 * qk, 2 * qk + 1
                S.op("dve", lambda e: e.tensor_tensor(out=rt[0], in0=PS(b1), in1=cos_r, op=ALU.mult),
                     reads=[f"ps{b1}", "cossin"], writes=["rt0"])
                S.op("dve", lambda e: e.tensor_tensor(out=rt[1], in0=PS(b2), in1=sin_r, op=ALU.mult),
                     reads=[f"ps{b2}", "cossin"], writes=["rt1"])
                S.op("dve", lambda e: e.tensor_tensor(out=rt[2], in0=PS(b1), in1=sin_r, op=ALU.mult),
                     reads=[f"ps{b1}", "cossin"], writes=["rt2"])
                S.op("dve", lambda e: e.tensor_tensor(out=rt[3], in0=PS(b2), in1=cos_r, op=ALU.mult),
                     reads=[f"ps{b2}", "cossin"], writes=["rt3"])
                if qk == 0:
                    S.op("dve", lambda e: e.tensor_tensor(out=rt[0], in0=rt[0], in1=rt[1], op=ALU.subtract),
                         reads=["rt0", "rt1"], writes=["rt0"])
                    S.op("dve", lambda e: e.tensor_tensor(out=rt[2], in0=rt[2], in1=rt[3], op=ALU.add),
                         reads=["rt2", "rt3"], writes=["rt2"])
                    S.op("act", lambda e: e.copy(out=qT[:, 0, :], in_=rt[0]), reads=["rt0"], writes=["qT0"])
                    S.op("act", lambda e: e.copy(out=qT[:, 1, :], in_=rt[2]), reads=["rt2"], writes=["qT1"])
                    S.op("dve", lambda e: e.tensor_tensor(out=qdT[:, 0, :], in0=rt[0], in1=rtab[:, h, 128:640], op=ALU.mult),
                         reads=["rt0", "rtab"], writes=["qdT0"])
                    S.op("dve", lambda e: e.tensor_tensor(out=qdT[:, 1, :], in0=rt[2], in1=rtab[:, h, 128:640], op=ALU.mult),
                         reads=["rt2", "rtab"], writes=["qdT1"])
                else:
                    S.op("pool", lambda e: e.tensor_tensor(out=kT[:, 0, :], in0=rt[0], in1=rt[1], op=ALU.subtract),
                         reads=["rt0", "rt1"], writes=["kT0"])
                    S.op("dve", lambda e: e.tensor_tensor(out=kT[:, 1, :], in0=rt[2], in1=rt[3], op=ALU.add),
                         reads=["rt2", "rt3"], writes=["kT1"])

            if h == H_RET - 1 and r == NRNG - 1:
                S.dma("pool", lambda e: e.dma_start(out=wout_b, in_=a_w_out[1024:2048, :].rearrange("(c p) n -> p c n", p=128)),
                      "wout1", writes=["wout1", "cossin"])

            def emit_vg(n):
                gn = r * 4 + n
                tk = slice(gn * 128, (gn + 1) * 128)
                vs = cnt["v"] % 3
                cnt["v"] += 1
                for kc in range(8):
                    S.op("pe", lambda e, kc=kc: e.matmul(PS(0), lhsT=xn[:, kc, tk], rhs=wv[:, kc, 0:512],
                                                         start=(kc == 0), stop=(kc == 7)),
                         reads=[wvk, f"xn{kc}_{r}"], writes=["ps0"])
                for kc in range(8):
                    S.op("pe", lambda e, kc=kc: e.matmul(PS(1), lhsT=xn[:, kc, tk], rhs=wv[:, kc, 512:1024],
                                                         start=(kc == 0), stop=(kc == 7)),
                         reads=[wgk, f"xn{kc}_{r}"], writes=["ps1"])
                if gn == 15 and h + 2 < H_RET:
                    load_wvg(h + 2)
                if gn == 15 and h == H_RET - 2:
                    S.dma("pool", lambda e: e.dma_start(out=wout_a, in_=a_w_out[0:1024, :].rearrange("(c p) n -> p c n", p=128)),
                          "wout0", writes=["wout0", f"wvg_v{h % 2}", f"wvg_g{h % 2}"])
                S.op("act", lambda e: e.copy(out=vtok[vs], in_=PS(0)), reads=["ps0"], writes=[f"vtok{vs}"])
                S.op("act", lambda e: e.activation(out=sgt[vs], in_=PS(1), func=AF.Silu),
                     reads=["ps1"], writes=[f"sgt{vs}"])
                return vs

            vs_next = emit_vg(0)
            ptr = PS(6).bitcast(BF16)
            for n in range(4):
                for dc in range(2):
                    S.op("pe", lambda e, n=n, dc=dc: e.transpose(ptr[:, n * 256 + dc * 128:n * 256 + (dc + 1) * 128],
                                                                 kT[:, dc, n * 128:(n + 1) * 128], ident_bf),
                         reads=[f"kT{dc}", "cmat"], writes=["ps6"])
            S.op("act", lambda e: e.mul(out=ktok.rearrange("p a b -> p (a b)"), in_=ptr, mul=small[:, KDEC0 + h:KDEC0 + h + 1]),
                 reads=["ps6", "small"], writes=["ktok"])
            for n in range(4):
                gn = r * 4 + n
                ck = slice(n * 128, (n + 1) * 128)
                vs = vs_next
                ss = cnt["sT"] % 2
                cnt["sT"] += 1
                for dc in range(2):
                    S.op("pe", lambda e, dc=dc: e.matmul(PS(5, 128), lhsT=kT[:, dc, ck], rhs=qT[:, dc, ck],
                                                         start=(dc == 0), stop=(dc == 1)),
                         reads=[f"kT{dc}", f"qT{dc}"], writes=["ps5"])
                S.op("dve", lambda e: e.tensor_tensor(out=sTb[ss], in0=PS(5, 128), in1=rtab[:, h, 0:128], op=ALU.mult),
                     reads=["ps5", "rtab"], writes=[f"sTb{ss}"])
                if pend_tr["f"] is not None:
                    pend_tr["f"]()
                    pend_tr["f"] = None
                sb_prev = (cnt["Sb"] - 1) % 2
                S.op("pe", lambda e: e.matmul(PS(4), lhsT=sTb[ss], rhs=vtok[vs], start=True, stop=(gn == 0)),
                     reads=[f"sTb{ss}", f"vtok{vs}"], writes=["ps4"])
                if gn > 0:
                    for dc in range(2):
                        S.op("pe", lambda e, dc=dc: e.matmul(PS(4), lhsT=qdT[:, dc, ck], rhs=Sb[sb_prev][:, dc, :],
                                                             start=False, stop=(dc == 1)),
                             reads=[f"qdT{dc}", f"Sb{sb_prev}_{dc}"], writes=["ps4"])
                ys = cnt["y"] % 2
                cnt["y"] += 1
                S.op("act", lambda e: e.activation(out=sqj, in_=PS(4), func=AF.Square, accum_out=ssr[:, 0:1]),
                     reads=["ps4"], writes=["sqj", "ssr0"])
                S.op("dve", lambda e: e.tensor_scalar(out=ssr[:, 1:2], in0=ssr[:, 0:1], scalar1=1.0 / 512, scalar2=EPS,
                                                      op0=ALU.mult, op1=ALU.add),
                     reads=["ssr0"], writes=["ssr1"])
                S.op("pool", lambda e: e.tensor_tensor(out=ssr[:, 2:3], in0=ssr[:, 1:2], in1=neghalf[:, 0:1], op=ALU.pow),
                     reads=["ssr1", "neghalf"], writes=["ssr2"])
                if gn < 15:
                    sb_new = cnt["Sb"] % 2
                    cnt["Sb"] += 1
                    for dc in range(2):
                        S.op("pe", lambda e, dc=dc: e.matmul(PS(2 + dc), lhsT=ktok[:, n, dc * 128:(dc + 1) * 128],
                                                             rhs=vtok[vs], start=True, stop=True),
                             reads=["ktok", f"vtok{vs}"], writes=[f"ps{2 + dc}"])
                        if gn == 0:
                            S.op("dve", lambda e, dc=dc: e.tensor_copy(out=Sf[:, dc, :], in_=PS(2 + dc)),
                                 reads=[f"ps{2 + dc}"], writes=[f"Sf{dc}"])
                        else:
                            S.op("dve", lambda e, dc=dc: e.scalar_tensor_tensor(out=Sf[:, dc, :], in0=Sf[:, dc, :], scalar=gC,
                                                                                in1=PS(2 + dc), op0=ALU.mult, op1=ALU.add),
                                 reads=[f"ps{2 + dc}", f"Sf{dc}"], writes=[f"Sf{dc}"])
                        S.op("act", lambda e, dc=dc: e.copy(out=Sb[sb_new][:, dc, :], in_=Sf[:, dc, :]),
                             reads=[f"Sf{dc}"], writes=[f"Sb{sb_new}_{dc}"])
                S.op("dve", lambda e: e.scalar_tensor_tensor(out=ybf[ys], in0=PS(4), scalar=ssr[:, 2:3], in1=sgt[vs],
                                                             op0=ALU.mult, op1=ALU.mult),
                     reads=["ps4", "ssr2", f"sgt{vs}"], writes=[f"ybf{ys}"])
                if n < 3:
                    vs_next = emit_vg(n + 1)
                if h == 0 and n == 1 and r + 1 < NRNG:
                    a0_range(r + 1)

                def tr_fn(ys=ys, r=r, h=h, ck=ck):
                    pyT = PS(7).bitcast(BF16)
                    for ec in range(4):
                        S.op("pe", lambda e, ec=ec: e.transpose(pyT[:, ec * 128:(ec + 1) * 128],
                                                                ybf[ys][:, ec * 128:(ec + 1) * 128], ident_bf),
                             reads=[f"ybf{ys}", "cmat"], writes=["ps7"])
                    yT_r = carve(BIG + r * 16 * KB, [16, 512], BF16)
                    S.op("act", lambda e: e.copy(out=yT_r[:, h * 4:(h + 1) * 4, ck],
                                                 in_=pyT[:, 0:512].rearrange("p (a b) -> p a b", a=4)),
                         reads=["ps7"], writes=[f"big{r}"])
                pend_tr["f"] = tr_fn
    pend_tr["f"]()
    S.barrier()

    hbufs = [carve(MISC, [8, 512], F32),
             carve(O + 24 * KB, [8, 512], F32)]
    if stop_after != "A":
        for fc in range(2):
            S.dma("pool", lambda e, fc=fc: e.dma_start(out=carve(O + 72 * KB + fc * 4 * KB, [2048], BF16), in_=w_up_r[0][fc]),
                  f"wu{fc}", writes=[f"wu{fc}"])
    xs2 = carve(WVG + (1 - (H_RET - 2) % 2) * 16 * KB, [8, 512], F32)
    def wout(k, dc):
        return (wout_a if k < 8 else wout_b)[:, k % 8, dc * 128:(dc + 1) * 128]

    for r in range(NRNG):
        yT_r = carve(BIG + r * 16 * KB, [16, 512], BF16)
        S.dma("sp", lambda e, r=r: e.dma_start(out=xs2, in_=xT_v[:, :, r * 512:(r + 1) * 512]), "xs2",
              writes=[f"xs2_{c}" for c in range(8)])
        proj_post(r, 16, wout, lambda k: yT_r[:, k, :],
                  lambda k: [f"wout{k // 8}", f"big{r}"], hbufs[r % 2], f"hb{r % 2}_", 1, sqa, rstd_a[r % 2], f"rstda{r % 2}",
                  lambda c: xs2[:, c, :], lambda c: [f"xs2_{c}"], lambda c: [f"big{r}"])
        if r == 2 and stop_after != "A":
            ffn_prenorm0(0)
    if stop_after == "A":
        S.barrier()
    else:
        S.soft_barrier()

    early_out = {"done": False, "half": False}

    def finish():
        for r in range(2 if early_out["done"] else 0, NRNG):
            c0 = 4 if (early_out["half"] and r >= 2) else 0
            S.dma("sp", lambda e, r=r, c0=c0: e.dma_start(out=outT_v[:, c0:8, r * 512:(r + 1) * 512], in_=xT_r(r)[:, c0:8, :]), "outd",
                  reads=[f"big{r}"])
        S.final_wait("sp")
        S.emit()
        return nc, S

    if stop_after == "A":
        return finish()

    def ffn(l, pre_done=False, wu_done=0, wu_alias=None):
        XNH = O
        TMP = O + 16 * KB
        MT = O + 28 * KB
        UP = O + 72 * KB
        NWU = 3
        NWD = 3
        xnh = carve(XNH, [8, 1024], BF16)
        sqf = [carve(TMP + i * KB, [512], BF16) for i in range(3)]
        rstd_f = [carve(TMP + 3 * KB + i * 2 * KB, [512], F32) for i in range(2)]
        halo = carve(TMP + 7 * KB, [44, 2], F32)
        rstd_p = [carve(TMP + 8 * KB + i * 2 * KB, [512], F32) for i in range(2)]
        mT = carve(MT, [22, 1024], BF16)
        wu = [carve(UP + i * 4 * KB, [8, 256], BF16) for i in range(NWU)]
        WDC = UP + NWU * 4 * KB
        wdc = [carve(WDC + i * 5632, [22, 128], BF16) for i in range(NWD)]
        HS = WDC + NWD * 5632
        hs = [[carve(HS + (s * 2 + j) * 4128, [1032], F32) for j in range(2)] for s in range(2)]
        AC = HS + 4 * 4128
        acc = [[carve(AC + (s * 2 + j) * 4 * KB, [1024], F32) for j in range(2)] for s in range(2)]
        SG = AC + 16 * KB
        sgf = [carve(SG + s * 2 * KB, [1024], BF16) for s in range(2)]
        assert SG + 4 * KB <= ARENA
        hb2 = [carve(HS + i * 16 * KB, [8, 512], F32) for i in range(2)]

        def hb_overlap(off, size):
            ks = []
            for rr_ in range(2):
                for dc_ in range(8):
                    a0 = rr_ * 16384 + dc_ * 2048
                    if off < a0 + 2048 and off + size > a0:
                        ks.append(f"hb{rr_}_{dc_}")
            return ks

        hs_alias = [[hb_overlap((s_ * 2 + j_) * 4128, 4128) for j_ in range(2)] for s_ in range(2)]
        acc_alias = [[hb_overlap(16512 + (s_ * 2 + j_) * 4096, 4096) for j_ in range(2)] for s_ in range(2)]
        gi_pre, gi_post = 5 + l, 7 + l

        def cw(j, fc):
            o = CONV0 + (l * 4 + j) * 44 + fc
            return small[:, o:o + 1]

        first_load = {"n": 0}

        def load_wu(fc):
            ws = fc % NWU
            extra = []
            if wu_alias is not None and first_load["n"] < NWU:
                extra = wu_alias[ws]
            first_load["n"] += 1
            S.dma("pool", lambda e: e.dma_start(out=wu[ws].rearrange("p a b -> p (a b)"), in_=w_up_r[l][fc]),
                  f"wu{ws}", writes=[f"wu{ws}"] + extra)

        def load_wdc(dc):
            sl = dc % NWD
            S.dma("pool", lambda e: e.dma_start(out=wdc[sl].rearrange("p a b -> p (a b)"), in_=w_down_r[l][dc]),
                  f"wdc{sl}", writes=[f"wdc{sl}"])

        def prenorm(hf):
            for rr in range(2):
                r = hf * 2 + rr
                xr = xT_r(r)
                stats_rstd(lambda c: xr[:, c, :], lambda c: [f"big{r}"], sqf, rstd_f[rr], f"rstdf{rr}", bank=5)
                apply_norm(lambda c: xr[:, c, :], lambda c: [f"big{r}"], gi_pre, rstd_f[rr], f"rstdf{rr}",
                           lambda c: xnh[:, c, rr * 512:(rr + 1) * 512], lambda c: [f"xnh{c}_{rr}"])

        for fc in range(wu_done, NWU - 1):
            load_wu(fc)
        if not pre_done:
            prenorm(0)
        for hf in range(2):
            if hf == 0:
                for st in range(2):
                    for j in range(2):
                        S.op("dve", lambda e: e.memset(hs[st][j][:, 0:2], 0.0), writes=[f"hs{st}{j}h"])
            for fc in range(22):
                ws = fc % NWU
                st = fc % 2
                if fc + NWU - 1 < 22:
                    load_wu(fc + NWU - 1)
                if fc == 16:
                    load_wdc(0)
                if fc == 18:
                    load_wdc(1)
                if fc == 20:
                    load_wdc(2)
                for j in range(2):
                    bb = st * 4 + j * 2
                    for rr in range(2):
                        for kc in range(8):
                            S.op("pe", lambda e, kc=kc, rr=rr: e.matmul(
                                PS(bb + rr), lhsT=wu[ws][:, kc, j * 128:(j + 1) * 128], rhs=xnh[:, kc, rr * 512:(rr + 1) * 512],
                                start=(kc == 0), stop=(kc == 7)),
                                reads=[f"wu{ws}", f"xnh{kc}_{rr}"], writes=[f"ps{bb + rr}"])
                    hsj = hs[st][j]
                    acj = acc[st][j]
                    hk = f"hs{st}{j}"
                    ak = f"acc{st}{j}"
                    fcj = fc + 22 * j
                    hsa = hs_alias[st][j] if hf == 1 else []
                    aca = acc_alias[st][j] if hf == 1 else []
                    if hf == 1:
                        S.op("act", lambda e: e.copy(out=hsj[:, 0:2], in_=halo[:, fcj, :]),
                             reads=[f"halo{fcj}"], writes=[hk + "h"] + hsa)
                    S.op("act", lambda e: e.copy(out=hsj[:, 2:1026], in_=psum[:, bb * 512:bb * 512 + 1024]),
                         reads=[f"ps{bb}", f"ps{bb + 1}"], writes=[hk] + hsa)
                    if hf == 0:
                        S.op("act", lambda e: e.copy(out=halo[:, fcj, :], in_=psum[:, bb * 512 + 1022:bb * 512 + 1024]),
                             reads=[f"ps{bb + 1}"], writes=[f"halo{fcj}"])
                    S.op("act", lambda e: e.activation(
                        out=acj, in_=hsj[:, 0:1024], func=AF.Identity, bias=cw(3, fcj), scale=cw(0, fcj)),
                        reads=[hk, hk + "h", "small"], writes=[ak] + aca)
                    S.op("dve", lambda e: e.scalar_tensor_tensor(
                        out=acj, in0=hsj[:, 1:1025], scalar=cw(1, fcj), in1=acj, op0=ALU.mult, op1=ALU.add),
                        reads=[hk, hk + "h", ak, "small"], writes=[ak])
                    S.op("dve", lambda e: e.scalar_tensor_tensor(
                        out=acj, in0=hsj[:, 2:1026], scalar=cw(2, fcj), in1=acj, op0=ALU.mult, op1=ALU.add),
                        reads=[hk, ak, "small"], writes=[ak])
                S.op("act", lambda e: e.activation(out=sgf[st], in_=acc[st][0], func=AF.Silu),
                     reads=[f"acc{st}0"], writes=[f"sgf{st}"])
                S.op("pool", lambda e: e.tensor_tensor(out=mT[:, fc, :], in0=sgf[st], in1=acc[st][1], op=ALU.mult),
                     reads=[f"sgf{st}", f"acc{st}1"], writes=[f"mT{fc}"])
            S.soft_barrier()
            if hf == 0:
                for fc in range(NWU - 1):
                    load_wu(fc)
            groups = [(dc, rr) for dc in range(8) for rr in range(2)]
            pend = None

            def mm_range(gidx, k0, k1):
                dc, rr = groups[gidx]
                b = gidx % 5
                sl = dc % NWD
                for k in range(k0, k1):
                    S.op("pe", lambda e, k=k: e.matmul(PS(b), lhsT=wdc[sl][:, k, :], rhs=mT[:, k, rr * 512:(rr + 1) * 512],
                                                      start=(k == 0), stop=(k == 21)),
                         reads=[f"wdc{sl}", f"mT{k}"], writes=[f"ps{b}"])

            def evac(gidx):
                nonlocal pend
                dc, rr = groups[gidx]
                b = gidx % 5
                if pend is not None:
                    pend()
                S.op("act", lambda e: e.mul(out=hb2[rr][:, dc, :], in_=PS(b), mul=gain(gi_post, dc)),
                     reads=[f"ps{b}", "small"], writes=[f"hb{rr}_{dc}"])
                j = rot["sq"] % len(sqf)
                rot["sq"] += 1
                S.op("act", lambda e: e.activation(out=sqf[j], in_=PS(b), func=AF.Square),
                     reads=[f"ps{b}"], writes=[f"sqt{j}"])

                def pend(j=j, rr=rr, dc=dc):
                    S.op("pe", lambda e: e.matmul(PS(6 + rr), lhsT=ones_bf, rhs=sqf[j], start=(dc == 0), stop=(dc == 7)),
                         reads=[f"sqt{j}", "cmat"], writes=[f"ps{6 + rr}"])

            for gidx in range(3):
                mm_range(gidx, 0, 20)
            for gidx in range(16):
                dc, rr = groups[gidx]
                if gidx < 3:
                    mm_range(gidx, 20, 22)
                else:
                    mm_range(gidx, 0, 22)
                evac(gidx)
                if rr == 1:
                    if dc + NWD < 8:
                        load_wdc(dc + NWD)
                    if hf == 0 and dc == 2:
                        pend()
                        pend = None
                        prenorm(1)
            pend()
            for rr in range(2):
                S.op("act", lambda e: e.activation(out=rstd_p[rr], in_=PS(6 + rr), func=AF.Sqrt, bias=epsb, scale=1.0),
                     reads=[f"ps{6 + rr}", "small"], writes=[f"rstdp{rr}"])
            for rr in range(2):
                S.op("dve", lambda e: e.reciprocal(out=rstd_p[rr], in_=rstd_p[rr]), reads=[f"rstdp{rr}"], writes=[f"rstdp{rr}"])
            for c in range(8):
                for rr in range(2):
                    r = hf * 2 + rr
                    xr = xT_r(r)
                    S.op("dve", lambda e: e.tensor_tensor(out=hb2[rr][:, c, :], in0=hb2[rr][:, c, :], in1=rstd_p[rr], op=ALU.mult),
                         reads=[f"hb{rr}_{c}", f"rstdp{rr}"], writes=[f"hb{rr}_{c}"])
                    eng = "dve"
                    last_half = (l == 1 and hf == 1 and stop_after == "F1")
                    S.op(eng, lambda e: e.tensor_tensor(out=xr[:, c, :], in0=hb2[rr][:, c, :], in1=xr[:, c, :], op=ALU.add),
                         reads=[f"hb{rr}_{c}", f"big{r}"], writes=[f"big{r}"] + ([f"bigo{r}"] if last_half and c < 4 else []))
                if c == 3 and l == 1 and hf == 1 and stop_after == "F1":
                    for rr in range(2):
                        r = hf * 2 + rr
                        S.dma("sp", lambda e, r=r: e.dma_start(out=outT_v[:, 0:4, r * 512:(r + 1) * 512], in_=xT_r(r)[:, 0:4, :]),
                              "outd", reads=[f"bigo{r}"])
                    early_out["half"] = True
            if hf == 0:
                if l == 1 and stop_after == "F1":
                    for r in range(2):
                        S.dma("sp", lambda e, r=r: e.dma_start(out=outT_v[:, :, r * 512:(r + 1) * 512], in_=xT_r(r)), "outd",
                              reads=[f"big{r}"])
                    early_out["done"] = True
        S.barrier()

    ffn(0, pre_done=True, wu_done=2)
    if stop_after == "F0":
        return finish()

    KT = O
    VT = O + 32 * KB
    R1 = O + 65 * KB
    R2 = O + 97 * KB
    SM = O + 129 * KB
    kTb = carve(KT, [8, T], BF16)
    vaug = carve(VT, [16, 8, 129], BF16)
    xnb = carve(R1, [8, T], BF16)
    oT = carve(R1, [8, T], BF16)
    wk = [carve(R2 + i * 2 * KB, [8, 128], BF16) for i in range(3)]
    wvb = carve(R2 + 6 * KB, [8, 1024], BF16)
    rstd_b = carve(R2 + 22 * KB, [T], F32)
    qTb = carve(R2, [8, T], BF16)
    wbo_a = carve(R2, [4, 1024], BF16)
    wbo_b = carve(R2 + 24 * KB, [4, 1024], BF16)
    hbb = carve(R2 + 8 * KB, [8, 512], F32)
    so = [SM]

    def sm(dims, dt):
        nb = int(np.prod(dims)) * (4 if dt == F32 else 2)
        nb = (nb + 31) // 32 * 32
        v = carve(so[0], dims, dt)
        so[0] += nb
        return v

    so[0] += 3 * KB
    wq = [sm([8, 128], BF16) for _ in range(2)]
    sqb = [carve(R2 + 30 * KB + i * KB, [512], BF16) for i in range(2)]
    of32 = [sm([128], F32) for _ in range(2)]
    onb = [sm([128], BF16) for _ in range(2)]
    rcp = sm([8], F32)
    ssb = sm([8], F32)
    sqjb = sm([128], BF16)
    sqb4 = [carve(SM + i * KB, [512], BF16) for i in range(2)]
    rstd_bo = carve(SM + 3 * KB, [512], F32)
    assert so[0] <= ARENA, so[0]

    lp = lamp.rearrange("p (a b) -> p a b", a=4)
    S.op("dve", lambda e: e.tensor_tensor(out=lamp[:, 0:64], in0=lp[:, 0, :], in1=lp[:, 1, :], op=ALU.mult),
         reads=["lamp"], writes=["lamp"])
    S.op("dve", lambda e: e.tensor_tensor(out=lamp[:, 128:192], in0=lp[:, 2, :], in1=lp[:, 3, :], op=ALU.mult),
         reads=["lamp"], writes=["lamp"])
    S.op("dve", lambda e: e.tensor_reduce(out=lamv[:, 0:1], in_=lamp[:, 0:64], axis=AX.X, op=ALU.add), reads=["lamp"], writes=["lamv0"])
    S.op("dve", lambda e: e.tensor_reduce(out=lamv[:, 1:2], in_=lamp[:, 128:192], axis=AX.X, op=ALU.add), reads=["lamp"], writes=["lamv1"])
    S.op("act", lambda e: e.activation(out=lamv[:, 2:4], in_=lamv[:, 0:2], func=AF.Exp), reads=["lamv0", "lamv1"], writes=["lamv2"])
    S.op("dve", lambda e: e.tensor_tensor(out=lamv[:, 4:5], in0=lamv[:, 3:4], in1=lamv[:, 2:3], op=ALU.subtract),
         reads=["lamv2"], writes=["lamv4"])
    S.op("dve", lambda e: e.tensor_scalar(out=lamv[:, 5:6], in0=lamv[:, 4:5], scalar1=-LAMBDA_INIT, scalar2=None, op0=ALU.add),
         reads=["lamv4"], writes=["neglam"])
    S.op("dve", lambda e: e.tensor_scalar(out=lamv[:, 6:7], in0=small[:, SUBLN0:SUBLN0 + 1], scalar1=1.0 - LAMBDA_INIT,
                                          scalar2=None, op0=ALU.mult),
         reads=["small"], writes=["gsub"])
    neglam = lamv[:, 5:6]
    gsub = lamv[:, 6:7]

    S.dma("pool", lambda e: e.dma_start(out=wvb, in_=w_kv[:, 1024:2048].rearrange("(c p) n -> p c n", p=128)), "wvb", writes=["wvb"])
    for r in range(NRNG):
        xr = xT_r(r)
        stats_rstd(lambda c: xr[:, c, :], lambda c: [f"big{r}"], sqb, rstd_b[:, r * 512:(r + 1) * 512], f"rstdb{r}")
        apply_norm(lambda c: xr[:, c, :], lambda c: [f"big{r}"], 2, rstd_b[:, r * 512:(r + 1) * 512], f"rstdb{r}",
                   lambda c: xnb[:, c, r * 512:(r + 1) * 512], lambda c: [f"xnb{c}_{r}"])
    bk = {"i": 0}

    def nbank():
        bk["i"] += 1
        return bk["i"] % 6

    for c in range(8):
        ws = c % 3
        S.dma("pool", lambda e, ws=ws, c=c: e.dma_start(out=wk[ws], in_=w_kv[:, c * 128:(c + 1) * 128].rearrange("(c p) n -> p c n", p=128)),
              f"wk{ws}", writes=[f"wk{ws}"])
        for r in range(NRNG):
            b = nbank()
            for kc in range(8):
                S.op("pe", lambda e, kc=kc, b=b, ws=ws, r=r: e.matmul(PS(b), lhsT=wk[ws][:, kc, :], rhs=xnb[:, kc, r * 512:(r + 1) * 512],
                                                                    start=(kc == 0), stop=(kc == 7)),
                     reads=[f"wk{ws}", f"xnb{kc}_{r}"], writes=[f"ps{b}"])
            S.op("act", lambda e, b=b, c=c, r=r: e.copy(out=kTb[:, c, r * 512:(r + 1) * 512], in_=PS(b)),
                 reads=[f"ps{b}"], writes=[f"kTb{c}"])
    for tt in range(16):
        S.op("pool", lambda e, tt=tt: e.memset(vaug[:, tt, :, 128:129], 1.0), writes=[f"vone{tt}"])
        for hh in range(2):
            b = nbank()
            for kc in range(8):
                S.op("pe", lambda e, kc=kc, b=b, tt=tt, hh=hh: e.matmul(PS(b), lhsT=xnb[:, kc, tt * 128:(tt + 1) * 128],
                                                                      rhs=wvb[:, kc, hh * 512:(hh + 1) * 512],
                                                                      start=(kc == 0), stop=(kc == 7)),
                     reads=["wvb", f"xnb{kc}_{tt // 4}"], writes=[f"ps{b}"])
            S.op("dve", lambda e, b=b, tt=tt, hh=hh: e.tensor_copy(out=vaug[:, tt, hh * 4:(hh + 1) * 4, 0:128],
                                                                   in_=PS(b).rearrange("p (a b) -> p a b", a=4)),
                 reads=[f"ps{b}"], writes=[f"vaug{tt}_{hh}"])
        if tt % 4 == 3:
            r = tt // 4
            xr = xT_r(r)
            apply_norm(lambda c: xr[:, c, :], lambda c: [f"big{r}"], 3, rstd_b[:, r * 512:(r + 1) * 512], f"rstdb{r}",
                       lambda c: xnb[:, c, r * 512:(r + 1) * 512], lambda c: [f"xnb{c}_{r}"])
    S.barrier()
    if stop_after == "B1":
        return finish()
    for c in range(8):
        ws = c % 2
        S.dma("pool", lambda e, ws=ws, c=c: e.dma_start(out=wq[ws], in_=b_w_q[:, c * 128:(c + 1) * 128].rearrange("(c p) n -> p c n", p=128)),
              f"wq{ws}", writes=[f"wq{ws}"])
        for r in range(NRNG):
            b = nbank()
            for kc in range(8):
                S.op("pe", lambda e, kc=kc, b=b, ws=ws, r=r: e.matmul(PS(b), lhsT=wq[ws][:, kc, :], rhs=xnb[:, kc, r * 512:(r + 1) * 512],
                                                                    start=(kc == 0), stop=(kc == 7)),
                     reads=[f"wq{ws}", f"xnb{kc}_{r}"], writes=[f"ps{b}"])
            S.op("act", lambda e, b=b, c=c, r=r: e.mul(out=qTb[:, c, r * 512:(r + 1) * 512], in_=PS(b), mul=0.125),
                 reads=[f"ps{b}"], writes=[f"qTb{c}"])
    S.barrier()
    if stop_after == "B2":
        return finish()
    pT3 = [carve(SM + i * KB, [512], BF16) for i in range(3)]
    accsb = carve(SM + 3 * KB, [4, 129], F32)
    STB = [0, 1, 3]
    ptrb = PS(2).bitcast(BF16)

    QREG = {0: ("oT", 6), 1: ("oT", 4), 2: ("qTb", 0), 3: ("oT", 6), 4: ("qTb", 0), 5: ("qTb", 2), 6: ("qTb", 0), 7: ("qTb", 2)}

    def alias_keys(h):
        nm, k = QREG[h]
        return [f"{nm}{k}", f"{nm}{k + 1}"]

    def qblk(h):
        nm, k = QREG[h]
        src = oT[:, k:k + 2, :] if nm == "oT" else qTb[:, k:k + 2, :]
        return src.rearrange("p a b -> p (a b)")

    def build_qblk(h):
        flat = qblk(h)
        v4 = flat.rearrange("p (r m c) -> p r m c", r=16, m=2)
        ak = alias_keys(h)
        S.op("pool", lambda e: e.memset(flat, 0.0), writes=ak)
        S.op("dve", lambda e: e.tensor_copy(out=v4[0:64, :, 0, :], in_=qTb[0:64, h, :].rearrange("p (r c) -> p r c", r=16)),
             reads=[f"qTb{h}"], writes=ak)
        S.op("dve", lambda e: e.tensor_copy(out=v4[64:128, :, 1, :], in_=qTb[64:128, h, :].rearrange("p (r c) -> p r c", r=16)),
             reads=[f"qTb{h}"], writes=ak)

    def st_mm(h, qb, kb, slot, col0=0):
        if h == 0 and qb == 0 and kb == 0:
            build_qblk(0)
        if qb == 8 and kb == 0 and h + 1 < 8:
            build_qblk(h + 1)
        if qb == 8 and kb == 0 and h == 7:
            S.dma("pool", lambda e: e.dma_start(out=wbo_a, in_=b_w_out[0:512, :].rearrange("(c p) n -> p c n", p=128)),
                  "wboa", writes=["wboa", "qTb0", "qTb1"])
            S.dma("pool", lambda e: e.dma_start(out=wbo_b, in_=b_w_out[512:1024, :].rearrange("(c p) n -> p c n", p=128)),
                  "wbob", writes=["wbob", "qTb6", "qTb7"])
        qb_ = qblk(h)
        rk = alias_keys(h) + [f"kTb{h}"]
        diag = kb == qb
        S.op("pe", lambda e: e.matmul(PS(slot, 256, col0), lhsT=kTb[:, h, kb * 128:(kb + 1) * 128], rhs=qb_[:, qb * 256:(qb + 1) * 256],
                                      start=True, stop=not diag),
             reads=rk, writes=[f"ps{slot}"])
        if diag:
            for m in range(2):
                S.op("pe", lambda e, m=m: e.matmul(PS(slot, 128, col0 + m * 128), lhsT=ident_bf, rhs=maskneg_bf, start=False, stop=(m == 1)),
                     reads=["cmat"], writes=[f"ps{slot}"])

    ACC2 = SM + 3 * KB
    ONB3 = ACC2 + 4128
    onb3 = [carve(ONB3 + p * 1024, [4, 128], BF16) for p in range(2)]
    FSM = ONB3 + 2048
    rcp8 = carve(FSM, [8], F32)
    ss4 = carve(FSM + 32, [8], F32)
    assert FSM + 64 <= ARENA
    BREG = {0: ("oT", 2), 1: ("oT", 2), 2: ("oT", 4), 3: ("oT", 4), 4: ("oT", 6), 5: ("oT", 6), 6: ("qTb", 4), 7: ("qTb", 4)}

    def accbuf(h, g):
        if g % 2 == 0:
            return carve(ACC2, [8, 129], F32), ["acc8A"]
        nm, k = BREG[h]
        base = (R1 if nm == "oT" else R2) + k * 4096
        return carve(base, [8, 129], F32), ["acc8B", f"{nm}{k}", f"{nm}{k + 1}"]

    def fin_copy(h, qb):
        par = qb % 2
        i4 = qb % 4
        ap8, keys = accbuf(h, qb // 4)
        S.op("act", lambda e: e.copy(out=ap8[:, i4 * 2:(i4 + 1) * 2, :],
                                     in_=psum[:, (4 + 2 * par) * 512:(6 + 2 * par) * 512].rearrange("p (a b) -> p a b", a=2)[:, :, 0:129]),
             reads=[f"ps{4 + 2 * par}", f"ps{5 + 2 * par}"], writes=keys)

    def finalize1(h, P, slot):
        acc8, keys = accbuf(h, P)
        a4 = acc8.rearrange("p (q m) c -> p q m c", m=2)
        S.op("dve", lambda e: e.reciprocal(out=rcp8.rearrange("p (a b) -> p a b", b=1), in_=acc8[:, :, 128:129]),
             reads=keys, writes=["rcp8"])
        S.op("dve", lambda e: e.tensor_scalar(out=rcp8.rearrange("p (q m) -> p q m", m=2)[:, :, 1],
                                              in0=rcp8.rearrange("p (q m) -> p q m", m=2)[:, :, 1],
                                              scalar1=neglam, scalar2=None, op0=ALU.mult),
             reads=["rcp8", "neglam"], writes=["rcp8"])
        S.op("dve", lambda e: e.tensor_tensor(out=acc8[:, :, 0:128], in0=acc8[:, :, 0:128],
                                              in1=rcp8.rearrange("p (a b) -> p a b", b=1).to_broadcast([128, 8, 128]), op=ALU.mult),
             reads=keys + ["rcp8"], writes=keys)
        S.op("dve", lambda e: e.tensor_tensor(out=a4[:, :, 0, 0:128], in0=a4[:, :, 0, 0:128], in1=a4[:, :, 1, 0:128], op=ALU.add),
             reads=keys, writes=keys)
        S.op("dve", lambda e: e.tensor_tensor(out=a4[:, :, 1, 0:128], in0=a4[:, :, 0, 0:128], in1=a4[:, :, 0, 0:128], op=ALU.mult),
             reads=keys, writes=keys)
        S.op("dve", lambda e: e.tensor_reduce(out=ss4[:, 0:4], in_=a4[:, :, 1, 0:128], axis=AX.X, op=ALU.add),
             reads=keys, writes=["ss4a"])
        S.op("dve", lambda e: e.tensor_scalar(out=ss4[:, 4:8], in0=ss4[:, 0:4], scalar1=1.0 / 128, scalar2=EPS,
                                              op0=ALU.mult, op1=ALU.add),
             reads=["ss4a"], writes=["ss4b"])
        S.op("pool", lambda e: e.tensor_tensor(out=ss4[:, 0:4], in0=ss4[:, 4:8], in1=neghalf[:, 0:4], op=ALU.pow),
             reads=["ss4b", "neghalf"], writes=["ss4c", "ss4a"])
        S.op("dve", lambda e: e.tensor_tensor(out=onb3[slot], in0=a4[:, :, 0, 0:128],
                                              in1=ss4[:, 0:4].rearrange("p (a b) -> p a b", b=1).to_broadcast([128, 4, 128]), op=ALU.mult),
             reads=keys + ["ss4c"], writes=[f"onb{slot}"])

    def finalize2(h, P, slot):
        q0 = P * 512
        for i4 in range(4):
            S.op("pe", lambda e, i4=i4: e.transpose(ptrb[:, i4 * 128:(i4 + 1) * 128], onb3[slot][:, i4, :], ident_bf),
                 reads=[f"onb{slot}", "cmat"], writes=["ps2"])
        S.op("dve", lambda e: e.tensor_scalar(out=oT[:, h, q0:q0 + 512], in0=ptrb[:, 0:512], scalar1=gsub, scalar2=None, op0=ALU.mult),
             reads=["ps2", "gsub"], writes=[f"oT{h}"])

    pend = {"p": None, "n": 0}

    steps = [(h, qb, kb) for h in range(8) for qb in range(16) for kb in range(qb + 1)]
    pairs = [(steps[2 * i], steps[2 * i + 1]) for i in range(len(steps) // 2)]

    def st_pair(p, slot):
        st_mm(*pairs[p][0], slot, 0)
        st_mm(*pairs[p][1], slot, 256)

    st_pair(0, STB[0])
    st_pair(1, STB[1])
    for p, pr in enumerate(pairs):
        slot = STB[p % 3]
        sl = p % 3
        if p + 2 < len(pairs):
            st_pair(p + 2, STB[(p + 2) % 3])
        S.op("act", lambda e: e.activation(out=pT3[sl], in_=PS(slot), func=AF.Exp),
             reads=[f"ps{slot}"], writes=[f"pT{sl}"])
        for idx, (h, qb, kb) in enumerate(pr):
            c0 = idx * 256
            par = qb % 2
            for m in range(2):
                S.op("pe", lambda e, m=m: e.matmul(PS(4 + 2 * par + m, 129, 0), lhsT=pT3[sl][:, c0 + m * 128:c0 + (m + 1) * 128],
                                                  rhs=vaug[:, kb, h, :], start=(kb == 0), stop=(kb == qb)),
                     reads=[f"pT{sl}", f"vaug{kb}_{h // 4}", f"vone{kb}"], writes=[f"ps{4 + 2 * par + m}"])
            if kb == qb:
                fin_copy(h, qb)
                if qb % 4 == 3:
                    if pend["p"] is not None:
                        finalize2(*pend["p"])
                    slot2 = pend["n"] % 2
                    pend["n"] += 1
                    finalize1(h, qb // 4, slot2)
                    pend["p"] = (h, qb // 4, slot2)
    finalize2(*pend["p"])
    S.barrier()
    if stop_after == "B3":
        return finish()
    for r in range(NRNG):
        xr = xT_r(r)
        proj_post(r, 8, lambda k, dc: (wbo_a if k < 4 else wbo_b)[:, k % 4, dc * 128:(dc + 1) * 128],
                  lambda k: oT[:, k, r * 512:(r + 1) * 512],
                  lambda k: ["wboa" if k < 4 else "wbob", f"oT{k}"], hbb, "hbb", 4, sqb4, rstd_bo, "rstdbo",
                  lambda c: xr[:, c, :], lambda c: [f"big{r}"], lambda c: [f"big{r}"])
        if r == 2 and stop_after != "B":
            ffn_prenorm0(1)
    if stop_after == "B":
        S.barrier()
        return finish()
    S.soft_barrier()
    ffn(1, pre_done=True, wu_alias={0: ["oT1", "oT2"], 1: ["oT2", "oT3"], 2: ["oT3", "oT4"]})
    return finish()


def _const_tables():
    f32 = np.float32
    cm = np.zeros((128, 4, 128), f32)
    cm[:, 0, :] = 1.0 / 1024.0
    cm[:, 1, :] = np.eye(128, dtype=f32)
    kk = np.arange(128)
    cm[:, 2, :] = (kk[:, None] <= kk[None, :]).astype(f32)
    cm[:, 3, :] = -30000.0 * (kk[:, None] > kk[None, :]).astype(f32)
    half = 128
    inv = 1.0 / (10000.0 ** np.linspace(0.0, 1.0, half, dtype=np.float64))
    ang = np.arange(T, dtype=np.float64)[None, :] * inv[:, None]
    cs = np.stack([np.cos(ang), np.sin(ang)], axis=1).astype(f32)
    rt = np.zeros((128, 4, 640), f32)
    kdec = np.zeros((128, 4), f32)
    idx = np.arange(128, dtype=np.float64)
    for h in range(4):
        lg = np.log1p(-(2.0 ** (-5.0 - h)))
        rel = idx[None, :] - idx[:, None]
        m = np.where(rel >= 0, np.exp(lg * np.maximum(rel, 0.0)), 0.0) * (256.0 ** -0.5)
        rt[:, h, 0:128] = m
        qd = np.exp(lg * (idx + 1.0))
        rt[:, h, 128:640] = np.tile(qd, 4)[None, :]
        kdec[:, h] = np.exp(lg * (127.0 - idx)) * (256.0 ** -0.5)
    return cm.reshape(128, 512), cs.reshape(128, 2 * T), rt.reshape(128, 4 * 640), kdec


def _lay_up(w):
    return np.ascontiguousarray(w.reshape(8, 128, 2, 22, 128).transpose(3, 1, 0, 2, 4).reshape(22, 128, 2048))


def _lay_down(w):
    return np.ascontiguousarray(w.reshape(22, 128, 8, 128).transpose(2, 1, 0, 3).reshape(8, 128, 2816))


_CACHE = {}


def _get_nc(stop_after="F1"):
    if stop_after not in _CACHE:
        _CACHE[stop_after] = build(stop_after)[0]
    return _CACHE[stop_after]


def make_in_maps(inputs):
    f32 = np.float32
    g = lambda k: np.asarray(inputs[k], dtype=f32)
    cm, cs, rt, kdec = _const_tables()
    small = np.zeros((128, 512), f32)
    gl = [g("a_norm_pre")[0], g("a_norm_post")[0], g("kv_norm"), g("b_norm_pre")[0], g("b_norm_post")[0],
          g("ffn_norm_pre")[0], g("ffn_norm_pre")[1], g("ffn_norm_post")[0], g("ffn_norm_post")[1]]
    for i, v in enumerate(gl):
        small[:, i * 8:(i + 1) * 8] = v.reshape(8, 128).T
    cw, cb = g("ffn_conv_w"), g("ffn_conv_b")
    for l in range(2):
        for j in range(3):
            small[:, 72 + (l * 4 + j) * 44:72 + (l * 4 + j + 1) * 44] = cw[l, j].reshape(44, 128).T
        small[:, 72 + (l * 4 + 3) * 44:72 + (l * 4 + 4) * 44] = cb[l].reshape(44, 128).T
    small[:, 424:428] = kdec
    small[:, 428] = g("b_subln")[0]
    small[:, 464] = EPS
    lamp = np.ascontiguousarray(np.broadcast_to(g("b_lambda")[0].reshape(1, 256), (128, 256)))
    shared = {
        "a_w_in": np.ascontiguousarray(g("a_w_in")[0]), "a_w_out": np.ascontiguousarray(g("a_w_out")[0]),
        "w_kv": np.ascontiguousarray(g("w_kv")), "b_w_q": np.ascontiguousarray(g("b_w_q")[0]),
        "b_w_out": np.ascontiguousarray(g("b_w_out")[0]),
        "w_up0": _lay_up(g("ffn_w_up")[0]), "w_up1": _lay_up(g("ffn_w_up")[1]),
        "w_down0": _lay_down(g("ffn_w_down")[0]), "w_down1": _lay_down(g("ffn_w_down")[1]),
        "cmat": cm, "cossin": cs, "rtab": rt, "small": small, "lamp": lamp,
    }
    x = g("x")
    maps = []
    for b in range(x.shape[0]):
        d = dict(shared)
        d["xT"] = np.ascontiguousarray(x[b].T)
        maps.append(d)
    return maps


def kernel(**inputs):
    nc = _get_nc("F1")
    in_maps = make_in_maps(inputs)
    res = run_bass_kernel_spmd(nc, in_maps, core_ids=list(range(len(in_maps))))
    out = np.stack([np.ascontiguousarray(r["outT"].T) for r in res.results], axis=0)
    return out.astype(np.float32)
```
